# Optimizing a Trainium2 kernel written in Bass

```python
import math
import jax, jax.numpy as jnp
from jax import lax
import numpy as np

D_MODEL = 2048
BATCH = 4
SEQ = 2048
DEPTH = 4
DEC_BATCH = 128
DEC_SEQ = 4
PAST_LEN = 16384
PAGE_SIZE = 128

N_EVEN = (DEPTH + 1) // 2
N_ODD = DEPTH // 2
D_FF = 5632
RW_WIDTH = D_MODEL // 2
RW_HEAD = 64
RW_HEADS = RW_WIDTH // RW_HEAD
D_DECAY_LORA = 96
D_AAA_LORA = 96
D_GATE_LORA = 256
P_RW = 3 * RW_WIDTH + D_DECAY_LORA + D_AAA_LORA + D_GATE_LORA
RW_GN_EPS = 64e-5
HG_WIDTH = D_MODEL // 2
HG_EXPAND = 128
HG_HEADS = HG_WIDTH // HG_EXPAND
HG_VDIM = HG_WIDTH // HG_HEADS
HG_KEY_WIDTH = HG_HEADS * HG_EXPAND
P_HG = 2 * HG_KEY_WIDTH + 2 * HG_WIDTH
P_EVEN = P_RW + P_HG
HG_CHUNK = 64
HG_NORM_EPS = 1e-5
LRU_WIDTH = D_MODEL
LRU_BLOCKS = 8
LRU_BLOCK = LRU_WIDTH // LRU_BLOCKS
CONV_W = 4
LRU_C = 8.0
NORM_EPS = 1e-6

kernel_name = "rwkv7_hgrn2_rglru_macaron_step"


def rmsnorm(x, g, eps=NORM_EPS):
    xf = x.astype(jnp.float32)
    y = xf * lax.rsqrt(jnp.mean(xf * xf, axis=-1, keepdims=True) + eps)
    return (y * g.astype(jnp.float32)).astype(x.dtype)


def swiglu_half(x, norm_g, w_gu, w_down):
    h = rmsnorm(x, norm_g) @ w_gu
    gate, up = jnp.split(h, 2, axis=-1)
    return (jax.nn.silu(gate) * up) @ w_down


def rwkv7_scan(S0, r, w, k, v, kk, b):
    def step(S, inp):
        r_t, w_t, k_t, v_t, kk_t, b_t = inp
        sa = jnp.einsum('bhvk,bhk->bhv', S, -kk_t)
        S = (S * w_t[:, :, None, :] + sa[..., None] * b_t[:, :, None, :]
             + v_t[..., None] * k_t[:, :, None, :])
        return S, jnp.einsum('bhvk,bhk->bhv', S, r_t)
    xs = tuple(jnp.moveaxis(a, 1, 0) for a in (r, w, k, v, kk, b))
    S, o = lax.scan(step, S0, xs)
    return jnp.moveaxis(o, 0, 1), S


def hgrn2_chunked(S0, q, k, v, logf):
    B, T, H, _ = q.shape
    c = math.gcd(T, HG_CHUNK)
    n = T // c

    def to_chunks(a):
        return a.reshape(B, n, c, H, a.shape[-1]).transpose(1, 0, 3, 2, 4)

    mask = jnp.tril(jnp.ones((c, c), dtype=bool))

    def step(S, inp):
        qc, kc, vc, gc = inp
        G = jnp.cumsum(gc, axis=2)
        diff = G[:, :, :, None, :] - G[:, :, None, :, :]
        decay = jnp.exp(jnp.where(mask[:, :, None], diff, -jnp.inf))
        A = jnp.einsum('bhtk,bhsk,bhtsk->bhts', qc, kc, decay)
        o = (jnp.einsum('bhts,bhsv->bhtv', A, vc)
             + jnp.einsum('bhtk,bhkv->bhtv', qc * jnp.exp(G), S))
        G_last = G[:, :, -1:, :]
        S = (jnp.exp(G_last[:, :, 0, :])[..., None] * S
             + jnp.einsum('bhsk,bhsv->bhkv', kc * jnp.exp(G_last - G), vc))
        return S, o

    S, o = lax.scan(step, S0, tuple(to_chunks(a) for a in (q, k, v, logf)))
    o = o.transpose(1, 0, 3, 2, 4).reshape(B, T, H, -1)
    return o, S


def rglru_scan(h0, a, b):
    def comb(x, y):
        a1, b1 = x
        a2, b2 = y
        return a1 * a2, a2 * b1 + b2
    A, Bc = lax.associative_scan(comb, (a, b), axis=1)
    h = A * h0[:, None, :] + Bc
    return h, h[:, -1]


def even_mixer(xn, shift, S_rw, S_hg, p, e):
    f32 = jnp.float32
    B, T, _ = xn.shape
    dt = xn.dtype
    proj = xn @ p['ev_w_in'][e]
    p_rw, p_hg = proj[..., :P_RW], proj[..., P_RW:]
    prev = jnp.concatenate([shift[:, None].astype(dt), p_rw[:, :-1]], axis=1)
    z = p_rw + (prev - p_rw) * p['rw_mu'][e]
    new_shift = p_rw[:, -1]
    idx = [RW_WIDTH, 2 * RW_WIDTH, 3 * RW_WIDTH, 3 * RW_WIDTH + D_DECAY_LORA,
           3 * RW_WIDTH + D_DECAY_LORA + D_AAA_LORA]
    r, k, v, zw, za, zg = jnp.split(z, idx, axis=-1)
    w_log = -jax.nn.softplus(-(p['rw_w0'][e] + jnp.tanh(zw) @ p['rw_w2'][e])) - 0.5
    decay = jnp.exp(-jnp.exp(w_log.astype(f32)))
    a = jax.nn.sigmoid(p['rw_a0'][e] + za @ p['rw_a2'][e])
    g = jax.nn.sigmoid(zg) @ p['rw_g2'][e]

    def heads(t):
        return t.reshape(B, T, RW_HEADS, RW_HEAD).astype(f32)

    r_h, k_h, v_h, a_h, w_h = heads(r), heads(k), heads(v), heads(a), heads(decay)
    kk = k_h * p['rw_k_k'][e].reshape(RW_HEADS, RW_HEAD).astype(f32)
    kk = kk / jnp.maximum(jnp.sqrt(jnp.sum(kk * kk, axis=-1, keepdims=True)), 1e-12)
    k_h = k_h * (1.0 + (a_h - 1.0) * p['rw_k_a'][e].reshape(RW_HEADS, RW_HEAD).astype(f32))
    o, S_rw_new = rwkv7_scan(S_rw.astype(f32), r_h, w_h, k_h, v_h, kk, kk * a_h)
    mu = jnp.mean(o, axis=-1, keepdims=True)
    var = jnp.mean(jnp.square(o - mu), axis=-1, keepdims=True)
    o = ((o - mu) * lax.rsqrt(var + RW_GN_EPS)
         * p['rw_ln_w'][e].reshape(RW_HEADS, RW_HEAD).astype(f32)
         + p['rw_ln_b'][e].reshape(RW_HEADS, RW_HEAD).astype(f32))
    bonus = jnp.sum(r_h * k_h * p['rw_r_k'][e].astype(f32), axis=-1, keepdims=True)
    o = o + bonus * v_h
    o_rw = o.reshape(B, T, RW_WIDTH).astype(dt) * g
    q, f, i, og = jnp.split(p_hg, [HG_KEY_WIDTH, 2 * HG_KEY_WIDTH, 2 * HG_KEY_WIDTH + HG_WIDTH], axis=-1)
    lbs = jnp.cumsum(jax.nn.softmax(p['hg_lb'].astype(f32), axis=0), axis=0)
    lb = (lbs - lbs[0])[e]
    logf = jnp.logaddexp(jnp.log(lb), jnp.log1p(-lb) + jax.nn.log_sigmoid(f.astype(f32)))
    k_hg = -jnp.expm1(logf)
    qh = jax.nn.silu(q.astype(f32)).reshape(B, T, HG_HEADS, HG_EXPAND)
    o_hg, S_hg_new = hgrn2_chunked(
        S_hg.astype(f32), qh, k_hg.reshape(B, T, HG_HEADS, HG_EXPAND),
        i.astype(f32).reshape(B, T, HG_HEADS, HG_VDIM), logf.reshape(B, T, HG_HEADS, HG_EXPAND))
    o_hg = (o_hg * lax.rsqrt(jnp.mean(o_hg * o_hg, axis=-1, keepdims=True) + HG_NORM_EPS)
            * p['hg_norm'][e].reshape(HG_HEADS, HG_VDIM).astype(f32))
    o_hg = o_hg.reshape(B, T, HG_WIDTH).astype(dt) * jax.nn.silu(og)
    y = jnp.concatenate([o_rw, o_hg], axis=-1) @ p['ev_w_out'][e]
    return y, new_shift, S_rw_new, S_hg_new


def odd_mixer(xn, conv_buf, h0, p, o_i):
    f32 = jnp.float32
    B, T, _ = xn.shape
    dt = xn.dtype
    proj = xn @ p['od_w_in'][o_i]
    gate, xb = jnp.split(proj, 2, axis=-1)
    xc = jnp.concatenate([conv_buf.astype(dt), xb], axis=1)
    w = p['conv_w'][o_i]
    xconv = p['conv_b'][o_i] + sum(xc[:, j:j + T] * w[j] for j in range(CONV_W))
    new_buf = xc[:, -(CONV_W - 1):]
    blocks = xconv.reshape(B, T, LRU_BLOCKS, LRU_BLOCK)
    ga = jnp.einsum('btnj,njk->btnk', blocks, p['lru_wa'][o_i]).reshape(B, T, LRU_WIDTH) + p['lru_ba'][o_i]
    gx = jnp.einsum('btnj,njk->btnk', blocks, p['lru_wx'][o_i]).reshape(B, T, LRU_WIDTH) + p['lru_bx'][o_i]
    log_a = (-LRU_C * jax.nn.sigmoid(ga.astype(f32))
             * jax.nn.softplus(-p['lru_lambda'][o_i].astype(f32)))
    a = jnp.exp(log_a)
    mult = jnp.sqrt(-jnp.expm1(2.0 * log_a))
    b = mult * jax.nn.sigmoid(gx.astype(f32)) * xconv.astype(f32)
    h, h_last = rglru_scan(h0.astype(f32), a, b)
    y = (h.astype(dt) * jax.nn.gelu(gate)) @ p['od_w_out'][o_i]
    return y, new_buf, h_last


def trunk(x, shift, S_rw, S_hg, conv_buf, h_lru, p):
    shifts, rws, hgs, convs, lrus = [], [], [], [], []
    for l in range(DEPTH):
        x = x + 0.5 * swiglu_half(x, p['ffn1_norm'][l], p['ffn1_w_gu'][l], p['ffn1_w_down'][l])
        xn = rmsnorm(x, p['mix_norm'][l])
        if l % 2 == 0:
            e = l // 2
            y, s_new, rw_new, hg_new = even_mixer(xn, shift[:, e], S_rw[:, e], S_hg[:, e], p, e)
            shifts.append(s_new)
            rws.append(rw_new)
            hgs.append(hg_new)
        else:
            o_i = l // 2
            y, c_new, h_new = odd_mixer(xn, conv_buf[:, o_i], h_lru[:, o_i], p, o_i)
            convs.append(c_new)
            lrus.append(h_new)
        x = x + y
        x = x + 0.5 * swiglu_half(x, p['ffn2_norm'][l], p['ffn2_w_gu'][l], p['ffn2_w_down'][l])
    y = rmsnorm(x, p['final_norm'])
    return (y,
            jnp.stack(shifts, axis=1).astype(shift.dtype),
            jnp.stack(rws, axis=1).astype(S_rw.dtype),
            jnp.stack(hgs, axis=1).astype(S_hg.dtype),
            jnp.stack(convs, axis=1).astype(conv_buf.dtype),
            jnp.stack(lrus, axis=1).astype(h_lru.dtype))


def setup_inputs(seed: int = 0) -> dict:
    key = jax.random.key(seed)
    ks = iter(jax.random.split(key, 48))

    def nrm(shape, scale):
        return jax.random.normal(next(ks), shape, jnp.float32) * scale

    def gain(shape):
        return 1.0 + nrm(shape, 0.01)

    a0 = jax.random.uniform(next(ks), (N_ODD, LRU_WIDTH), jnp.float32, 0.9, 0.999)
    s = a0 ** (1.0 / LRU_C)
    lru_lambda = jnp.log(s) - jnp.log1p(-s)
    return {
        'x_prompt': nrm((BATCH, SEQ, D_MODEL), 1.0),
        'x_sample': nrm((DEC_BATCH, DEC_SEQ, D_MODEL), 1.0),
        'state_rwkv_shift': nrm((DEC_BATCH, N_EVEN, P_RW), 1.0),
        'state_rwkv': nrm((DEC_BATCH, N_EVEN, RW_HEADS, RW_HEAD, RW_HEAD), 0.3),
        'state_hgrn': nrm((DEC_BATCH, N_EVEN, HG_HEADS, HG_EXPAND, HG_VDIM), 0.3),
        'state_conv': nrm((DEC_BATCH, N_ODD, CONV_W - 1, LRU_WIDTH), 1.0),
        'state_lru': nrm((DEC_BATCH, N_ODD, LRU_WIDTH), 0.5),
        'ffn1_norm': gain((DEPTH, D_MODEL)),
        'ffn1_w_gu': nrm((DEPTH, D_MODEL, 2 * D_FF), D_MODEL ** -0.5),
        'ffn1_w_down': nrm((DEPTH, D_FF, D_MODEL), D_FF ** -0.5),
        'mix_norm': gain((DEPTH, D_MODEL)),
        'ffn2_norm': gain((DEPTH, D_MODEL)),
        'ffn2_w_gu': nrm((DEPTH, D_MODEL, 2 * D_FF), D_MODEL ** -0.5),
        'ffn2_w_down': nrm((DEPTH, D_FF, D_MODEL), D_FF ** -0.5),
        'ev_w_in': nrm((N_EVEN, D_MODEL, P_EVEN), D_MODEL ** -0.5),
        'rw_mu': jax.random.uniform(next(ks), (N_EVEN, P_RW), jnp.float32),
        'rw_w0': jax.random.uniform(next(ks), (N_EVEN, RW_WIDTH), jnp.float32, -6.0, 1.0),
        'rw_w2': nrm((N_EVEN, D_DECAY_LORA, RW_WIDTH), 0.5 * D_DECAY_LORA ** -0.5),
        'rw_a0': nrm((N_EVEN, RW_WIDTH), 0.1),
        'rw_a2': nrm((N_EVEN, D_AAA_LORA, RW_WIDTH), D_AAA_LORA ** -0.5),
        'rw_g2': nrm((N_EVEN, D_GATE_LORA, RW_WIDTH), D_GATE_LORA ** -0.5),
        'rw_k_k': 0.85 + nrm((N_EVEN, RW_WIDTH), 0.05),
        'rw_k_a': 1.0 + nrm((N_EVEN, RW_WIDTH), 0.05),
        'rw_r_k': nrm((N_EVEN, RW_HEADS, RW_HEAD), 0.1),
        'rw_ln_w': gain((N_EVEN, RW_WIDTH)),
        'rw_ln_b': nrm((N_EVEN, RW_WIDTH), 0.01),
        'hg_lb': nrm((N_EVEN, HG_KEY_WIDTH), 0.5),
        'hg_norm': gain((N_EVEN, HG_WIDTH)),
        'ev_w_out': nrm((N_EVEN, RW_WIDTH + HG_WIDTH, D_MODEL), (RW_WIDTH + HG_WIDTH) ** -0.5),
        'od_w_in': nrm((N_ODD, D_MODEL, 2 * LRU_WIDTH), D_MODEL ** -0.5),
        'conv_w': nrm((N_ODD, CONV_W, LRU_WIDTH), CONV_W ** -0.5),
        'conv_b': nrm((N_ODD, LRU_WIDTH), 0.01),
        'lru_wa': nrm((N_ODD, LRU_BLOCKS, LRU_BLOCK, LRU_BLOCK), LRU_BLOCK ** -0.5),
        'lru_ba': nrm((N_ODD, LRU_WIDTH), 0.01),
        'lru_wx': nrm((N_ODD, LRU_BLOCKS, LRU_BLOCK, LRU_BLOCK), LRU_BLOCK ** -0.5),
        'lru_bx': nrm((N_ODD, LRU_WIDTH), 0.01),
        'lru_lambda': lru_lambda,
        'od_w_out': nrm((N_ODD, LRU_WIDTH, D_MODEL), LRU_WIDTH ** -0.5),
        'final_norm': gain((D_MODEL,)),
    }


def reference(x_prompt, x_sample, state_rwkv_shift, state_rwkv, state_hgrn, state_conv, state_lru,
              ffn1_norm, ffn1_w_gu, ffn1_w_down, mix_norm, ffn2_norm, ffn2_w_gu, ffn2_w_down,
              ev_w_in, rw_mu, rw_w0, rw_w2, rw_a0, rw_a2, rw_g2, rw_k_k, rw_k_a, rw_r_k,
              rw_ln_w, rw_ln_b, hg_lb, hg_norm, ev_w_out,
              od_w_in, conv_w, conv_b, lru_wa, lru_ba, lru_wx, lru_bx, lru_lambda, od_w_out,
              final_norm):
    p = dict(ffn1_norm=ffn1_norm, ffn1_w_gu=ffn1_w_gu, ffn1_w_down=ffn1_w_down, mix_norm=mix_norm,
             ffn2_norm=ffn2_norm, ffn2_w_gu=ffn2_w_gu, ffn2_w_down=ffn2_w_down,
             ev_w_in=ev_w_in, rw_mu=rw_mu, rw_w0=rw_w0, rw_w2=rw_w2, rw_a0=rw_a0, rw_a2=rw_a2,
             rw_g2=rw_g2, rw_k_k=rw_k_k, rw_k_a=rw_k_a, rw_r_k=rw_r_k, rw_ln_w=rw_ln_w,
             rw_ln_b=rw_ln_b, hg_lb=hg_lb, hg_norm=hg_norm, ev_w_out=ev_w_out,
             od_w_in=od_w_in, conv_w=conv_w, conv_b=conv_b, lru_wa=lru_wa, lru_ba=lru_ba,
             lru_wx=lru_wx, lru_bx=lru_bx, lru_lambda=lru_lambda, od_w_out=od_w_out,
             final_norm=final_norm)
    Bp = x_prompt.shape[0]
    dt = x_prompt.dtype
    z_shift = jnp.zeros((Bp,) + state_rwkv_shift.shape[1:], dt)
    z_rw = jnp.zeros((Bp,) + state_rwkv.shape[1:], dt)
    z_hg = jnp.zeros((Bp,) + state_hgrn.shape[1:], dt)
    z_conv = jnp.zeros((Bp,) + state_conv.shape[1:], dt)
    z_lru = jnp.zeros((Bp,) + state_lru.shape[1:], dt)
    y_prompt, p_shift, p_rwkv, p_hgrn, p_conv, p_lru = trunk(
        x_prompt, z_shift, z_rw, z_hg, z_conv, z_lru, p)
    y_sample, s_shift, s_rwkv, s_hgrn, s_conv, s_lru = trunk(
        x_sample, state_rwkv_shift, state_rwkv, state_hgrn, state_conv, state_lru, p)
    return (y_prompt, y_sample, p_shift, p_rwkv, p_hgrn, p_conv, p_lru,
            s_shift, s_rwkv, s_hgrn, s_conv, s_lru)
```

```python
import numpy as np
import concourse.bass as bass
import concourse.mybir as mybir
from concourse.bass_utils import run_bass_kernel_spmd

F32 = mybir.dt.float32
BF16 = mybir.dt.bfloat16
AF = mybir.ActivationFunctionType
ALU = mybir.AluOpType
AX = mybir.AxisListType
MAXV = 30000
NCORES = 8


class Cfg:
    def __init__(self, D=2048, DFF=5632, SEQ=2048, NSS=16, DEPTH=4):
        self.D, self.DFF, self.SEQ, self.NSS, self.DEPTH = D, DFF, SEQ, NSS, DEPTH
        self.KC = D // 128
        self.FC = DFF // 128
        self.TP = SEQ // 2
        self.RW = D // 2
        self.NHP = self.RW // 128
        self.NH = self.RW // 64
        self.NHG = self.RW // 128
        self.NLB = D // 256
        self.P_RW = 3 * self.RW + 448
        self.P_EVEN = self.P_RW + 4 * self.RW
        self.NEV = (DEPTH + 1) // 2
        self.NOD = DEPTH // 2


class Buf:
    __slots__ = ("name", "w", "r", "excl")

    def __init__(self, name, excl=False):
        self.name = name
        self.w = None
        self.r = {}
        self.excl = excl


class Prog:
    def __init__(self, nc, sems):
        self.nc = nc
        self.sems = list(sems)
        self.si = 0
        self.E = {}
        for e in ("pe", "act", "dve", "pool", "sp"):
            self.E[e] = dict(ops=[], sem=None, val=0, waited={})
        self.ring = {}
        self.nops = 0

    def new_sem(self):
        s = self.sems[self.si]
        self.si += 1
        return s

    def _deps(self, e, reads, writes):
        deps = {}

        def add(t):
            if t is None:
                return
            s, v, te = t
            if e == "pe" and te == "pe":
                return
            k = id(s)
            if k not in deps or deps[k][1] < v:
                deps[k] = (s, v)

        for b in reads:
            add(b.w)
            if b.excl:
                for t in b.r.values():
                    if t[2] != e:
                        add(t)
        for b in writes:
            add(b.w)
            for t in b.r.values():
                add(t)
        W = self.E[e]["waited"]
        out = []
        for k, (s, v) in deps.items():
            if W.get(k, 0) >= v:
                continue
            W[k] = v
            out.append((s, v))
        return out

    def _mark(self, tok, reads, writes):
        k = id(tok[0])
        for b in reads:
            o = b.r.get(k)
            if o is None or o[1] < tok[1]:
                b.r[k] = tok
        for b in writes:
            b.w = tok
            b.r = {}

    def op(self, e, fn, reads=(), writes=()):
        E = self.E[e]
        waits = self._deps(e, reads, writes)
        if E["sem"] is None or E["val"] >= MAXV:
            E["sem"] = self.new_sem()
            E["val"] = 0
        E["val"] += 1
        tok = (E["sem"], E["val"], e)
        E["ops"].append((waits, fn, E["sem"], 1))
        self._mark(tok, reads, writes)
        self.nops += 1
        return tok

    def dma(self, q, out, in_, reads=(), writes=(), **kw):
        if q not in self.ring:
            self.ring[q] = dict(slots=[[self.new_sem(), 0] for _ in range(10)], i=0)
        R = self.ring[q]
        E = self.E[q]
        sl = R["slots"][R["i"] % len(R["slots"])]
        R["i"] += 1
        waits = self._deps(q, reads, writes)
        if sl[1] > 0:
            k = id(sl[0])
            if E["waited"].get(k, 0) < sl[1]:
                E["waited"][k] = sl[1]
                waits.append((sl[0], sl[1]))
        if sl[1] + 16 > MAXV:
            sl[0] = self.new_sem()
            sl[1] = 0
        sl[1] += 16
        tok = (sl[0], sl[1], "dma")
        E["ops"].append((waits, lambda eng: eng.dma_start(out=out, in_=in_, **kw), sl[0], 16))
        self._mark(tok, reads, writes)
        return tok

    def emit(self, block):
        nc = self.nc
        P = self

        def run(eng, name):
            for waits, fn, sem, inc in P.E[name]["ops"]:
                for s, v in waits:
                    eng.wait_ge(s, v)
                fn(eng).then_inc(sem, inc)
            if name in P.ring:
                for s, v in P.ring[name]["slots"]:
                    if v > 0:
                        eng.wait_ge(s, v)

        @block.tensor
        def _(eng):
            run(eng, "pe")

        @block.scalar
        def _(eng):
            run(eng, "act")

        @block.vector
        def _(eng):
            run(eng, "dve")

        @block.gpsimd
        def _(eng):
            run(eng, "pool")

        @block.sync
        def _(eng):
            run(eng, "sp")


class K:
    def __init__(self, cfg, stages=("ffn", "odd", "even")):
        self.c = cfg
        self.stages = stages
        self.nc = bass.Bass("TRN2", target_bir_lowering=False)
        self.ctx = []

    def sb(self, name, shape, dt=F32):
        cm = self.nc.sbuf_tensor(name, list(shape), dt)
        t = cm.__enter__()
        self.ctx.append(cm)
        return t

    def dram(self, name, shape, kind, dt=F32):
        return self.nc.dram_tensor(name, list(shape), dt, kind=kind).ap()

    def build(self):
        c, nc = self.c, self.nc
        D, KC, TP = c.D, c.KC, c.TP
        NMAX = TP + 64
        self.NMAX = NMAX
        I, O = "ExternalInput", "ExternalOutput"
        d = self.d = {}
        d["xp"] = self.dram("xp", [2 * TP, D], I)
        d["xs"] = self.dram("xs", [64, D], I)
        d["st_shift"] = self.dram("st_shift", [16, c.NEV, c.P_RW], I)
        d["st_rwkv"] = self.dram("st_rwkv", [16, c.NEV, c.NH, 64, 64], I)
        d["st_hgrn"] = self.dram("st_hgrn", [16, c.NEV, c.NHG, 128, 128], I)
        d["st_conv"] = self.dram("st_conv", [16, c.NOD, 3, D], I)
        d["st_lru"] = self.dram("st_lru", [16, c.NOD, D], I)
        L = c.DEPTH
        wshapes = dict(
            ffn1_norm=[L, D], ffn1_w_gu=[L, D, 2 * c.DFF], ffn1_w_down=[L, c.DFF, D], mix_norm=[L, D],
            ffn2_norm=[L, D], ffn2_w_gu=[L, D, 2 * c.DFF], ffn2_w_down=[L, c.DFF, D],
            ev_w_in=[c.NEV, D, c.P_EVEN], rw_mu=[c.NEV, c.P_RW], rw_w0=[c.NEV, c.RW],
            rw_w2=[c.NEV, 96, c.RW], rw_a0=[c.NEV, c.RW], rw_a2=[c.NEV, 96, c.RW], rw_g2=[c.NEV, 256, c.RW],
            rw_k_k=[c.NEV, c.RW], rw_k_a=[c.NEV, c.RW], rw_r_k=[c.NEV, c.NH, 64], rw_ln_w=[c.NEV, c.RW],
            rw_ln_b=[c.NEV, c.RW], hg_lb=[c.NEV, c.RW], hg_norm=[c.NEV, c.RW], ev_w_out=[c.NEV, D, D],
            od_w_in=[c.NOD, D, 2 * D], conv_w=[c.NOD, 4, D], conv_b=[c.NOD, D],
            lru_wa=[c.NOD, c.NLB, 256, 256], lru_ba=[c.NOD, D], lru_wx=[c.NOD, c.NLB, 256, 256],
            lru_bx=[c.NOD, D], lru_lambda=[c.NOD, D], od_w_out=[c.NOD, D, D], final_norm=[D])
        self.wshapes = wshapes
        for k, s in wshapes.items():
            d[k] = self.dram(k, s, I)
        d["yp"] = self.dram("yp", [2 * TP, D], O)
        d["ys"] = self.dram("ys", [64, D], O)
        d["o_shift_p"] = self.dram("o_shift_p", [c.NEV, c.P_RW], O)
        d["o_rwkv_p"] = self.dram("o_rwkv_p", [c.NEV, c.NH, 64, 64], O)
        d["o_hgrn_p"] = self.dram("o_hgrn_p", [c.NEV, c.NHG, 128, 128], O)
        d["o_conv_p"] = self.dram("o_conv_p", [c.NOD, 3, D], O)
        d["o_lru_p"] = self.dram("o_lru_p", [c.NOD, D], O)
        d["o_shift_s"] = self.dram("o_shift_s", [16, c.NEV, c.P_RW], O)
        d["o_rwkv_s"] = self.dram("o_rwkv_s", [16, c.NEV, c.NH, 64, 64], O)
        d["o_hgrn_s"] = self.dram("o_hgrn_s", [16, c.NEV, c.NHG, 128, 128], O)
        d["o_conv_s"] = self.dram("o_conv_s", [16, c.NOD, 3, D], O)
        d["o_lru_s"] = self.dram("o_lru_s", [16, c.NOD, D], O)

        self.x = self.sb("x", [128, KC * NMAX])
        self.xn = self.sb("xn", [128, KC * NMAX], BF16)
        self.bx = [Buf(f"x{k}") for k in range(KC)]
        self.bxn = Buf("xn")
        self.WS = 16 * 128
        self.wst = [self.sb(f"wst{i}", [128, self.WS]) for i in range(2)]
        self.wbf = [self.sb(f"wbf{i}", [128, self.WS], BF16) for i in range(2)]
        self.bwst = [Buf(f"wst{i}") for i in range(2)]
        self.bwbf = [Buf(f"wbf{i}") for i in range(2)]
        self.wi = 0
        self.wj = 0
        self.AR = 14 * 1024
        self.ar = self.sb("arena", [128, self.AR])
        self.live = []
        self.ident = self.sb("ident", [128, 128])
        self.ones = self.sb("ones", [128, 128])
        self.gains = self.sb("gains", [128, (3 * L + 1) * KC])
        self.cst = self.sb("cst", [128, 800])
        self.ccol = {}
        self.cnext = 0
        self.pers = self.sb("pers", [128, c.NOD * KC * 4])
        self.bpers = Buf("pers")
        self.rstd = self.ar[:, self.AR - NMAX:self.AR]
        self.sq = [self.ar[:, self.AR - (2 + i) * NMAX:self.AR - (1 + i) * NMAX] for i in range(2)]
        self.eps = self.sb("eps", [128, 4])
        self.bconst = Buf("const")
        self.ps = []
        for i in range(8):
            cm = nc.psum_tensor(f"ps{i}", [128, 512], F32)
            self.ps.append(cm.__enter__())
            self.ctx.append(cm)
        self.bps = [Buf(f"ps{i}", excl=True) for i in range(8)]
        self.pi = 0
        sems = []
        for i in range(56):
            cm = nc.semaphore(f"s{i}")
            sems.append(cm.__enter__())
            self.ctx.append(cm)
        self.P = Prog(nc, sems)
        self.setup()
        for ti in range(2):
            self.tile(ti)
        cmb = nc.Block()
        block = cmb.__enter__()
        self.P.emit(block)
        cmb.__exit__(None, None, None)
        for cm in reversed(self.ctx):
            cm.__exit__(None, None, None)
        return nc

    def aclaim(self, name, off, size):
        assert off + size <= self.AR, (name, off, size)
        nb = Buf(name)
        keep = []
        for (o, sz, b) in self.live:
            if o < off + size and off < o + sz:
                toks = list(b.r.values()) + ([b.w] if b.w is not None else [])
                for t in toks:
                    k = id(t[0])
                    if k not in nb.r or nb.r[k][1] < t[1]:
                        nb.r[k] = t
                if not (off <= o and o + sz <= off + size):
                    keep.append((o, sz, b))
            else:
                keep.append((o, sz, b))
        keep.append((off, size, nb))
        self.live = keep
        return self.ar[:, off:off + size], nb

    def psum(self):
        i = self.pi % 8
        self.pi += 1
        return self.ps[i], self.bps[i]

    def ntiles(self, n):
        out, t = [], 0
        while t < n:
            m = min(512, n - t)
            out.append((t, m))
            t += m
        return out

    def xv(self, kc, t0=0, m=None):
        m = self.n - t0 if m is None else m
        return self.x[:, kc * self.NMAX + t0: kc * self.NMAX + t0 + m]

    def xnv(self, kc, t0=0, m=None):
        m = self.n - t0 if m is None else m
        return self.xn[:, kc * self.NMAX + t0: kc * self.NMAX + t0 + m]

    def setup(self):
        P, c, d = self.P, self.c, self.d
        L, KC = c.DEPTH, c.KC
        bc = self.bconst
        P.op("pool", lambda e: e.memset(self.ones[:], 1.0), writes=[bc])
        P.op("pool", lambda e: e.memset(self.cst[:], 0.0), writes=[bc])
        P.op("pool", lambda e: e.memset(self.eps[:, 0:1], 1e-6), writes=[bc])
        P.op("pool", lambda e: e.memset(self.eps[:, 1:2], 1.0), writes=[bc])
        P.op("pool", lambda e: e.affine_select(out=self.ident[:], in_=self.ones[:], pattern=[[1, 128]],
                                                compare_op=ALU.is_equal, fill=0.0, base=0, channel_multiplier=-1),
             reads=[bc], writes=[bc])
        names = [f"ffn1_norm{l}" for l in range(L)] + [f"mix_norm{l}" for l in range(L)] + \
                [f"ffn2_norm{l}" for l in range(L)] + ["final_norm"]
        self.gidx = {nm: i for i, nm in enumerate(names)}
        for i, nm in enumerate(names):
            if "nogain" in self.stages:
                break
            src = d["final_norm"] if nm == "final_norm" else d[nm[:-1]][int(nm[-1])]
            P.dma("sp", self.gains[:, i * KC:(i + 1) * KC], src.rearrange("(k p) -> p k", p=128),
                  writes=[bc], allow_slow_non_contiguous=True)
        if "odd" in self.stages:
            self.setup_odd()
        if "even" in self.stages:
            self.setup_even()

    def cload(self, key, src, rows, ncols):
        c0 = self.cnext
        self.cnext += ncols
        assert self.cnext <= 800
        self.ccol[key] = c0
        dst = self.cst[0:rows, c0:c0 + ncols]
        if len(src.shape) == 3:
            dst = dst.rearrange("p (k j) -> p k j", j=src.shape[2])
        self.P.dma("sp", dst, src, writes=[self.bconst], allow_slow_non_contiguous=True)
        return c0

    def cs(self, key, i=0, rows=128):
        c0 = self.ccol[key] + i
        return self.cst[0:rows, c0:c0 + 1]

    def setup_odd(self):
        P, c, d = self.P, self.c, self.d
        KC = c.KC
        bc = self.bconst
        for o in range(c.NOD):
            for j in range(4):
                c0 = self.cload(f"convw{o}_{j}", d["conv_w"][o, j].rearrange("(k p) -> p k", p=128), 128, KC)
            for nm in ("conv_b", "lru_ba", "lru_bx", "lru_lambda"):
                self.cload(f"{nm}{o}", d[nm][o].rearrange("(k p) -> p k", p=128), 128, KC)
            lam = self.cst[:, self.ccol[f"lru_lambda{o}"]: self.ccol[f"lru_lambda{o}"] + KC]
            c0 = self.cnext
            self.cnext += 6 * KC
            t = [self.cst[:, c0 + i * KC: c0 + (i + 1) * KC] for i in range(6)]
            self.ccol[f"csp{o}"] = c0 + 4 * KC
            self.ccol[f"csp2{o}"] = c0 + 5 * KC
            op = lambda eng, fn: P.op(eng, fn, reads=[bc], writes=[bc])
            op("act", lambda e, t=t, lam=lam: e.activation(out=t[0], in_=lam, func=AF.Abs))
            op("act", lambda e, t=t, lam=lam: e.activation(out=t[0], in_=t[0], func=AF.Exp, scale=-1.0))
            op("dve", lambda e, t=t, lam=lam: e.tensor_scalar(out=t[1], in0=t[0], scalar1=2.0, scalar2=None, op0=ALU.add))
            op("dve", lambda e, t=t, lam=lam: e.reciprocal(out=t[1], in_=t[1]))
            op("dve", lambda e, t=t, lam=lam: e.tensor_tensor(out=t[1], in0=t[1], in1=t[0], op=ALU.mult))
            op("dve", lambda e, t=t, lam=lam: e.tensor_tensor(out=t[2], in0=t[1], in1=t[1], op=ALU.mult))
            op("dve", lambda e, t=t, lam=lam: e.tensor_scalar(out=t[3], in0=t[2], scalar1=1.0 / 9, scalar2=1.0 / 7, op0=ALU.mult, op1=ALU.add))
            op("dve", lambda e, t=t, lam=lam: e.tensor_tensor(out=t[3], in0=t[3], in1=t[2], op=ALU.mult))
            op("dve", lambda e, t=t, lam=lam: e.tensor_scalar(out=t[3], in0=t[3], scalar1=1.0 / 5, scalar2=None, op0=ALU.add))
            op("dve", lambda e, t=t, lam=lam: e.tensor_tensor(out=t[3], in0=t[3], in1=t[2], op=ALU.mult))
            op("dve", lambda e, t=t, lam=lam: e.tensor_scalar(out=t[3], in0=t[3], scalar1=1.0 / 3, scalar2=None, op0=ALU.add))
            op("dve", lambda e, t=t, lam=lam: e.tensor_tensor(out=t[3], in0=t[3], in1=t[2], op=ALU.mult))
            op("dve", lambda e, t=t, lam=lam: e.tensor_scalar(out=t[3], in0=t[3], scalar1=1.0, scalar2=None, op0=ALU.add))
            op("dve", lambda e, t=t, lam=lam: e.tensor_tensor(out=t[3], in0=t[3], in1=t[1], op=ALU.mult))
            op("dve", lambda e, t=t, lam=lam: e.tensor_scalar(out=t[2], in0=lam, scalar1=-1.0, scalar2=0.0, op0=ALU.mult, op1=ALU.max))
            op("dve", lambda e, t=t, lam=lam: e.scalar_tensor_tensor(out=t[3], in0=t[3], scalar=2.0, in1=t[2], op0=ALU.mult, op1=ALU.add))
            op("dve", lambda e, t=t, lam=lam: e.tensor_scalar(out=t[4], in0=t[3], scalar1=-8.0, scalar2=None, op0=ALU.mult))
            op("dve", lambda e, t=t, lam=lam: e.tensor_scalar(out=t[5], in0=t[3], scalar1=-16.0, scalar2=None, op0=ALU.mult))
        P.op("pool", lambda e: e.memset(self.pers[:], 0.0), writes=[self.bpers])

    def setup_even(self):
        P, c, d = self.P, self.c, self.d
        KC, RW, NHP, NHG, NEV = c.KC, c.RW, c.NHP, c.NHG, c.NEV
        bc = self.bconst
        segs = []
        for hp in range(NHP):
            segs += [(f"r{hp}", hp * 128, 128), (f"k{hp}", RW + hp * 128, 128), (f"v{hp}", 2 * RW + hp * 128, 128)]
        segs += [("zw", 3 * RW, 96), ("za", 3 * RW + 96, 96), ("zg0", 3 * RW + 192, 128), ("zg1", 3 * RW + 320, 128)]
        self.segs = {nm: (i, c0, m) for i, (nm, c0, m) in enumerate(segs)}
        NS = len(segs)
        self.NSEG = NS
        for e in range(NEV):
            mu0 = self.cnext
            for (nm, c0, m) in segs:
                self.cload(f"mu{e}_{nm}", d["rw_mu"][e, c0:c0 + m].rearrange("(p o) -> p o", o=1), m, 1)
            om0 = self.cnext
            self.cnext += NS
            self.ccol[f"mu{e}"] = mu0
            self.ccol[f"om{e}"] = om0
            P.op("dve", lambda en, mu0=mu0, om0=om0: en.tensor_scalar(
                out=self.cst[:, om0:om0 + NS], in0=self.cst[:, mu0:mu0 + NS], scalar1=-1.0, scalar2=1.0,
                op0=ALU.mult, op1=ALU.add), reads=[bc], writes=[bc])
            for nm in ("rw_w0", "rw_a0", "rw_k_k", "rw_k_a", "rw_ln_w", "rw_ln_b"):
                self.cload(f"{nm}{e}", d[nm][e].rearrange("(k p) -> p k", p=128), 128, NHP)
            self.cload(f"rw_r_k{e}", d["rw_r_k"][e].rearrange("(a h) k -> (h k) a", h=2), 128, NHP)
            self.cload(f"hg_norm{e}", d["hg_norm"][e].rearrange("(k p) -> p k", p=128), 128, NHG)
            self.cload(f"hg_lbraw{e}", d["hg_lb"][e].rearrange("(k p) -> p k", p=128), 128, NHG)
        col = lambda key: self.cst[:, self.ccol[key]:self.ccol[key] + NHG]
        t0 = self.cnext
        self.cnext += (3 * NEV + 1) * NHG
        ex = [self.cst[:, t0 + i * NHG:t0 + (i + 1) * NHG] for i in range(NEV)]
        tot = self.cst[:, t0 + NEV * NHG:t0 + (NEV + 1) * NHG]
        for e in range(NEV):
            self.ccol[f"lb{e}"] = t0 + (NEV + 1 + e) * NHG
            self.ccol[f"oml{e}"] = t0 + (2 * NEV + 1 + e) * NHG
            P.op("act", lambda en, e=e: en.activation(out=ex[e], in_=col(f"hg_lbraw{e}"), func=AF.Exp), reads=[bc], writes=[bc])
        P.op("dve", lambda en: en.tensor_copy(out=tot, in_=ex[0]), reads=[bc], writes=[bc])
        for e in range(1, NEV):
            P.op("dve", lambda en, e=e: en.tensor_tensor(out=tot, in0=tot, in1=ex[e], op=ALU.add), reads=[bc], writes=[bc])
        P.op("dve", lambda en: en.reciprocal(out=tot, in_=tot), reads=[bc], writes=[bc])
        P.op("pool", lambda en: en.memset(col("lb0"), 0.0), reads=[bc], writes=[bc])
        for e in range(1, NEV):
            P.op("dve", lambda en, e=e: en.tensor_tensor(out=ex[e], in0=ex[e], in1=tot, op=ALU.mult), reads=[bc], writes=[bc])
            P.op("dve", lambda en, e=e: en.tensor_tensor(out=col(f"lb{e}"), in0=col(f"lb{e - 1}"), in1=ex[e], op=ALU.add),
                 reads=[bc], writes=[bc])
        for e in range(NEV):
            P.op("dve", lambda en, e=e: en.tensor_scalar(out=col(f"oml{e}"), in0=col(f"lb{e}"), scalar1=-1.0, scalar2=1.0,
                                                         op0=ALU.mult, op1=ALU.add), reads=[bc], writes=[bc])
        self.bones = self.sb("bones", [128, 128])
        self.msu = self.sb("msu", [64, 256])
        self.mui = self.sb("mui", [64, 256])
        self.msl = self.sb("msl", [64, 256])
        self.msu_s = self.sb("msu_s", [64, 128])
        self.mui_s = self.sb("mui_s", [64, 128])
        self.msl_s = self.sb("msl_s", [64, 128])
        self.id4 = self.sb("id4", [64, 256])
        self.rowm = self.sb("rowm", [64, 16])
        self.mrst = self.sb("mrst", [128, 128])
        self.mrst_s = self.sb("mrst_s", [128, 64])
        self.eps2 = self.sb("eps2", [128, 2])
        self.prevp = self.sb("prevp", [128, NEV * NS])
        self.bprev = Buf("prevp")
        self.ST = self.sb("ST", [128, NEV * NHP * 64])
        self.bST = Buf("ST")
        self.SH = self.sb("SHg", [128, NEV * NHG * 128])
        self.bSH = Buf("SH")
        op = lambda eng, fn: P.op(eng, fn, reads=[bc], writes=[bc])
        op("pool", lambda en: en.memset(self.bones[:], 0.0))
        op("pool", lambda en: en.memset(self.bones[0:64, 0:64], 1.0))
        op("pool", lambda en: en.memset(self.bones[64:128, 64:128], 1.0))
        op("pool", lambda en: en.memset(self.eps2[:, 0:1], 64e-5))
        op("pool", lambda en: en.memset(self.eps2[:, 1:2], 1e-5))
        op("pool", lambda en: en.memset(self.prevp[:], 0.0))
        P.op("pool", lambda en: en.memset(self.ST[:], 0.0), writes=[self.bST])
        P.op("pool", lambda en: en.memset(self.SH[:], 0.0), writes=[self.bSH])
        on64 = self.ones[0:64, 0:64]

        def sel(out, in_, pattern, cmp, base, cm):
            op("pool", lambda en: en.affine_select(out=out, in_=in_, pattern=pattern, compare_op=cmp, fill=0.0,
                                                   base=base, channel_multiplier=cm))
        for (t, nsl) in ((self.msu, 4), (self.mui, 4), (self.msl, 4), (self.id4, 4), (self.msu_s, 2), (self.mui_s, 2), (self.msl_s, 2)):
            op("pool", lambda en, t=t: en.memset(t[:], 1.0))
        for (t, nsl) in ((self.msu, 4), (self.msu_s, 2)):
            sel(t[:], t[:], [[0, nsl], [1, 64]], ALU.is_gt, 0, -1)
        for (t, nsl) in ((self.mui, 4), (self.mui_s, 2)):
            sel(t[:], t[:], [[0, nsl], [1, 64]], ALU.is_ge, 0, -1)
        for (t, nsl) in ((self.msl, 4), (self.msl_s, 2)):
            sel(t[:], t[:], [[0, nsl], [-1, 64]], ALU.is_gt, 0, 1)
        sel(self.id4[:], self.id4[:], [[0, 4], [1, 64]], ALU.is_equal, 0, -1)
        for t in (self.msu_s, self.mui_s, self.msl_s):
            sel(t[:], t[:], [[0, 2], [-4, 16], [0, 4]], ALU.is_ge, 0, 1)
            sel(t[:], t[:], [[0, 2], [4, 16], [0, 4]], ALU.is_ge, 3, -1)
        op("pool", lambda en: en.memset(self.rowm[:], 1.0))
        sel(self.rowm[:], self.rowm[:], [[-4, 16]], ALU.is_ge, 0, 1)
        sel(self.rowm[:], self.rowm[:], [[4, 16]], ALU.is_ge, 3, -1)
        op("pool", lambda en: en.memset(self.mrst[:], 1.0))
        op("pool", lambda en: en.memset(self.mrst[:].rearrange("p (c t) -> p c t", t=64)[:, :, 0:1], 0.0))
        op("pool", lambda en: en.memset(self.mrst_s[:], 1.0))
        op("pool", lambda en: en.memset(self.mrst_s[:].rearrange("p (c t) -> p c t", t=4)[:, :, 0:1], 0.0))

    def gain(self, nm, kc):
        i = self.gidx[nm]
        return self.gains[:, i * self.c.KC + kc: i * self.c.KC + kc + 1]

    def wload(self, src, rows, kcn, cols, cast=True, pool_cast=False):
        P = self.P
        i = self.wi % 2
        self.wi += 1
        if not cast:
            st, bst = self.wst[i], self.bwst[i]
            dst = st[0:rows, 0:kcn * cols].rearrange("p (k c) -> p k c", c=cols)
            P.dma("sp", dst, src, writes=[bst])
            return st, bst
        j = self.wj % 2
        self.wj += 1
        st, bf, bst, bbf = self.wst[i], self.wbf[j], self.bwst[i], self.bwbf[j]
        n = kcn * cols
        assert n <= self.WS
        dst = st[0:rows, 0:n].rearrange("p (k c) -> p k c", c=cols)
        P.dma("sp", dst, src, writes=[bst])
        if pool_cast:
            P.op("pool", lambda e: e.tensor_copy(out=bf[0:rows, 0:n], in_=st[0:rows, 0:n]), reads=[bst], writes=[bbf])
            return bf, bbf
        a = (n // 2) // 2 * 2
        b = a + ((n - a) // 2) // 2 * 2
        P.op("pool", lambda e: e.tensor_copy(out=bf[0:rows, 0:a], in_=st[0:rows, 0:a]), reads=[bst], writes=[bbf])
        P.op("act", lambda e: e.copy(out=bf[0:rows, a:b], in_=st[0:rows, a:b]), reads=[bst], writes=[bbf])
        P.op("dve", lambda e: e.tensor_copy(out=bf[0:rows, b:n], in_=st[0:rows, b:n]), reads=[bst], writes=[bbf])
        return bf, bbf

    def rms_stats(self):
        P, c = self.P, self.c
        n, KC = self.n, c.KC
        NM = self.NMAX
        self.brstd = self.aclaim("rstd", self.AR - NM, NM)[1]
        self.bsq = [self.aclaim(f"sq{i}", self.AR - (2 + i) * NM, NM)[1] for i in range(2)]
        nts = self.ntiles(n)
        pss = [self.psum() for _ in nts]
        for kc in range(KC):
            sq, bsq = self.sq[kc % 2], self.bsq[kc % 2]
            P.op("act", lambda e, sq=sq, xin=self.xv(kc), n=n: e.activation(out=sq[:, 0:n], in_=xin, func=AF.Square),
                 reads=[self.bx[kc]], writes=[bsq])
            for (t0, m), (ps, bps) in zip(nts, pss):
                P.op("pe", lambda e, sq=sq, ps=ps, t0=t0, m=m, kc=kc: e.matmul(
                    ps[:, 0:m], lhsT=self.ones[:], rhs=sq[:, t0:t0 + m], start=(kc == 0), stop=(kc == KC - 1)),
                    reads=[bsq, self.bconst], writes=[bps])
        for (t0, m), (ps, bps) in zip(nts, pss):
            P.op("act", lambda e, ps=ps, t0=t0, m=m: e.activation(
                out=self.rstd[:, t0:t0 + m], in_=ps[:, 0:m], func=AF.Ln, scale=1.0 / c.D, bias=self.eps[:, 0:1]),
                reads=[bps, self.bconst], writes=[self.brstd])
        P.op("act", lambda e, n=n: e.activation(out=self.rstd[:, 0:n], in_=self.rstd[:, 0:n], func=AF.Exp, scale=-0.5),
             reads=[self.brstd], writes=[self.brstd])

    def rmsnorm(self, gname):
        P, c = self.P, self.c
        self.rms_stats()
        for kc in range(c.KC):
            P.op("dve", lambda e, o=self.xnv(kc), i0=self.xv(kc), g=self.gain(gname, kc), r=self.rstd[:, 0:self.n]:
                 e.scalar_tensor_tensor(out=o, in0=i0, scalar=g, in1=r, op0=ALU.mult, op1=ALU.mult),
                 reads=[self.bx[kc], self.brstd, self.bconst], writes=[self.bxn])

    def load_x(self, ti):
        P, c, d = self.P, self.c, self.d
        KC, TP, D = c.KC, c.TP, c.D
        blocks = [(d["xp"], ti * TP + t, t, 128) for t in range(0, TP, 128)]
        if ti == 0:
            blocks.append((d["xs"], 0, TP, 64))
        for src, r0, t0, m in blocks:
            for h0 in range(0, D, self.WS):
                i = self.wi % 2
                self.wi += 1
                st, bst = self.wst[i], self.bwst[i]
                w = min(self.WS, D - h0)
                P.dma("sp", st[0:m, 0:w], src[r0:r0 + m, h0:h0 + w], writes=[bst])
                for k0 in range(0, w // 128, 4):
                    ps, bps = self.psum()
                    g = min(4, w // 128 - k0)
                    for j in range(g):
                        P.op("pe", lambda e, ps=ps, st=st, m=m, j=j, k0=k0: e.transpose(
                            out=ps[:, j * 128:j * 128 + m], in_=st[0:m, (k0 + j) * 128:(k0 + j + 1) * 128],
                            identity=self.ident[0:m, 0:m]), reads=[bst, self.bconst], writes=[bps])
                    for j in range(g):
                        kc = h0 // 128 + k0 + j
                        eng = "act" if (self.pi % 2 == 0) else "dve"
                        if eng == "act":
                            P.op("act", lambda e, ps=ps, o=self.xv(kc, t0, m), j=j, m=m: e.copy(
                                out=o, in_=ps[:, j * 128:j * 128 + m]), reads=[bps], writes=[self.bx[kc]])
                        else:
                            P.op("dve", lambda e, ps=ps, o=self.xv(kc, t0, m), j=j, m=m: e.tensor_copy(
                                out=o, in_=ps[:, j * 128:j * 128 + m]), reads=[bps], writes=[self.bx[kc]])

    def store_y(self, ti):
        P, c, d = self.P, self.c, self.d
        KC, TP, D = c.KC, c.TP, c.D
        dbg = "dbg1" in self.stages
        if not dbg:
            self.rms_stats()
        blocks = [(d["yp"], ti * TP + t, t, 128) for t in range(0, TP, 128)]
        if ti == 0:
            blocks.append((d["ys"], 0, TP, 64))
        tmp, btmp = self.aclaim("ytmp", 0, 512)
        for src, r0, t0, m in blocks:
            for h0 in range(0, D, self.WS):
                i = self.wi % 2
                self.wi += 1
                st, bst = self.wst[i], self.bwst[i]
                w = min(self.WS, D - h0)
                for k0 in range(0, w // 128, 4):
                    g = min(4, w // 128 - k0)
                    ps, bps = self.psum()
                    for j in range(g):
                        kc = h0 // 128 + k0 + j
                        if dbg:
                            P.op("dve", lambda e, i0=self.xv(kc, t0, m), j=j, m=m: e.tensor_copy(
                                out=tmp[:, j * 128:j * 128 + m], in_=i0), reads=[self.bx[kc]], writes=[btmp])
                            continue
                        P.op("dve", lambda e, i0=self.xv(kc, t0, m), g=self.gain("final_norm", kc), j=j, t0=t0, m=m: e.scalar_tensor_tensor(
                            out=tmp[:, j * 128:j * 128 + m], in0=i0, scalar=g,
                            in1=self.rstd[:, t0:t0 + m], op0=ALU.mult, op1=ALU.mult),
                            reads=[self.bx[kc], self.brstd, self.bconst], writes=[btmp])
                    for j in range(g):
                        P.op("pe", lambda e, ps=ps, j=j, m=m: e.transpose(
                            out=ps[0:m, j * 128:(j + 1) * 128], in_=tmp[:, j * 128:j * 128 + m], identity=self.ident[:]),
                            reads=[btmp, self.bconst], writes=[bps])
                    P.op("act", lambda e, ps=ps, st=st, k0=k0, g=g, m=m: e.copy(
                        out=st[0:m, k0 * 128:(k0 + g) * 128], in_=ps[0:m, 0:g * 128]), reads=[bps], writes=[bst])
                P.dma("pool", src[r0:r0 + m, h0:h0 + w], st[0:m, 0:w], reads=[bst])

    def ffn(self, l, which):
        P, c, d = self.P, self.c, self.d
        KC, FC, D, DFF = c.KC, c.FC, c.D, c.DFF
        n = self.n
        nts = self.ntiles(n)
        wgu = d[f"ffn{which}_w_gu"][l]
        wdn = d[f"ffn{which}_w_down"][l]
        self.rmsnorm(f"ffn{which}_norm{l}")
        GS = 8
        hN = self.NMAX
        self.live = [(o, z, b) for (o, z, b) in self.live]
        h = self.ar.bitcast(BF16)
        bh = [self.aclaim(f"h{f}", f * hN // 2, hN // 2)[1] for f in range(GS)]
        sg, bsg = self.aclaim("sg", GS * hN // 2, 512)
        jobs = []
        for f0 in range(0, FC, GS):
            gs = min(GS, FC - f0)
            for f in range(gs):
                jobs += [("g", f0, gs, f), ("u", f0, gs, f)]
            DW = max(128, (self.WS // gs) // 128 * 128)
            for d0 in range(0, D, DW):
                jobs.append(("d", f0, gs, (d0, min(DW, D - d0))))
        hnd = {}

        def fetch(i):
            kind, f0, gs, x = jobs[i]
            if kind == "d":
                d0, dw = x
                hnd[i] = self.wload(wdn[f0 * 128:(f0 + gs) * 128, d0:d0 + dw].rearrange("(k p) c -> p k c", p=128), 128, gs, dw)
            else:
                c0 = (f0 + x) * 128 + (DFF if kind == "u" else 0)
                hnd[i] = self.wload(wgu[:, c0:c0 + 128].rearrange("(k p) c -> p k c", p=128), 128, KC, 128)
        fetch(0)
        gp = None
        for i, (kind, f0, gs, x) in enumerate(jobs):
            if i + 1 < len(jobs):
                fetch(i + 1)
            w, bw = hnd.pop(i)
            if kind in ("g", "u"):
                f = x
                pss = []
                for (t0, m) in nts:
                    ps, bps = self.psum()
                    for kc in range(KC):
                        P.op("pe", lambda e, ps=ps, w=w, kc=kc, r=self.xnv(kc, t0, m), m=m: e.matmul(
                            ps[:, 0:m], lhsT=w[:, kc * 128:(kc + 1) * 128], rhs=r, start=(kc == 0), stop=(kc == KC - 1)),
                            reads=[bw, self.bxn], writes=[bps])
                    pss.append((ps, bps))
                    if kind == "u":
                        psg, bpg = gp[len(pss) - 1]
                        P.op("act", lambda e, psg=psg, m=m: e.activation(out=sg[:, 0:m], in_=psg[:, 0:m], func=AF.Silu),
                             reads=[bpg], writes=[bsg])
                        P.op("dve", lambda e, ps=ps, f=f, t0=t0, m=m: e.tensor_tensor(
                            out=h[:, f * hN + t0: f * hN + t0 + m], in0=ps[:, 0:m], in1=sg[:, 0:m], op=ALU.mult),
                            reads=[bps, bsg], writes=[bh[f]])
                if kind == "g":
                    gp = pss
            else:
                d0, dw = x
                wd, bwd = w, bw
                for dj in range(dw // 128):
                    kc = d0 // 128 + dj
                    for (t0, m) in nts:
                        ps, bps = self.psum()
                        for f in range(gs):
                            P.op("pe", lambda e, ps=ps, wd=wd, f=f, dj=dj, dw=dw, t0=t0, m=m: e.matmul(
                                ps[:, 0:m], lhsT=wd[:, f * dw + dj * 128: f * dw + (dj + 1) * 128],
                                rhs=h[:, f * hN + t0: f * hN + t0 + m], start=(f == 0), stop=(f == gs - 1)),
                                reads=[bwd, bh[f]], writes=[bps])
                        P.op("dve", lambda e, ps=ps, xo=self.xv(kc, t0, m), m=m: e.scalar_tensor_tensor(
                            out=xo, in0=ps[:, 0:m], scalar=0.5, in1=xo,
                            op0=ALU.mult, op1=ALU.add), reads=[bps, self.bx[kc]], writes=[self.bx[kc]])

    def outproj(self, wout, ch, ob, bob, c0=0, ncols=None):
        P, c = self.P, self.c
        D, KC = c.D, c.KC
        ncols = self.n if ncols is None else ncols
        nts = self.ntiles(ncols)
        for d0 in range(0, D, self.WS):
            dw = min(self.WS, D - d0)
            w, bw = self.wload(wout[ch * 128:(ch + 1) * 128, d0:d0 + dw].rearrange("(k p) c -> p k c", p=128), 128, 1, dw,
                               pool_cast=True)
            for dj in range(dw // 128):
                kc = d0 // 128 + dj
                for (t0, mm) in nts:
                    ps, bps = self.psum()
                    P.op("pe", lambda e, ps=ps, w=w, dj=dj, t0=t0, mm=mm: e.matmul(
                        ps[:, 0:mm], lhsT=w[:, dj * 128:(dj + 1) * 128], rhs=ob[:, t0:t0 + mm], start=True, stop=True),
                        reads=[bw, bob], writes=[bps])
                    P.op("dve", lambda e, ps=ps, xo=self.xv(kc, c0 + t0, mm), mm=mm: e.tensor_tensor(
                        out=xo, in0=ps[:, 0:mm], in1=xo, op=ALU.add), reads=[bps, self.bx[kc]], writes=[self.bx[kc]])

    def proj(self, wsrc, m, evac):
        P, c = self.P, self.c
        KC = c.KC
        w, bw = self.wload(wsrc.rearrange("(k p) c -> p k c", p=128), 128, KC, m, pool_cast=True)
        for (t0, mm) in self.ntiles(self.n):
            ps, bps = self.psum()
            for kc in range(KC):
                P.op("pe", lambda e, ps=ps, w=w, kc=kc, r=self.xnv(kc, t0, mm), mm=mm: e.matmul(
                    ps[0:m, 0:mm], lhsT=w[:, kc * m:(kc + 1) * m], rhs=r, start=(kc == 0), stop=(kc == KC - 1)),
                    reads=[bw, self.bxn], writes=[bps])
            evac(ps, bps, t0, mm)

    def odd(self, l):
        P, c, d = self.P, self.c, self.d
        o = l // 2
        KC, TP, D = c.KC, c.TP, c.D
        n, ti, NM = self.n, self.ti, self.NMAX
        S = (ti == 0)
        nts = self.ntiles(n)
        win, wout = d["od_w_in"][o], d["od_w_out"][o]
        self.rmsnorm(f"mix_norm{l}")
        off = [0]

        def A(name, size):
            ap, b = self.aclaim(name, off[0], size)
            off[0] += size
            return ap, b
        XE = [A(f"XE{i}", TP + 4) for i in range(2)]
        XS = [A(f"XS{i}", 112) for i in range(2)]
        XC = [A(f"XC{i}", NM) for i in range(2)]
        Aa, bA = A("A", NM)
        T1, bT = A("T1", NM)
        H, bH = A("H", NM)
        GL, bG = A("GL", NM)
        obf, bob = A("ob", NM // 2)
        ob = obf.bitcast(BF16)
        H0, bH0 = A("H0", 16)
        SO, bSO = A("SO", 256)
        pcol = lambda ch, j: self.pers[:, (o * KC + ch) * 4 + j:(o * KC + ch) * 4 + j + 1]
        ptail = lambda ch: self.pers[:, (o * KC + ch) * 4:(o * KC + ch) * 4 + 3]
        if S:
            si = self.wi % 2
            self.wi += 1
            sst, bsst = self.wst[si], self.bwst[si]
            for j in range(3):
                P.dma("sp", sst[j * 16:(j + 1) * 16, 0:D], d["st_conv"][:, o, j, :], writes=[bsst])
            P.dma("sp", sst[48:64, 0:D], d["st_lru"][:, o, :], writes=[bsst])
            SST, bSST = A("SST", KC * 64)
            for k0 in range(0, KC, 8):
                g = min(8, KC - k0)
                ps, bps = self.psum()
                for j in range(g):
                    P.op("pe", lambda e, ps=ps, j=j, k0=k0: e.transpose(
                        out=ps[:, j * 64:(j + 1) * 64], in_=sst[0:64, (k0 + j) * 128:(k0 + j + 1) * 128],
                        identity=self.ident[0:64, 0:64]), reads=[bsst, self.bconst], writes=[bps])
                P.op("act", lambda e, ps=ps, k0=k0, g=g: e.copy(out=SST[:, k0 * 64:(k0 + g) * 64], in_=ps[:, 0:g * 64]),
                     reads=[bps], writes=[bSST])
        for b in range(c.NLB):
            for cc in range(2):
                ch = 2 * b + cc
                xe, bxe = XE[cc]
                xs, bxs = XS[cc]
                xc, bxc = XC[cc]
                if S:
                    P.op("act", lambda e, xs=xs, ch=ch: e.copy(out=xs[:, 0:48], in_=SST[:, ch * 64:ch * 64 + 48]),
                         reads=[bSST], writes=[bxs])
                    P.op("pool", lambda e, xe=xe: e.memset(xe[:, 0:3], 0.0), writes=[bxe])
                else:
                    P.op("pool", lambda e, xe=xe, ch=ch: e.tensor_copy(out=xe[:, 0:3], in_=ptail(ch)), reads=[self.bpers], writes=[bxe])

                def evac(ps, bps, t0, mm, xe=xe, bxe=bxe, xs=xs, bxs=bxs):
                    pe = min(t0 + mm, TP)
                    if pe > t0:
                        P.op("act", lambda e: e.copy(out=xe[:, 3 + t0:3 + pe], in_=ps[:, 0:pe - t0]), reads=[bps], writes=[bxe])
                    if t0 + mm > TP:
                        a0 = max(t0, TP) - t0
                        P.op("act", lambda e: e.copy(
                            out=xs[:, 48:112].rearrange("p (t s) -> p s t", s=16),
                            in_=ps[:, a0:a0 + 64].rearrange("p (s t) -> p s t", t=4)), reads=[bps], writes=[bxs])
                self.proj(win[:, D + ch * 128:D + (ch + 1) * 128], 128, evac)
                cw = lambda j, ch=ch: self.cs(f"convw{o}_{j}", ch)
                cb = self.cs(f"conv_b{o}", ch)
                P.op("dve", lambda e, xe=xe, xc=xc, ch=ch, cb=cb: e.tensor_scalar(
                    out=xc[:, 0:TP], in0=xe[:, 3:3 + TP], scalar1=cw(3, ch), scalar2=cb, op0=ALU.mult, op1=ALU.add),
                    reads=[bxe, self.bconst], writes=[bxc])
                for j in range(3):
                    P.op("dve", lambda e, xe=xe, xc=xc, ch=ch, j=j: e.scalar_tensor_tensor(
                        out=xc[:, 0:TP], in0=xe[:, j:j + TP], scalar=cw(j, ch), in1=xc[:, 0:TP], op0=ALU.mult, op1=ALU.add),
                        reads=[bxe, bxc, self.bconst], writes=[bxc])
                if S:
                    P.op("dve", lambda e, xs=xs, xc=xc, ch=ch, cb=cb: e.tensor_scalar(
                        out=xc[:, TP:TP + 64], in0=xs[:, 48:112], scalar1=cw(3, ch), scalar2=cb, op0=ALU.mult, op1=ALU.add),
                        reads=[bxs, self.bconst], writes=[bxc])
                    for j in range(3):
                        P.op("dve", lambda e, xs=xs, xc=xc, ch=ch, j=j: e.scalar_tensor_tensor(
                            out=xc[:, TP:TP + 64], in0=xs[:, j * 16:j * 16 + 64], scalar=cw(j, ch), in1=xc[:, TP:TP + 64],
                            op0=ALU.mult, op1=ALU.add), reads=[bxs, bxc, self.bconst], writes=[bxc])
                if S:
                    P.op("pool", lambda e, xe=xe, ch=ch: e.tensor_copy(out=ptail(ch), in_=xe[:, TP:TP + 3]),
                         reads=[bxe], writes=[self.bpers])
                    ps, bps = self.psum()
                    P.op("pe", lambda e, ps=ps, xs=xs: e.transpose(out=ps[0:112, 0:128], in_=xs[:, 0:112], identity=self.ident[:]),
                         reads=[bxs, self.bconst], writes=[bps])
                    so, bso = self.aclaim(f"cso{cc}", off[0] + cc * 128, 128)
                    P.op("act", lambda e, ps=ps, so=so: e.copy(out=so[0:112, :], in_=ps[0:112, 0:128]), reads=[bps], writes=[bso])
                    for j in range(3):
                        P.dma("pool", d["o_conv_s"][:, o, j, ch * 128:(ch + 1) * 128], so[64 + 16 * j:80 + 16 * j, :], reads=[bso])
                else:
                    P.dma("pool", d["o_conv_p"][o, :, ch * 128:(ch + 1) * 128].rearrange("j p -> p j"), xe[:, TP:TP + 3],
                          reads=[bxe], allow_slow_non_contiguous=True)
            for cc in range(2):
                ch = 2 * b + cc
                xc, bxc = XC[cc]
                wa, bwa = self.wload(d["lru_wa"][o, b].rearrange("(j p) k -> p j k", p=128), 128, 2, 256, cast=False)
                wx, bwx = self.wload(d["lru_wx"][o, b].rearrange("(j p) k -> p j k", p=128), 128, 2, 256, cast=False)
                for (t0, mm) in nts:
                    for (wm, bwm, dst, bd, bias) in ((wa, bwa, Aa, bA, self.cs(f"lru_ba{o}", ch)), (wx, bwx, T1, bT, self.cs(f"lru_bx{o}", ch))):
                        ps, bps = self.psum()
                        for j in range(2):
                            P.op("pe", lambda e, ps=ps, wm=wm, j=j, cc=cc, t0=t0, mm=mm: e.matmul(
                                ps[:, 0:mm], lhsT=wm[:, j * 256 + cc * 128:j * 256 + (cc + 1) * 128],
                                rhs=XC[j][0][:, t0:t0 + mm], start=(j == 0), stop=(j == 1)),
                                reads=[bwm, XC[j][1]], writes=[bps])
                        P.op("act", lambda e, ps=ps, dst=dst, bias=bias, t0=t0, mm=mm: e.activation(
                            out=dst[:, t0:t0 + mm], in_=ps[:, 0:mm], func=AF.Sigmoid, bias=bias),
                            reads=[bps, self.bconst], writes=[bd])
                csp, csp2 = self.cs(f"csp{o}", ch), self.cs(f"csp2{o}", ch)
                P.op("act", lambda e, csp2=csp2: e.activation(out=H[:, 0:n], in_=Aa[:, 0:n], func=AF.Exp, scale=csp2),
                     reads=[bA, self.bconst], writes=[bH])
                P.op("act", lambda e: e.activation(out=H[:, 0:n], in_=H[:, 0:n], func=AF.Sqrt, scale=-1.0, bias=self.eps[:, 1:2]),
                     reads=[bH, self.bconst], writes=[bH])
                P.op("act", lambda e, csp=csp: e.activation(out=Aa[:, 0:n], in_=Aa[:, 0:n], func=AF.Exp, scale=csp),
                     reads=[bA, self.bconst], writes=[bA])
                P.op("dve", lambda e: e.tensor_tensor(out=T1[:, 0:n], in0=T1[:, 0:n], in1=H[:, 0:n], op=ALU.mult),
                     reads=[bT, bH], writes=[bT])
                P.op("dve", lambda e, xc=xc: e.tensor_tensor(out=T1[:, 0:n], in0=T1[:, 0:n], in1=xc[:, 0:n], op=ALU.mult),
                     reads=[bT, bxc], writes=[bT])
                init = 0.0 if S else pcol(ch, 3)
                P.op("dve", lambda e, init=init: e.tensor_tensor_scan(
                    out=H[:, 0:TP], data0=Aa[:, 0:TP], data1=T1[:, 0:TP], initial=init, op0=ALU.mult, op1=ALU.add),
                    reads=[bA, bT, self.bpers], writes=[bH])
                if S:
                    h0 = SST[:, ch * 64 + 48:ch * 64 + 64]
                    bh0 = bSST
                    for t in range(4):
                        cs_ = slice(TP + 16 * t, TP + 16 * t + 16)
                        prev = h0 if t == 0 else H[:, TP + 16 * (t - 1):TP + 16 * t]
                        P.op("dve", lambda e, cs_=cs_, prev=prev: e.tensor_tensor(out=H[:, cs_], in0=Aa[:, cs_], in1=prev, op=ALU.mult),
                             reads=[bA, bH, bh0], writes=[bH])
                        P.op("dve", lambda e, cs_=cs_: e.tensor_tensor(out=H[:, cs_], in0=H[:, cs_], in1=T1[:, cs_], op=ALU.add),
                             reads=[bH, bT], writes=[bH])
                    P.op("pool", lambda e, ch=ch: e.tensor_copy(out=pcol(ch, 3), in_=H[:, TP - 1:TP]), reads=[bH], writes=[self.bpers])
                    ps, bps = self.psum()
                    P.op("pe", lambda e, ps=ps: e.transpose(out=ps[0:16, 0:128], in_=H[:, TP + 48:TP + 64], identity=self.ident[:]),
                         reads=[bH, self.bconst], writes=[bps])
                    lo, blo = self.aclaim(f"lso{cc}", off[0] + 256 + cc * 128, 128)
                    P.op("act", lambda e, ps=ps, lo=lo: e.copy(out=lo[0:16, :], in_=ps[0:16, 0:128]), reads=[bps], writes=[blo])
                    P.dma("pool", d["o_lru_s"][:, o, ch * 128:(ch + 1) * 128], lo[0:16, :], reads=[blo])
                else:
                    P.dma("pool", d["o_lru_p"][o, ch * 128:(ch + 1) * 128].rearrange("(p o) -> p o", o=1), H[:, TP - 1:TP],
                          reads=[bH], allow_slow_non_contiguous=True)
                def evg(ps, bps, t0, mm):
                    P.op("act", lambda e: e.copy(out=GL[:, t0:t0 + mm], in_=ps[:, 0:mm]), reads=[bps], writes=[bG])
                self.proj(win[:, ch * 128:(ch + 1) * 128], 128, evg)
                P.op("dve", lambda e: e.tensor_tensor(out=T1[:, 0:n], in0=GL[:, 0:n], in1=GL[:, 0:n], op=ALU.mult), reads=[bG], writes=[bT])
                P.op("dve", lambda e: e.tensor_scalar(out=T1[:, 0:n], in0=T1[:, 0:n], scalar1=0.044715, scalar2=1.0, op0=ALU.mult, op1=ALU.add),
                     reads=[bT], writes=[bT])
                P.op("dve", lambda e: e.tensor_tensor(out=T1[:, 0:n], in0=T1[:, 0:n], in1=GL[:, 0:n], op=ALU.mult), reads=[bT, bG], writes=[bT])
                P.op("act", lambda e: e.activation(out=T1[:, 0:n], in_=T1[:, 0:n], func=AF.Sigmoid, scale=1.5957691216057308),
                     reads=[bT], writes=[bT])
                P.op("dve", lambda e: e.tensor_tensor(out=GL[:, 0:n], in0=GL[:, 0:n], in1=T1[:, 0:n], op=ALU.mult), reads=[bT, bG], writes=[bG])
                P.op("dve", lambda e: e.tensor_tensor(out=ob[:, 0:TP], in0=H[:, 0:TP], in1=GL[:, 0:TP], op=ALU.mult),
                     reads=[bH, bG], writes=[bob])
                if S:
                    P.op("dve", lambda e: e.tensor_tensor(
                        out=ob[:, TP:TP + 64].rearrange("p (s t) -> p s t", t=4),
                        in0=H[:, TP:TP + 64].rearrange("p (t s) -> p s t", s=16),
                        in1=GL[:, TP:TP + 64].rearrange("p (s t) -> p s t", t=4), op=ALU.mult), reads=[bH, bG], writes=[bob])
                self.outproj(wout, ch, ob, bob)

    def even(self, l):
        P, c, d = self.P, self.c, self.d
        e = l // 2
        KC, TP, D, RW, NHP, NHG = c.KC, c.TP, c.D, c.RW, c.NHP, c.NHG
        n, ti, NM, NS = self.n, self.ti, self.NMAX, self.NSEG
        S = (ti == 0)
        bc = self.bconst
        win, wout = d["ev_w_in"][e], d["ev_w_out"][e]
        self.rmsnorm(f"mix_norm{l}")
        off = [0]

        def A(name, size):
            ap, b = self.aclaim(name, off[0], size)
            off[0] += size
            return ap, b
        nts = self.ntiles(n)
        chunks = [(c0, "p") for c0 in range(0, TP, 128)] + ([(TP, "s")] if S else [])
        lor = {}
        for nm in ("zw", "za", "zg0", "zg1"):
            ap, b = A("L" + nm, NM // 2)
            lor[nm] = (ap.bitcast(BF16), b)
        Pb, bPb = A("Pb", 512)
        shs, bshs = A("shs", 16)
        ob_f, bob = A("ob", NM // 2)
        ob = ob_f.bitcast(BF16)
        base_off = off[0]
        R, bR = A("R", NM)
        Kk, bK = A("Kk", NM)
        V, bV = A("V", NM)
        Zt, bZt = R, bR

        def seginfo(nm):
            i, c0, m = self.segs[nm]
            mu = self.cst[0:m, self.ccol[f"mu{e}"] + i:self.ccol[f"mu{e}"] + i + 1]
            om = self.cst[0:m, self.ccol[f"om{e}"] + i:self.ccol[f"om{e}"] + i + 1]
            pv = self.prevp[0:m, e * NS + i:e * NS + i + 1]
            return c0, m, mu, om, pv

        def lerp_proj(nm, Z, bZ):
            c0, m, mu, om, pv = seginfo(nm)
            if S:
                P.dma("sp", shs[0:m, 0:16], d["st_shift"][:, e, c0:c0 + m].rearrange("s c -> c s"), writes=[bshs],
                      allow_slow_non_contiguous=True)

            def evac(ps, bps, t0, mm):
                P.op("act", lambda en: en.copy(out=Pb[0:m, 0:mm], in_=ps[0:m, 0:mm]), reads=[bps], writes=[bPb])
                P.op("dve", lambda en: en.tensor_scalar(out=Z[0:m, t0:t0 + mm], in0=Pb[0:m, 0:mm], scalar1=om, scalar2=None,
                                                        op0=ALU.mult), reads=[bPb, bc], writes=[bZ])
                pe = min(t0 + mm, TP)
                pw = pe - t0
                if pw > 0:
                    if pw > 1:
                        P.op("dve", lambda en: en.scalar_tensor_tensor(
                            out=Z[0:m, t0 + 1:pe], in0=Pb[0:m, 0:pw - 1], scalar=mu, in1=Z[0:m, t0 + 1:pe],
                            op0=ALU.mult, op1=ALU.add), reads=[bPb, bZ, bc], writes=[bZ])
                    P.op("dve", lambda en: en.scalar_tensor_tensor(
                        out=Z[0:m, t0:t0 + 1], in0=pv, scalar=mu, in1=Z[0:m, t0:t0 + 1], op0=ALU.mult, op1=ALU.add),
                        reads=[self.bprev, bZ, bc], writes=[bZ])
                    P.op("dve", lambda en: en.tensor_copy(out=pv, in_=Pb[0:m, pw - 1:pw]), reads=[bPb], writes=[self.bprev])
                    if ti == 1 and pe == TP:
                        P.dma("pool", d["o_shift_p"][e, c0:c0 + m].rearrange("(p o) -> p o", o=1), Pb[0:m, pw - 1:pw],
                              reads=[bPb], allow_slow_non_contiguous=True)
                if t0 + mm > TP:
                    a0 = max(t0, TP) - t0
                    P3 = Pb[0:m, a0:a0 + 64].rearrange("p (s t) -> p s t", t=4)
                    Z3 = Z[0:m, TP:TP + 64].rearrange("p (s t) -> p s t", t=4)
                    P.op("dve", lambda en: en.scalar_tensor_tensor(
                        out=Z3[:, :, 1:4], in0=P3[:, :, 0:3], scalar=mu, in1=Z3[:, :, 1:4], op0=ALU.mult, op1=ALU.add),
                        reads=[bPb, bZ, bc], writes=[bZ])
                    P.op("dve", lambda en: en.scalar_tensor_tensor(
                        out=Z3[:, :, 0:1], in0=shs[0:m, 0:16].rearrange("p (s o) -> p s o", o=1), scalar=mu,
                        in1=Z3[:, :, 0:1], op0=ALU.mult, op1=ALU.add), reads=[bshs, bZ, bc], writes=[bZ])
                    P.dma("pool", d["o_shift_s"][:, e, c0:c0 + m].rearrange("s (c o) -> c s o", o=1), P3[:, :, 3:4],
                          reads=[bPb], allow_slow_non_contiguous=True)
            self.proj(win[:, c0:c0 + m], m, evac)

        for nm, fn in (("zw", AF.Tanh), ("za", AF.Copy), ("zg0", AF.Sigmoid), ("zg1", AF.Sigmoid)):
            m = self.segs[nm][2]
            lerp_proj(nm, Zt, bZt)
            dst, bd = lor[nm]
            P.op("act", lambda en, dst=dst, m=m, fn=fn: en.activation(out=dst[0:m, 0:n], in_=Zt[0:m, 0:n], func=fn),
                 reads=[bZt], writes=[bd])
        TW, ZA, SG0, SG1 = lor["zw"], lor["za"], lor["zg0"], lor["zg1"]
        names = ["SGW", "AA", "G", "KKN", "CS", "E1", "E2", "E3", "T", "BV", "AL", "BE", "KT", "RT", "OT", "Y"]
        W = {}
        for nm in names:
            W[nm] = A(nm, 128)
        TM, bTM = A("TM", 768)
        PA = {nm: A(nm, 256) for nm in ("N", "L", "N2", "L2", "PM", "MAK", "MRB", "MRK")}
        XS, bXS = A("XS", 128)
        Xsb, bXsb = A("Xsb", 128)
        Usb, bUsb = A("Usb", 128)
        OS, bOS = A("OS", 64)
        T2, bT2 = A("T2", 64)
        if S:
            SL, bSL = A("SL", 512)
            STs, bSTs = A("STs", 1024)
            BTq, bBTq = A("BTq", 128)
            KTq, bKTq = A("KTq", 128)
            T2s, bT2s = A("T2s", 256)

        def cv(key, hp):
            return self.cs(f"{key}{e}", hp)

        for hp in range(NHP):
            lerp_proj(f"r{hp}", R, bR)
            lerp_proj(f"k{hp}", Kk, bK)
            lerp_proj(f"v{hp}", V, bV)
            STp = self.ST[:, (e * NHP + hp) * 64:(e * NHP + hp + 1) * 64]
            def do_chunk(hp, c0, kind, STp):
                sm = (kind == "s")
                WD = 64 if sm else 128
                NCI = WD // 64
                sl = slice(c0, c0 + WD)
                t = {k: v[0][:, 0:WD] for k, v in W.items()}
                b = {k: v[1] for k, v in W.items()}
                for (wkey, src, kr, dst, act, bias) in (("rw_w2", TW, 96, "SGW", AF.Sigmoid, cv("rw_w0", hp)),
                                                        ("rw_a2", ZA, 96, "AA", AF.Sigmoid, cv("rw_a0", hp))):
                    w, bw = self.wload(d[wkey][e][:, hp * 128:(hp + 1) * 128].rearrange("(k p) c -> p k c", p=kr), kr, 1, 128,
                                       pool_cast=True)
                    ps, bps = self.psum()
                    P.op("pe", lambda en, ps=ps, w=w, src=src, kr=kr: en.matmul(
                        ps[:, 0:WD], lhsT=w[0:kr, 0:128], rhs=src[0][0:kr, sl], start=True, stop=True),
                        reads=[bw, src[1]], writes=[bps])
                    P.op("act", lambda en, ps=ps, dst=dst, act=act, bias=bias: en.activation(
                        out=t[dst], in_=ps[:, 0:WD], func=act, bias=bias), reads=[bps, bc], writes=[b[dst]])
                w, bw = self.wload(d["rw_g2"][e][:, hp * 128:(hp + 1) * 128].rearrange("(k p) c -> p k c", p=128), 128, 2, 128,
                                   pool_cast=True)
                ps, bps = self.psum()
                for j, sg in enumerate((SG0, SG1)):
                    P.op("pe", lambda en, ps=ps, w=w, sg=sg, j=j: en.matmul(
                        ps[:, 0:WD], lhsT=w[:, j * 128:(j + 1) * 128], rhs=sg[0][:, sl], start=(j == 0), stop=(j == 1)),
                        reads=[bw, sg[1]], writes=[bps])
                P.op("act", lambda en, ps=ps: en.copy(out=t["G"], in_=ps[:, 0:WD]), reads=[bps], writes=[b["G"]])
                dv = lambda fn, rd, wr: P.op("dve", fn, reads=rd, writes=wr)
                ac = lambda fn, rd, wr: P.op("act", fn, reads=rd, writes=wr)
                dv(lambda en: en.tensor_scalar(out=t["SGW"], in0=t["SGW"], scalar1=-0.6065306597126334, scalar2=None, op0=ALU.mult),
                   [b["SGW"]], [b["SGW"]])
                dv(lambda en: en.tensor_scalar(out=t["KKN"], in0=Kk[:, sl], scalar1=cv("rw_k_k", hp), scalar2=None, op0=ALU.mult),
                   [bK, bc], [b["KKN"]])
                dv(lambda en: en.tensor_tensor(out=t["T"], in0=t["KKN"], in1=t["KKN"], op=ALU.mult), [b["KKN"]], [b["T"]])
                ps, bps = self.psum()
                P.op("pe", lambda en, ps=ps: en.matmul(ps[:, 0:WD], lhsT=self.bones[:], rhs=t["T"], start=True, stop=True),
                     reads=[b["T"], bc], writes=[bps])
                dv(lambda en, ps=ps: en.tensor_scalar(out=t["T"], in0=ps[:, 0:WD], scalar1=1e-19, scalar2=None, op0=ALU.max),
                   [bps], [b["T"]])
                ac(lambda en: en.activation(out=t["T"], in_=t["T"], func=AF.Ln), [b["T"]], [b["T"]])
                ac(lambda en: en.activation(out=t["T"], in_=t["T"], func=AF.Exp, scale=-0.5), [b["T"]], [b["T"]])
                dv(lambda en: en.tensor_tensor(out=t["KKN"], in0=t["KKN"], in1=t["T"], op=ALU.mult), [b["KKN"], b["T"]], [b["KKN"]])
                dv(lambda en: en.tensor_scalar(out=t["T"], in0=t["AA"], scalar1=-1.0, scalar2=cv("rw_k_a", hp), op0=ALU.add, op1=ALU.mult),
                   [b["AA"], bc], [b["T"]])
                dv(lambda en: en.scalar_tensor_tensor(out=t["KT"], in0=t["T"], scalar=1.0, in1=Kk[:, sl], op0=ALU.add, op1=ALU.mult),
                   [b["T"], bK], [b["KT"]])
                dv(lambda en: en.tensor_tensor(out=t["T"], in0=R[:, sl], in1=t["KT"], op=ALU.mult), [bR, b["KT"]], [b["T"]])
                dv(lambda en: en.tensor_scalar(out=t["T"], in0=t["T"], scalar1=cv("rw_r_k", hp), scalar2=None, op0=ALU.mult),
                   [b["T"], bc], [b["T"]])
                ps, bps = self.psum()
                P.op("pe", lambda en, ps=ps: en.matmul(ps[:, 0:WD], lhsT=self.bones[:], rhs=t["T"], start=True, stop=True),
                     reads=[b["T"], bc], writes=[bps])
                dv(lambda en, ps=ps: en.tensor_tensor(out=t["BV"], in0=ps[:, 0:WD], in1=V[:, sl], op=ALU.mult), [bps, bV], [b["BV"]])
                dv(lambda en: en.tensor_tensor(out=t["BE"], in0=t["KKN"], in1=t["AA"], op=ALU.mult), [b["KKN"], b["AA"]], [b["BE"]])
                rst = self.mrst_s[:, 0:64] if sm else self.mrst[:, 0:128]
                dv(lambda en, rst=rst: en.tensor_tensor_scan(out=t["CS"], data0=rst, data1=t["SGW"], initial=0.0,
                                                             op0=ALU.mult, op1=ALU.add), [b["SGW"], bc], [b["CS"]])
                ac(lambda en: en.activation(out=t["E1"], in_=t["CS"], func=AF.Exp), [b["CS"]], [b["E1"]])
                ac(lambda en: en.activation(out=t["E2"], in_=t["CS"], func=AF.Exp, scale=-1.0), [b["CS"]], [b["E2"]])
                dv(lambda en: en.tensor_tensor(out=t["T"], in0=t["CS"], in1=t["SGW"], op=ALU.subtract), [b["CS"], b["SGW"]], [b["T"]])
                ac(lambda en: en.activation(out=t["E3"], in_=t["T"], func=AF.Exp), [b["T"]], [b["E3"]])
                dv(lambda en: en.tensor_tensor(out=t["RT"], in0=R[:, sl], in1=t["E1"], op=ALU.mult), [bR, b["E1"]], [b["RT"]])
                dv(lambda en: en.tensor_tensor(out=t["KT"], in0=t["KT"], in1=t["E2"], op=ALU.mult), [b["KT"], b["E2"]], [b["KT"]])
                dv(lambda en: en.tensor_tensor(out=t["BE"], in0=t["BE"], in1=t["E2"], op=ALU.mult), [b["BE"], b["E2"]], [b["BE"]])
                dv(lambda en: en.scalar_tensor_tensor(out=t["AL"], in0=t["KKN"], scalar=-1.0, in1=t["E3"], op0=ALU.mult, op1=ALU.mult),
                   [b["KKN"], b["E3"]], [b["AL"]])
                for ci in range(NCI):
                    cs_ = slice(ci * 64, (ci + 1) * 64)
                    ps, bps = self.psum()
                    for j, (src, bs_) in enumerate(((V[:, c0 + ci * 64:c0 + (ci + 1) * 64], bV), (t["BE"][:, cs_], b["BE"]),
                                                    (t["KT"][:, cs_], b["KT"]))):
                        P.op("pe", lambda en, ps=ps, src=src, j=j: en.transpose(out=ps[0:64, j * 128:(j + 1) * 128], in_=src,
                                                                                 identity=self.ident[:]),
                             reads=[bs_, bc], writes=[bps])
                    ac(lambda en, ps=ps, ci=ci: en.copy(out=TM[0:64, ci * 384:(ci + 1) * 384], in_=ps[0:64, 0:384]), [bps], [bTM])
                VM = lambda ci, h2: TM[0:64, ci * 384 + h2 * 64:ci * 384 + (h2 + 1) * 64]
                BT = lambda ci, h2: TM[0:64, ci * 384 + 128 + h2 * 64:ci * 384 + 128 + (h2 + 1) * 64]
                KTm = lambda ci, h2: TM[0:64, ci * 384 + 256 + h2 * 64:ci * 384 + 256 + (h2 + 1) * 64]
                msu, mui, msl = (self.msu_s, self.mui_s, self.msl_s) if sm else (self.msu, self.mui, self.msl)
                NSL = 2 * NCI
                SW = NSL * 64
                LOWP = ("N", "L", "N2", "L2", "PM")
                pa = {k: (v[0].bitcast(BF16)[:, 0:SW] if k in LOWP else v[0][:, 0:SW]) for k, v in PA.items()}
                pb = {k: v[1] for k, v in PA.items()}
                Xb = Xsb.bitcast(BF16)
                slot = lambda h2, ci: slice((h2 * NCI + ci) * 64, (h2 * NCI + ci + 1) * 64)

                def kstage(dst, lhs, rhs, mask):
                    for h2 in range(2):
                        ps, bps = self.psum()
                        hs = slice(h2 * 64, (h2 + 1) * 64)
                        for ci in range(NCI):
                            cs_ = slice(ci * 64, (ci + 1) * 64)
                            P.op("pe", lambda en, ps=ps, hs=hs, cs_=cs_: en.matmul(ps[0:64, cs_], lhsT=t[lhs][hs, cs_], rhs=t[rhs][hs, cs_],
                                                                                   start=True, stop=True),
                                 reads=[b[lhs], b[rhs]], writes=[bps])
                        dv(lambda en, ps=ps, h2=h2: en.tensor_tensor(
                            out=pa[dst][0:64, h2 * NCI * 64:(h2 + 1) * NCI * 64], in0=ps[0:64, 0:NCI * 64], in1=mask[0:64, 0:NCI * 64],
                            op=ALU.mult), [bps, bc], [pb[dst]])
                kstage("N", "BE", "AL", msu)
                kstage("L", "AL", "BE", msl)
                kstage("MAK", "KT", "AL", msu)
                kstage("MRB", "BE", "RT", mui)
                kstage("MRK", "KT", "RT", mui)
                dv(lambda en: en.tensor_tensor(out=pa["PM"][0:64, :], in0=pa["N"][0:64, :], in1=self.id4[0:64, 0:SW], op=ALU.add),
                   [pb["N"], bc], [pb["PM"]])
                cur = ("N", "L", "N2", "L2")
                for step in range(1 if sm else 5):
                    Nn, Ln, N2n, L2n = cur
                    for (dst, lh, rh, eng) in ((N2n, Ln, Nn, "act"), (L2n, Nn, Ln, "dve")):
                        ps, bps = self.psum()
                        for u in range(NSL):
                            us = slice(u * 64, (u + 1) * 64)
                            P.op("pe", lambda en, ps=ps, us=us, lh=lh, rh=rh: en.matmul(
                                ps[0:64, us], lhsT=pa[lh][0:64, us], rhs=pa[rh][0:64, us], start=True, stop=True),
                                reads=[pb[lh], pb[rh]], writes=[bps])
                        if eng == "act":
                            ac(lambda en, ps=ps, dst=dst: en.copy(out=pa[dst][0:64, :], in_=ps[0:64, 0:SW]), [bps], [pb[dst]])
                        else:
                            dv(lambda en, ps=ps, dst=dst: en.tensor_copy(out=pa[dst][0:64, :], in_=ps[0:64, 0:SW]), [bps], [pb[dst]])
                    ps, bps = self.psum()
                    for u in range(NSL):
                        us = slice(u * 64, (u + 1) * 64)
                        P.op("pe", lambda en, ps=ps, us=us, L2n=L2n: en.matmul(
                            ps[0:64, us], lhsT=pa[L2n][0:64, us], rhs=pa["PM"][0:64, us], start=True, stop=True),
                            reads=[pb[L2n], pb["PM"]], writes=[bps])
                    dv(lambda en, ps=ps: en.tensor_tensor(out=pa["PM"][0:64, :], in0=ps[0:64, 0:SW], in1=pa["PM"][0:64, :], op=ALU.add),
                       [bps, pb["PM"]], [pb["PM"]])
                    cur = (N2n, L2n, Nn, Ln)
                if sm:
                    for g in range(4):
                        for h2 in range(2):
                            P.dma("sp", SL[0:64, :].rearrange("p (s c) -> p s c", c=128)[:, :, h2 * 64:(h2 + 1) * 64],
                                  d["st_rwkv"][g * 4:(g + 1) * 4, e, 2 * hp + h2].rearrange("s v k -> v s k"), writes=[bSL])
                        ps, bps = self.psum()
                        for q in range(4):
                            P.op("pe", lambda en, ps=ps, q=q: en.transpose(out=ps[:, q * 64:(q + 1) * 64],
                                                                           in_=SL[0:64, q * 128:(q + 1) * 128], identity=self.ident[0:64, 0:64]),
                                 reads=[bSL, bc], writes=[bps])
                        ac(lambda en, ps=ps, g=g: en.copy(out=STs[:, g * 256:(g + 1) * 256], in_=ps[:, 0:256]), [bps], [bSTs])
                for ci in range(NCI):
                    co = ci * 64
                    if sm:
                        subs = [(4 * q, 4, STs[:, q * 64:(q + 1) * 64], bSTs) for q in range(16)]
                    else:
                        subs = [(0, 64, STp, self.bST)]
                    for h2 in range(2):
                        hs = slice(h2 * 64, (h2 + 1) * 64)
                        ps, bps = self.psum()
                        ps2, bps2 = self.psum()
                        for (o_, ln, Ssrc, bS_) in subs:
                            P.op("pe", lambda en, ps=ps, o_=o_, ln=ln, Ssrc=Ssrc, hs=hs, co=co: en.matmul(
                                ps[0:64, o_:o_ + ln], lhsT=Ssrc[hs, :], rhs=t["AL"][hs, co + o_:co + o_ + ln], start=True, stop=True),
                                reads=[bS_, b["AL"]], writes=[bps])
                            P.op("pe", lambda en, ps2=ps2, o_=o_, ln=ln, Ssrc=Ssrc, hs=hs, co=co: en.matmul(
                                ps2[hs, o_:o_ + ln], lhsT=Ssrc[hs, :], rhs=t["RT"][hs, co + o_:co + o_ + ln], start=True, stop=True),
                                reads=[bS_, b["RT"]], writes=[bps2])
                        ac(lambda en, ps=ps, hs=hs: en.copy(out=XS[0:64, hs], in_=ps[0:64, 0:64]), [bps], [bXS])
                        ac(lambda en, ps2=ps2, hs=hs: en.copy(out=OS[hs, 0:64], in_=ps2[hs, 0:64]), [bps2], [bOS])
                    ps, bps = self.psum()
                    for h2 in range(2):
                        hs = slice(h2 * 64, (h2 + 1) * 64)
                        P.op("pe", lambda en, ps=ps, hs=hs, h2=h2, ci=ci: en.matmul(ps[0:64, hs], lhsT=pa["MAK"][0:64, slot(h2, ci)],
                                                                                    rhs=VM(ci, h2), start=True, stop=False),
                             reads=[pb["MAK"], bTM], writes=[bps])
                        P.op("pe", lambda en, ps=ps, hs=hs: en.matmul(ps[0:64, hs], lhsT=XS[0:64, hs], rhs=self.ident[0:64, 0:64],
                                                                      start=False, stop=True),
                             reads=[bXS, bc], writes=[bps])
                    ac(lambda en, ps=ps: en.copy(out=Xb[0:64, 0:128], in_=ps[0:64, 0:128]), [bps], [bXsb])
                    ps, bps = self.psum()
                    for h2 in range(2):
                        hs = slice(h2 * 64, (h2 + 1) * 64)
                        P.op("pe", lambda en, ps=ps, hs=hs, h2=h2, ci=ci: en.matmul(ps[0:64, hs], lhsT=pa["PM"][0:64, slot(h2, ci)],
                                                                                    rhs=Xb[0:64, hs], start=True, stop=True),
                             reads=[pb["PM"], bXsb], writes=[bps])
                    ac(lambda en, ps=ps: en.copy(out=Usb[0:64, :], in_=ps[0:64, 0:128]), [bps], [bUsb])
                    ps, bps = self.psum()
                    for h2 in range(2):
                        hs = slice(h2 * 64, (h2 + 1) * 64)
                        P.op("pe", lambda en, ps=ps, hs=hs, h2=h2, ci=ci: en.matmul(ps[hs, 0:64], lhsT=Usb[0:64, hs],
                                                                                    rhs=pa["MRB"][0:64, slot(h2, ci)], start=True, stop=False),
                             reads=[bUsb, pb["MRB"]], writes=[bps])
                        P.op("pe", lambda en, ps=ps, hs=hs, h2=h2, ci=ci: en.matmul(ps[hs, 0:64], lhsT=VM(ci, h2),
                                                                                    rhs=pa["MRK"][0:64, slot(h2, ci)], start=False, stop=True),
                             reads=[bTM, pb["MRK"]], writes=[bps])
                    dv(lambda en, ps=ps, co=co: en.tensor_tensor(out=t["OT"][:, co:co + 64], in0=ps[:, 0:64], in1=OS[:, 0:64], op=ALU.add),
                       [bps, bOS], [b["OT"]])
                    if not sm:
                        ps, bps = self.psum()
                        for h2 in range(2):
                            hs = slice(h2 * 64, (h2 + 1) * 64)
                            P.op("pe", lambda en, ps=ps, hs=hs, h2=h2, ci=ci: en.matmul(ps[hs, 0:64], lhsT=BT(ci, h2), rhs=Usb[0:64, hs],
                                                                                        start=True, stop=False),
                                 reads=[bTM, bUsb], writes=[bps])
                            P.op("pe", lambda en, ps=ps, hs=hs, h2=h2, ci=ci: en.matmul(ps[hs, 0:64], lhsT=KTm(ci, h2), rhs=VM(ci, h2),
                                                                                        start=False, stop=True),
                                 reads=[bTM], writes=[bps])
                        dv(lambda en, ps=ps: en.tensor_tensor(out=T2[:, 0:64], in0=ps[:, 0:64], in1=STp, op=ALU.add), [bps, self.bST], [bT2])
                        dv(lambda en, co=co: en.tensor_scalar(out=STp, in0=T2[:, 0:64], scalar1=t["E1"][:, co + 63:co + 64], scalar2=None,
                                                              op0=ALU.mult), [bT2, b["E1"]], [self.bST])
                        if ti == 1 and c0 + co == TP - 64:
                            ps, bps = self.psum()
                            P.op("pe", lambda en, ps=ps: en.transpose(out=ps[0:64, 0:128], in_=STp, identity=self.ident[:]),
                                 reads=[self.bST, bc], writes=[bps])
                            ac(lambda en, ps=ps: en.copy(out=Xsb[0:64, :], in_=ps[0:64, 0:128]), [bps], [bXsb])
                            P.dma("pool", d["o_rwkv_p"][e, 2 * hp:2 * hp + 2].rearrange("h v k -> v h k"),
                                  Xsb[0:64, :].rearrange("p (h k) -> p h k", k=64), reads=[bXsb])
                    else:
                        for g in range(4):
                            ps, bps = self.psum()
                            for q4 in range(4):
                                q = g * 4 + q4
                                dv(lambda en, q=q: en.tensor_scalar(out=BTq[0:64, :], in0=TM[0:64, 128:256], scalar1=self.rowm[:, q:q + 1],
                                                                    scalar2=None, op0=ALU.mult), [bTM, bc], [bBTq])
                                dv(lambda en, q=q: en.tensor_scalar(out=KTq[0:64, :], in0=TM[0:64, 256:384], scalar1=self.rowm[:, q:q + 1],
                                                                    scalar2=None, op0=ALU.mult), [bTM, bc], [bKTq])
                                for h2 in range(2):
                                    hs = slice(h2 * 64, (h2 + 1) * 64)
                                    P.op("pe", lambda en, ps=ps, hs=hs, q4=q4: en.matmul(
                                        ps[hs, q4 * 64:(q4 + 1) * 64], lhsT=BTq[0:64, hs], rhs=Usb[0:64, hs], start=True, stop=False),
                                        reads=[bBTq, bUsb], writes=[bps])
                                    P.op("pe", lambda en, ps=ps, hs=hs, h2=h2, q4=q4: en.matmul(
                                        ps[hs, q4 * 64:(q4 + 1) * 64], lhsT=KTq[0:64, hs], rhs=VM(0, h2), start=False, stop=True),
                                        reads=[bKTq, bTM], writes=[bps])
                            gs = slice(g * 256, (g + 1) * 256)
                            dv(lambda en, ps=ps, gs=gs: en.tensor_tensor(out=T2s[:, 0:256], in0=ps[:, 0:256], in1=STs[:, gs], op=ALU.add),
                               [bps, bSTs], [bT2s])
                            lam = t["E1"][:, 0:64].rearrange("p (s t) -> p s t", t=4)[:, g * 4:(g + 1) * 4, 3:4]
                            dv(lambda en, gs=gs, lam=lam: en.tensor_tensor(
                                out=STs[:, gs].rearrange("p (s v) -> p s v", v=64), in0=T2s[:, 0:256].rearrange("p (s v) -> p s v", v=64),
                                in1=lam.broadcast_to([128, 4, 64]), op=ALU.mult), [bT2s, b["E1"]], [bSTs])
                            ps, bps = self.psum()
                            for q4 in range(4):
                                P.op("pe", lambda en, ps=ps, q4=q4, g=g: en.transpose(
                                    out=ps[0:64, q4 * 128:(q4 + 1) * 128], in_=STs[:, (g * 4 + q4) * 64:(g * 4 + q4 + 1) * 64],
                                    identity=self.ident[:]), reads=[bSTs, bc], writes=[bps])
                            ac(lambda en, ps=ps: en.copy(out=SL[0:64, 0:512], in_=ps[0:64, 0:512]), [bps], [bSL])
                            for h2 in range(2):
                                P.dma("pool", d["o_rwkv_s"][g * 4:(g + 1) * 4, e, 2 * hp + h2].rearrange("s v k -> v s k"),
                                      SL[0:64, :].rearrange("p (s c) -> p s c", c=128)[:, :, h2 * 64:(h2 + 1) * 64], reads=[bSL])
                ps, bps = self.psum()
                P.op("pe", lambda en, ps=ps: en.matmul(ps[:, 0:WD], lhsT=self.bones[:], rhs=t["OT"], start=True, stop=True),
                     reads=[b["OT"], bc], writes=[bps])
                dv(lambda en, ps=ps: en.scalar_tensor_tensor(out=t["Y"], in0=ps[:, 0:WD], scalar=-1.0 / 64, in1=t["OT"],
                                                             op0=ALU.mult, op1=ALU.add), [bps, b["OT"]], [b["Y"]])
                dv(lambda en: en.tensor_tensor(out=t["T"], in0=t["Y"], in1=t["Y"], op=ALU.mult), [b["Y"]], [b["T"]])
                ps, bps = self.psum()
                P.op("pe", lambda en, ps=ps: en.matmul(ps[:, 0:WD], lhsT=self.bones[:], rhs=t["T"], start=True, stop=True),
                     reads=[b["T"], bc], writes=[bps])
                ac(lambda en, ps=ps: en.activation(out=t["T"], in_=ps[:, 0:WD], func=AF.Ln, scale=1.0 / 64, bias=self.eps2[:, 0:1]),
                   [bps, bc], [b["T"]])
                ac(lambda en: en.activation(out=t["T"], in_=t["T"], func=AF.Exp, scale=-0.5), [b["T"]], [b["T"]])
                dv(lambda en: en.tensor_tensor(out=t["Y"], in0=t["Y"], in1=t["T"], op=ALU.mult), [b["Y"], b["T"]], [b["Y"]])
                dv(lambda en: en.tensor_scalar(out=t["Y"], in0=t["Y"], scalar1=cv("rw_ln_w", hp), scalar2=cv("rw_ln_b", hp),
                                               op0=ALU.mult, op1=ALU.add), [b["Y"], bc], [b["Y"]])
                dv(lambda en: en.tensor_tensor(out=t["Y"], in0=t["Y"], in1=t["BV"], op=ALU.add), [b["Y"], b["BV"]], [b["Y"]])
                dv(lambda en: en.tensor_tensor(out=ob[:, sl], in0=t["Y"], in1=t["G"], op=ALU.mult), [b["Y"], b["G"]], [bob])
            for (c0, kind) in chunks:
                do_chunk(hp, c0, kind, STp)
            self.outproj(wout, hp, ob, bob)
        if "nohg" not in self.stages:
            self.hgrn(l, base_off, ob, bob)

    def hgrn(self, l, base_off, ob, bob):
        P, c, d = self.P, self.c, self.d
        e = l // 2
        KC, TP, D, RW, NHP, NHG = c.KC, c.TP, c.D, c.RW, c.NHP, c.NHG
        n, ti, NM = self.n, self.ti, self.NMAX
        S = (ti == 0)
        bc = self.bconst
        win, wout = d["ev_w_in"][e], d["ev_w_out"][e]
        off = [base_off]

        def A(name, size):
            ap, b = self.aclaim(name, off[0], size)
            off[0] += size
            return ap, b
        chunks = [(c0, "p") for c0 in range(0, TP, 128)] + ([(TP, "s")] if S else [])
        Q, bQ = A("hQ", NM)
        LF, bLF = A("hLF", NM)
        KH, bKH = A("hKH", NM)
        IV, bIV = A("hIV", NM)
        OG, bOG = A("hOG", NM)
        W = {nm: A("h" + nm, 128) for nm in ("G", "EG", "QP", "KP", "KPP", "T", "OT", "Y")}
        TMf, bTM = A("hTM", 512)
        ATf, bAT = A("hAT", 128)
        if S:
            Ss, bSs = A("hSs", 2048)
            KQ, bKQ = A("hKQ", 128)
        P.op("pool", lambda en: en.memset(TMf[:, :], 0.0), writes=[bTM])
        P.op("pool", lambda en: en.memset(ATf[:, :], 0.0), writes=[bAT])
        P.op("pool", lambda en: en.memset(KQ[:, :], 0.0), writes=[bKQ]) if S else None
        Wt = W
        b = {k: v[1] for k, v in W.items()}
        dv = lambda fn, rd, wr: P.op("dve", fn, reads=rd, writes=wr)
        ac = lambda fn, rd, wr: P.op("act", fn, reads=rd, writes=wr)
        for hg in range(NHG):
            lb = self.cs(f"lb{e}", hg)
            oml = self.cs(f"oml{e}", hg)
            colq = c.P_RW + hg * 128

            def ev_q(ps, bps, t0, mm):
                ac(lambda en: en.activation(out=Q[:, t0:t0 + mm], in_=ps[:, 0:mm], func=AF.Silu), [bps], [bQ])

            def ev_f(ps, bps, t0, mm, lb=lb, oml=oml):
                ac(lambda en: en.activation(out=KH[:, t0:t0 + mm], in_=ps[:, 0:mm], func=AF.Sigmoid), [bps], [bKH])
                dv(lambda en: en.tensor_scalar(out=KH[:, t0:t0 + mm], in0=KH[:, t0:t0 + mm], scalar1=oml, scalar2=lb,
                                               op0=ALU.mult, op1=ALU.add), [bKH, bc], [bKH])
                ac(lambda en: en.activation(out=LF[:, t0:t0 + mm], in_=KH[:, t0:t0 + mm], func=AF.Ln), [bKH], [bLF])
                dv(lambda en: en.tensor_scalar(out=KH[:, t0:t0 + mm], in0=KH[:, t0:t0 + mm], scalar1=-1.0, scalar2=1.0,
                                               op0=ALU.mult, op1=ALU.add), [bKH, bLF], [bKH])

            def ev_i(ps, bps, t0, mm):
                ac(lambda en: en.copy(out=IV[:, t0:t0 + mm], in_=ps[:, 0:mm]), [bps], [bIV])

            def ev_o(ps, bps, t0, mm):
                ac(lambda en: en.activation(out=OG[:, t0:t0 + mm], in_=ps[:, 0:mm], func=AF.Silu), [bps], [bOG])
            self.proj(win[:, colq:colq + 128], 128, ev_q)
            self.proj(win[:, colq + RW:colq + RW + 128], 128, ev_f)
            self.proj(win[:, colq + 2 * RW:colq + 2 * RW + 128], 128, ev_i)
            self.proj(win[:, colq + 3 * RW:colq + 3 * RW + 128], 128, ev_o)
            SHp = self.SH[:, (e * NHG + hg) * 128:(e * NHG + hg + 1) * 128]

            def do_chunk(hg, c0, kind, SHp):
                sm = (kind == "s")
                WD = 64 if sm else 128
                NCI = WD // 64
                sl = slice(c0, c0 + WD)
                t = {k: v[0][:, 0:WD] for k, v in Wt.items()}
                rst = self.mrst_s[:, 0:64] if sm else self.mrst[:, 0:128]
                dv(lambda en: en.tensor_tensor_scan(out=t["G"], data0=rst, data1=LF[:, sl], initial=0.0, op0=ALU.mult, op1=ALU.add),
                   [bLF, bc], [b["G"]])
                ac(lambda en: en.activation(out=t["EG"], in_=t["G"], func=AF.Exp), [b["G"]], [b["EG"]])
                dv(lambda en: en.tensor_tensor(out=t["QP"], in0=Q[:, sl], in1=t["EG"], op=ALU.mult), [bQ, b["EG"]], [b["QP"]])
                ac(lambda en: en.activation(out=t["T"], in_=t["G"], func=AF.Exp, scale=-1.0), [b["G"]], [b["T"]])
                dv(lambda en: en.tensor_tensor(out=t["KP"], in0=KH[:, sl], in1=t["T"], op=ALU.mult), [bKH, b["T"]], [b["KP"]])
                tl = 4 if sm else 64
                g3 = t["G"].rearrange("p (s t) -> p s t", t=tl)
                dv(lambda en: en.tensor_tensor(out=t["T"].rearrange("p (s t) -> p s t", t=tl),
                                               in0=g3[:, :, tl - 1:tl].broadcast_to([128, WD // tl, tl]), in1=g3, op=ALU.subtract),
                   [b["G"]], [b["T"]])
                ac(lambda en: en.activation(out=t["T"], in_=t["T"], func=AF.Exp), [b["T"]], [b["T"]])
                dv(lambda en: en.tensor_tensor(out=t["KPP"], in0=KH[:, sl], in1=t["T"], op=ALU.mult), [bKH, b["T"]], [b["KPP"]])
                mui = self.mui_s if sm else self.mui
                psA, bpsA = self.psum()
                for ci in range(NCI):
                    cs_ = slice(ci * 64, (ci + 1) * 64)
                    ps, bps = self.psum()
                    P.op("pe", lambda en, ps=ps, ci=ci: en.transpose(out=ps[0:64, 0:128], in_=IV[:, c0 + ci * 64:c0 + (ci + 1) * 64],
                                                                     identity=self.ident[:]), reads=[bIV, bc], writes=[bps])
                    P.op("pe", lambda en, ps=ps, cs_=cs_: en.transpose(out=ps[0:64, 128:256], in_=t["KPP"][:, cs_], identity=self.ident[:]),
                         reads=[b["KPP"], bc], writes=[bps])
                    ac(lambda en, ps=ps, ci=ci: en.copy(out=TMf[0:64, ci * 256:(ci + 1) * 256], in_=ps[0:64, 0:256]), [bps], [bTM])
                    P.op("pe", lambda en, cs_=cs_: en.matmul(psA[0:64, cs_], lhsT=t["KP"][:, cs_], rhs=t["QP"][:, cs_], start=True, stop=True),
                         reads=[b["KP"], b["QP"]], writes=[bpsA])
                dv(lambda en: en.tensor_tensor(out=ATf[0:64, 0:WD], in0=psA[0:64, 0:WD], in1=mui[0:64, 0:WD], op=ALU.mult),
                   [bpsA, bc], [bAT])
                if sm:
                    P.dma("sp", Ss[:, :].rearrange("p (s v) -> p s v", v=128), d["st_hgrn"][:, e, hg].rearrange("s k v -> k s v"),
                          writes=[bSs])
                for ci in range(NCI):
                    cs_ = slice(ci * 64, (ci + 1) * 64)
                    IM = TMf[:, ci * 256:ci * 256 + 128]
                    KT2 = TMf[:, ci * 256 + 128:ci * 256 + 256]
                    ps, bps = self.psum()
                    P.op("pe", lambda en, ps=ps, IM=IM, cs_=cs_: en.matmul(ps[:, 0:64], lhsT=IM, rhs=ATf[:, cs_], start=True, stop=False),
                         reads=[bTM, bAT], writes=[bps])
                    if sm:
                        for q in range(16):
                            P.op("pe", lambda en, ps=ps, q=q: en.matmul(ps[:, 4 * q:4 * q + 4], lhsT=Ss[:, q * 128:(q + 1) * 128],
                                                                        rhs=t["QP"][:, 4 * q:4 * q + 4], start=False, stop=(q == 15)),
                                 reads=[bSs, b["QP"]], writes=[bps])
                    else:
                        P.op("pe", lambda en, ps=ps, cs_=cs_: en.matmul(ps[:, 0:64], lhsT=SHp, rhs=t["QP"][:, cs_], start=False, stop=True),
                             reads=[self.bSH, b["QP"]], writes=[bps])
                    ac(lambda en, ps=ps, cs_=cs_: en.copy(out=t["OT"][:, cs_], in_=ps[:, 0:64]), [bps], [b["OT"]])
                    if not sm:
                        ps, bps = self.psum()
                        P.op("pe", lambda en, ps=ps, IM=IM, KT2=KT2: en.matmul(ps[:, 0:128], lhsT=KT2, rhs=IM, start=True, stop=True),
                             reads=[bTM], writes=[bps])
                        dv(lambda en, ps=ps, ci=ci: en.scalar_tensor_tensor(out=SHp, in0=SHp, scalar=t["EG"][:, ci * 64 + 63:ci * 64 + 64],
                                                                            in1=ps[:, 0:128], op0=ALU.mult, op1=ALU.add),
                           [bps, self.bSH, b["EG"]], [self.bSH])
                        if ti == 1 and c0 + ci * 64 == TP - 64:
                            P.dma("pool", d["o_hgrn_p"][e, hg], SHp, reads=[self.bSH])
                    else:
                        eg3 = t["EG"][:, 0:64].rearrange("p (s t) -> p s t", t=4)
                        for g in range(4):
                            ps, bps = self.psum()
                            for q4 in range(4):
                                q = 4 * g + q4
                                dv(lambda en, q=q: en.tensor_scalar(out=KQ[0:64, :], in0=TMf[0:64, 128:256], scalar1=self.rowm[:, q:q + 1],
                                                                    scalar2=None, op0=ALU.mult), [bTM, bc], [bKQ])
                                P.op("pe", lambda en, ps=ps, q4=q4, IM=IM: en.matmul(ps[:, q4 * 128:(q4 + 1) * 128], lhsT=KQ[:, :], rhs=IM,
                                                                                     start=True, stop=True), reads=[bKQ, bTM], writes=[bps])
                            gs = slice(g * 512, (g + 1) * 512)
                            dv(lambda en, gs=gs, g=g: en.tensor_tensor(
                                out=Ss[:, gs].rearrange("p (s v) -> p s v", v=128), in0=Ss[:, gs].rearrange("p (s v) -> p s v", v=128),
                                in1=eg3[:, 4 * g:4 * g + 4, 3:4].broadcast_to([128, 4, 128]), op=ALU.mult), [bSs, b["EG"]], [bSs])
                            dv(lambda en, ps=ps, gs=gs: en.tensor_tensor(out=Ss[:, gs], in0=ps[:, 0:512], in1=Ss[:, gs], op=ALU.add),
                               [bps, bSs], [bSs])
                        P.dma("pool", d["o_hgrn_s"][:, e, hg].rearrange("s k v -> k s v"), Ss[:, :].rearrange("p (s v) -> p s v", v=128),
                              reads=[bSs])
                dv(lambda en: en.tensor_tensor(out=t["T"], in0=t["OT"], in1=t["OT"], op=ALU.mult), [b["OT"]], [b["T"]])
                ps, bps = self.psum()
                P.op("pe", lambda en, ps=ps: en.matmul(ps[:, 0:WD], lhsT=self.ones[:], rhs=t["T"], start=True, stop=True),
                     reads=[b["T"], bc], writes=[bps])
                ac(lambda en, ps=ps: en.activation(out=t["T"], in_=ps[:, 0:WD], func=AF.Ln, scale=1.0 / 128, bias=self.eps2[:, 1:2]),
                   [bps, bc], [b["T"]])
                ac(lambda en: en.activation(out=t["T"], in_=t["T"], func=AF.Exp, scale=-0.5), [b["T"]], [b["T"]])
                dv(lambda en: en.scalar_tensor_tensor(out=t["Y"], in0=t["OT"], scalar=self.cs(f"hg_norm{e}", hg), in1=t["T"],
                                                      op0=ALU.mult, op1=ALU.mult), [b["OT"], b["T"], bc], [b["Y"]])
                dv(lambda en: en.tensor_tensor(out=ob[:, sl], in0=t["Y"], in1=OG[:, sl], op=ALU.mult), [b["Y"], bOG], [bob])
            for (c0, kind) in chunks:
                do_chunk(hg, c0, kind, SHp)
            self.outproj(wout, NHP + hg, ob, bob)

    def tile(self, ti):
        c = self.c
        self.ti = ti
        self.n = c.TP + (64 if ti == 0 else 0)
        self.load_x(ti)
        for l in range(c.DEPTH):
            if "ffn" in self.stages:
                self.ffn(l, 1)
            if l % 2 == 0 and "even" in self.stages:
                self.even(l)
            if l % 2 == 1 and "odd" in self.stages:
                self.odd(l)
            if "ffn" in self.stages:
                self.ffn(l, 2)
        self.store_y(ti)


INPUT_ORDER = ["x_prompt", "x_sample", "state_rwkv_shift", "state_rwkv", "state_hgrn", "state_conv", "state_lru"]


def make_in_maps(cfg, inputs):
    f = lambda a: np.ascontiguousarray(np.asarray(a, dtype=np.float32))
    nb = inputs["x_prompt"].shape[0]
    maps = []
    for core in range(NCORES):
        m = {}
        if core < nb:
            m["xp"] = f(inputs["x_prompt"][core])
        else:
            m["xp"] = np.zeros((cfg.SEQ, cfg.D), np.float32)
        sl = slice(core * 16, (core + 1) * 16)
        m["xs"] = f(inputs["x_sample"][sl]).reshape(64, cfg.D)
        m["st_shift"] = f(inputs["state_rwkv_shift"][sl])
        m["st_rwkv"] = f(inputs["state_rwkv"][sl])
        m["st_hgrn"] = f(inputs["state_hgrn"][sl])
        m["st_conv"] = f(inputs["state_conv"][sl])
        m["st_lru"] = f(inputs["state_lru"][sl])
        for k in inputs:
            if k not in INPUT_ORDER:
                m[k] = f(inputs[k])
        maps.append(m)
    return maps


def assemble(cfg, res, nb):
    R = res.results
    cat = lambda k, cores: np.stack([R[i][k] for i in cores], 0)
    P = list(range(nb))
    A = list(range(NCORES))
    y_prompt = cat("yp", P)
    y_sample = np.concatenate([R[i]["ys"].reshape(16, 4, cfg.D) for i in A], 0)
    outs = [y_prompt, y_sample]
    for k in ["o_shift_p", "o_rwkv_p", "o_hgrn_p", "o_conv_p", "o_lru_p"]:
        outs.append(cat(k, P))
    for k in ["o_shift_s", "o_rwkv_s", "o_hgrn_s", "o_conv_s", "o_lru_s"]:
        outs.append(np.concatenate([R[i][k] for i in A], 0))
    return tuple(np.ascontiguousarray(o.astype(np.float32)) for o in outs)


_CACHE = {}
STAGES = ("ffn", "odd", "even")


def run(cfg, inputs, stages=("ffn", "odd", "even")):
    key = (cfg.D, cfg.DFF, cfg.SEQ, stages)
    if key not in _CACHE:
        _CACHE[key] = K(cfg, stages).build()
    nc = _CACHE[key]
    maps = make_in_maps(cfg, inputs)
    res = run_bass_kernel_spmd(nc, maps, core_ids=list(range(NCORES)))
    return assemble(cfg, res, inputs["x_prompt"].shape[0])


def kernel(**inputs):
    return run(Cfg(), inputs, STAGES)
```

```python
import numpy as np
import concourse.bass as bass
import concourse.mybir as mybir
from concourse.bass_utils import run_bass_kernel_spmd

F32 = mybir.dt.float32
BF16 = mybir.dt.bfloat16
AF = mybir.ActivationFunctionType
ALU = mybir.AluOpType
AX = mybir.AxisListType
MAXV = 30000
NCORES = 8


class Cfg:
    def __init__(self, D=2048, DFF=5632, SEQ=2048, NSS=16, DEPTH=4):
        self.D, self.DFF, self.SEQ, self.NSS, self.DEPTH = D, DFF, SEQ, NSS, DEPTH
        self.KC = D // 128
        self.FC = DFF // 128
        self.TP = SEQ // 2
        self.RW = D // 2
        self.NHP = self.RW // 128
        self.NH = self.RW // 64
        self.NHG = self.RW // 128
        self.NLB = D // 256
        self.P_RW = 3 * self.RW + 448
        self.P_EVEN = self.P_RW + 4 * self.RW
        self.NEV = (DEPTH + 1) // 2
        self.NOD = DEPTH // 2


class Buf:
    __slots__ = ("name", "w", "r", "excl")

    def __init__(self, name, excl=False):
        self.name = name
        self.w = None
        self.r = {}
        self.excl = excl


class Prog:
    def __init__(self, nc, sems):
        self.nc = nc
        self.sems = list(sems)
        self.si = 0
        self.E = {}
        for e in ("pe", "act", "dve", "pool", "sp"):
            self.E[e] = dict(ops=[], sem=None, val=0, waited={})
        self.ring = {}
        self.nops = 0

    def new_sem(self):
        s = self.sems[self.si]
        self.si += 1
        return s

    def _deps(self, e, reads, writes):
        deps = {}

        def add(t):
            if t is None:
                return
            s, v, te = t
            if e == "pe" and te == "pe":
                return
            k = id(s)
            if k not in deps or deps[k][1] < v:
                deps[k] = (s, v)

        for b in reads:
            add(b.w)
            if b.excl:
                for t in b.r.values():
                    if t[2] != e:
                        add(t)
        for b in writes:
            add(b.w)
            for t in b.r.values():
                add(t)
        W = self.E[e]["waited"]
        out = []
        for k, (s, v) in deps.items():
            if W.get(k, 0) >= v:
                continue
            W[k] = v
            out.append((s, v))
        return out

    def _mark(self, tok, reads, writes):
        k = id(tok[0])
        for b in reads:
            o = b.r.get(k)
            if o is None or o[1] < tok[1]:
                b.r[k] = tok
        for b in writes:
            b.w = tok
            b.r = {}

    def op(self, e, fn, reads=(), writes=()):
        E = self.E[e]
        waits = self._deps(e, reads, writes)
        if E["sem"] is None or E["val"] >= MAXV:
            E["sem"] = self.new_sem()
            E["val"] = 0
        E["val"] += 1
        tok = (E["sem"], E["val"], e)
        E["ops"].append((waits, fn, E["sem"], 1))
        self._mark(tok, reads, writes)
        self.nops += 1
        return tok

    def dma(self, q, out, in_, reads=(), writes=(), **kw):
        if q not in self.ring:
            self.ring[q] = dict(slots=[[self.new_sem(), 0] for _ in range(10)], i=0)
        R = self.ring[q]
        E = self.E[q]
        sl = R["slots"][R["i"] % len(R["slots"])]
        R["i"] += 1
        waits = self._deps(q, reads, writes)
        if sl[1] > 0:
            k = id(sl[0])
            if E["waited"].get(k, 0) < sl[1]:
                E["waited"][k] = sl[1]
                waits.append((sl[0], sl[1]))
        if sl[1] + 16 > MAXV:
            sl[0] = self.new_sem()
            sl[1] = 0
        sl[1] += 16
        tok = (sl[0], sl[1], "dma")
        E["ops"].append((waits, lambda eng: eng.dma_start(out=out, in_=in_, **kw), sl[0], 16))
        self._mark(tok, reads, writes)
        return tok

    def emit(self, block):
        nc = self.nc
        P = self

        def run(eng, name):
            for waits, fn, sem, inc in P.E[name]["ops"]:
                for s, v in waits:
                    eng.wait_ge(s, v)
                fn(eng).then_inc(sem, inc)
            if name in P.ring:
                for s, v in P.ring[name]["slots"]:
                    if v > 0:
                        eng.wait_ge(s, v)

        @block.tensor
        def _(eng):
            run(eng, "pe")

        @block.scalar
        def _(eng):
            run(eng, "act")

        @block.vector
        def _(eng):
            run(eng, "dve")

        @block.gpsimd
        def _(eng):
            run(eng, "pool")

        @block.sync
        def _(eng):
            run(eng, "sp")


class K:
    def __init__(self, cfg, stages=("ffn", "odd", "even")):
        self.c = cfg
        self.stages = stages
        self.nc = bass.Bass("TRN2", target_bir_lowering=False)
        self.ctx = []

    def sb(self, name, shape, dt=F32):
        cm = self.nc.sbuf_tensor(name, list(shape), dt)
        t = cm.__enter__()
        self.ctx.append(cm)
        return t

    def dram(self, name, shape, kind, dt=F32):
        return self.nc.dram_tensor(name, list(shape), dt, kind=kind).ap()

    def build(self):
        c, nc = self.c, self.nc
        D, KC, TP = c.D, c.KC, c.TP
        NMAX = TP + 64
        self.NMAX = NMAX
        I, O = "ExternalInput", "ExternalOutput"
        d = self.d = {}
        d["xp"] = self.dram("xp", [2 * TP, D], I)
        d["xs"] = self.dram("xs", [64, D], I)
        d["st_shift"] = self.dram("st_shift", [16, c.NEV, c.P_RW], I)
        d["st_rwkv"] = self.dram("st_rwkv", [16, c.NEV, c.NH, 64, 64], I)
        d["st_hgrn"] = self.dram("st_hgrn", [16, c.NEV, c.NHG, 128, 128], I)
        d["st_conv"] = self.dram("st_conv", [16, c.NOD, 3, D], I)
        d["st_lru"] = self.dram("st_lru", [16, c.NOD, D], I)
        L = c.DEPTH
        wshapes = dict(
            ffn1_norm=[L, D], ffn1_w_gu=[L, D, 2 * c.DFF], ffn1_w_down=[L, c.DFF, D], mix_norm=[L, D],
            ffn2_norm=[L, D], ffn2_w_gu=[L, D, 2 * c.DFF], ffn2_w_down=[L, c.DFF, D],
            ev_w_in=[c.NEV, D, c.P_EVEN], rw_mu=[c.NEV, c.P_RW], rw_w0=[c.NEV, c.RW],
            rw_w2=[c.NEV, 96, c.RW], rw_a0=[c.NEV, c.RW], rw_a2=[c.NEV, 96, c.RW], rw_g2=[c.NEV, 256, c.RW],
            rw_k_k=[c.NEV, c.RW], rw_k_a=[c.NEV, c.RW], rw_r_k=[c.NEV, c.NH, 64], rw_ln_w=[c.NEV, c.RW],
            rw_ln_b=[c.NEV, c.RW], hg_lb=[c.NEV, c.RW], hg_norm=[c.NEV, c.RW], ev_w_out=[c.NEV, D, D],
            od_w_in=[c.NOD, D, 2 * D], conv_w=[c.NOD, 4, D], conv_b=[c.NOD, D],
            lru_wa=[c.NOD, c.NLB, 256, 256], lru_ba=[c.NOD, D], lru_wx=[c.NOD, c.NLB, 256, 256],
            lru_bx=[c.NOD, D], lru_lambda=[c.NOD, D], od_w_out=[c.NOD, D, D], final_norm=[D])
        self.wshapes = wshapes
        for k, s in wshapes.items():
            d[k] = self.dram(k, s, I)
        d["yp"] = self.dram("yp", [2 * TP, D], O)
        d["ys"] = self.dram("ys", [64, D], O)
        d["o_shift_p"] = self.dram("o_shift_p", [c.NEV, c.P_RW], O)
        d["o_rwkv_p"] = self.dram("o_rwkv_p", [c.NEV, c.NH, 64, 64], O)
        d["o_hgrn_p"] = self.dram("o_hgrn_p", [c.NEV, c.NHG, 128, 128], O)
        d["o_conv_p"] = self.dram("o_conv_p", [c.NOD, 3, D], O)
        d["o_lru_p"] = self.dram("o_lru_p", [c.NOD, D], O)
        d["o_shift_s"] = self.dram("o_shift_s", [16, c.NEV, c.P_RW], O)
        d["o_rwkv_s"] = self.dram("o_rwkv_s", [16, c.NEV, c.NH, 64, 64], O)
        d["o_hgrn_s"] = self.dram("o_hgrn_s", [16, c.NEV, c.NHG, 128, 128], O)
        d["o_conv_s"] = self.dram("o_conv_s", [16, c.NOD, 3, D], O)
        d["o_lru_s"] = self.dram("o_lru_s", [16, c.NOD, D], O)

        self.x = self.sb("x", [128, KC * NMAX])
        self.xn = self.sb("xn", [128, KC * NMAX], BF16)
        self.bx = [Buf(f"x{k}") for k in range(KC)]
        self.bxn = Buf("xn")
        self.WS = 16 * 128
        self.wst = [self.sb(f"wst{i}", [128, self.WS]) for i in range(2)]
        self.wbf = [self.sb(f"wbf{i}", [128, self.WS], BF16) for i in range(2)]
        self.bwst = [Buf(f"wst{i}") for i in range(2)]
        self.bwbf = [Buf(f"wbf{i}") for i in range(2)]
        self.wi = 0
        self.wj = 0
        self.AR = 14 * 1024
        self.ar = self.sb("arena", [128, self.AR])
        self.live = []
        self.ident = self.sb("ident", [128, 128])
        self.ones = self.sb("ones", [128, 128])
        self.gains = self.sb("gains", [128, (3 * L + 1) * KC])
        self.cst = self.sb("cst", [128, 800])
        self.ccol = {}
        self.cnext = 0
        self.pers = self.sb("pers", [128, c.NOD * KC * 4])
        self.bpers = Buf("pers")
        self.rstd = self.ar[:, self.AR - NMAX:self.AR]
        self.sq = [self.ar[:, self.AR - (2 + i) * NMAX:self.AR - (1 + i) * NMAX] for i in range(2)]
        self.eps = self.sb("eps", [128, 4])
        self.bconst = Buf("const")
        self.ps = []
        for i in range(8):
            cm = nc.psum_tensor(f"ps{i}", [128, 512], F32)
            self.ps.append(cm.__enter__())
            self.ctx.append(cm)
        self.bps = [Buf(f"ps{i}", excl=True) for i in range(8)]
        self.pi = 0
        sems = []
        for i in range(56):
            cm = nc.semaphore(f"s{i}")
            sems.append(cm.__enter__())
            self.ctx.append(cm)
        self.P = Prog(nc, sems)
        self.setup()
        for ti in range(2):
            self.tile(ti)
        cmb = nc.Block()
        block = cmb.__enter__()
        self.P.emit(block)
        cmb.__exit__(None, None, None)
        for cm in reversed(self.ctx):
            cm.__exit__(None, None, None)
        return nc

    def aclaim(self, name, off, size):
        assert off + size <= self.AR, (name, off, size)
        nb = Buf(name)
        keep = []
        for (o, sz, b) in self.live:
            if o < off + size and off < o + sz:
                toks = list(b.r.values()) + ([b.w] if b.w is not None else [])
                for t in toks:
                    k = id(t[0])
                    if k not in nb.r or nb.r[k][1] < t[1]:
                        nb.r[k] = t
                if not (off <= o and o + sz <= off + size):
                    keep.append((o, sz, b))
            else:
                keep.append((o, sz, b))
        keep.append((off, size, nb))
        self.live = keep
        return self.ar[:, off:off + size], nb

    def psum(self):
        i = self.pi % 8
        self.pi += 1
        return self.ps[i], self.bps[i]

    def ntiles(self, n):
        out, t = [], 0
        while t < n:
            m = min(512, n - t)
            out.append((t, m))
            t += m
        return out

    def xv(self, kc, t0=0, m=None):
        m = self.n - t0 if m is None else m
        return self.x[:, kc * self.NMAX + t0: kc * self.NMAX + t0 + m]

    def xnv(self, kc, t0=0, m=None):
        m = self.n - t0 if m is None else m
        return self.xn[:, kc * self.NMAX + t0: kc * self.NMAX + t0 + m]

    def setup(self):
        P, c, d = self.P, self.c, self.d
        L, KC = c.DEPTH, c.KC
        bc = self.bconst
        P.op("pool", lambda e: e.memset(self.ones[:], 1.0), writes=[bc])
        P.op("pool", lambda e: e.memset(self.cst[:], 0.0), writes=[bc])
        P.op("pool", lambda e: e.memset(self.eps[:, 0:1], 1e-6), writes=[bc])
        P.op("pool", lambda e: e.memset(self.eps[:, 1:2], 1.0), writes=[bc])
        P.op("pool", lambda e: e.affine_select(out=self.ident[:], in_=self.ones[:], pattern=[[1, 128]],
                                                compare_op=ALU.is_equal, fill=0.0, base=0, channel_multiplier=-1),
             reads=[bc], writes=[bc])
        names = [f"ffn1_norm{l}" for l in range(L)] + [f"mix_norm{l}" for l in range(L)] + \
                [f"ffn2_norm{l}" for l in range(L)] + ["final_norm"]
        self.gidx = {nm: i for i, nm in enumerate(names)}
        for i, nm in enumerate(names):
            if "nogain" in self.stages:
                break
            src = d["final_norm"] if nm == "final_norm" else d[nm[:-1]][int(nm[-1])]
            P.dma("sp", self.gains[:, i * KC:(i + 1) * KC], src.rearrange("(k p) -> p k", p=128),
                  writes=[bc], allow_slow_non_contiguous=True)
        if "odd" in self.stages:
            self.setup_odd()
        if "even" in self.stages:
            self.setup_even()

    def cload(self, key, src, rows, ncols):
        c0 = self.cnext
        self.cnext += ncols
        assert self.cnext <= 800
        self.ccol[key] = c0
        dst = self.cst[0:rows, c0:c0 + ncols]
        if len(src.shape) == 3:
            dst = dst.rearrange("p (k j) -> p k j", j=src.shape[2])
        self.P.dma("sp", dst, src, writes=[self.bconst], allow_slow_non_contiguous=True)
        return c0

    def cs(self, key, i=0, rows=128):
        c0 = self.ccol[key] + i
        return self.cst[0:rows, c0:c0 + 1]

    def setup_odd(self):
        P, c, d = self.P, self.c, self.d
        KC = c.KC
        bc = self.bconst
        for o in range(c.NOD):
            for j in range(4):
                c0 = self.cload(f"convw{o}_{j}", d["conv_w"][o, j].rearrange("(k p) -> p k", p=128), 128, KC)
            for nm in ("conv_b", "lru_ba", "lru_bx", "lru_lambda"):
                self.cload(f"{nm}{o}", d[nm][o].rearrange("(k p) -> p k", p=128), 128, KC)
            lam = self.cst[:, self.ccol[f"lru_lambda{o}"]: self.ccol[f"lru_lambda{o}"] + KC]
            c0 = self.cnext
            self.cnext += 6 * KC
            t = [self.cst[:, c0 + i * KC: c0 + (i + 1) * KC] for i in range(6)]
            self.ccol[f"csp{o}"] = c0 + 4 * KC
            self.ccol[f"csp2{o}"] = c0 + 5 * KC
            op = lambda eng, fn: P.op(eng, fn, reads=[bc], writes=[bc])
            op("act", lambda e, t=t, lam=lam: e.activation(out=t[0], in_=lam, func=AF.Abs))
            op("act", lambda e, t=t, lam=lam: e.activation(out=t[0], in_=t[0], func=AF.Exp, scale=-1.0))
            op("dve", lambda e, t=t, lam=lam: e.tensor_scalar(out=t[1], in0=t[0], scalar1=2.0, scalar2=None, op0=ALU.add))
            op("dve", lambda e, t=t, lam=lam: e.reciprocal(out=t[1], in_=t[1]))
            op("dve", lambda e, t=t, lam=lam: e.tensor_tensor(out=t[1], in0=t[1], in1=t[0], op=ALU.mult))
            op("dve", lambda e, t=t, lam=lam: e.tensor_tensor(out=t[2], in0=t[1], in1=t[1], op=ALU.mult))
            op("dve", lambda e, t=t, lam=lam: e.tensor_scalar(out=t[3], in0=t[2], scalar1=1.0 / 9, scalar2=1.0 / 7, op0=ALU.mult, op1=ALU.add))
            op("dve", lambda e, t=t, lam=lam: e.tensor_tensor(out=t[3], in0=t[3], in1=t[2], op=ALU.mult))
            op("dve", lambda e, t=t, lam=lam: e.tensor_scalar(out=t[3], in0=t[3], scalar1=1.0 / 5, scalar2=None, op0=ALU.add))
            op("dve", lambda e, t=t, lam=lam: e.tensor_tensor(out=t[3], in0=t[3], in1=t[2], op=ALU.mult))
            op("dve", lambda e, t=t, lam=lam: e.tensor_scalar(out=t[3], in0=t[3], scalar1=1.0 / 3, scalar2=None, op0=ALU.add))
            op("dve", lambda e, t=t, lam=lam: e.tensor_tensor(out=t[3], in0=t[3], in1=t[2], op=ALU.mult))
            op("dve", lambda e, t=t, lam=lam: e.tensor_scalar(out=t[3], in0=t[3], scalar1=1.0, scalar2=None, op0=ALU.add))
            op("dve", lambda e, t=t, lam=lam: e.tensor_tensor(out=t[3], in0=t[3], in1=t[1], op=ALU.mult))
            op("dve", lambda e, t=t, lam=lam: e.tensor_scalar(out=t[2], in0=lam, scalar1=-1.0, scalar2=0.0, op0=ALU.mult, op1=ALU.max))
            op("dve", lambda e, t=t, lam=lam: e.scalar_tensor_tensor(out=t[3], in0=t[3], scalar=2.0, in1=t[2], op0=ALU.mult, op1=ALU.add))
            op("dve", lambda e, t=t, lam=lam: e.tensor_scalar(out=t[4], in0=t[3], scalar1=-8.0, scalar2=None, op0=ALU.mult))
            op("dve", lambda e, t=t, lam=lam: e.tensor_scalar(out=t[5], in0=t[3], scalar1=-16.0, scalar2=None, op0=ALU.mult))
        P.op("pool", lambda e: e.memset(self.pers[:], 0.0), writes=[self.bpers])

    def setup_even(self):
        P, c, d = self.P, self.c, self.d
        KC, RW, NHP, NHG, NEV = c.KC, c.RW, c.NHP, c.NHG, c.NEV
        bc = self.bconst
        segs = []
        for hp in range(NHP):
            segs += [(f"r{hp}", hp * 128, 128), (f"k{hp}", RW + hp * 128, 128), (f"v{hp}", 2 * RW + hp * 128, 128)]
        segs += [("zw", 3 * RW, 96), ("za", 3 * RW + 96, 96), ("zg0", 3 * RW + 192, 128), ("zg1", 3 * RW + 320, 128)]
        self.segs = {nm: (i, c0, m) for i, (nm, c0, m) in enumerate(segs)}
        NS = len(segs)
        self.NSEG = NS
        for e in range(NEV):
            mu0 = self.cnext
            for (nm, c0, m) in segs:
                self.cload(f"mu{e}_{nm}", d["rw_mu"][e, c0:c0 + m].rearrange("(p o) -> p o", o=1), m, 1)
            om0 = self.cnext
            self.cnext += NS
            self.ccol[f"mu{e}"] = mu0
            self.ccol[f"om{e}"] = om0
            P.op("dve", lambda en, mu0=mu0, om0=om0: en.tensor_scalar(
                out=self.cst[:, om0:om0 + NS], in0=self.cst[:, mu0:mu0 + NS], scalar1=-1.0, scalar2=1.0,
                op0=ALU.mult, op1=ALU.add), reads=[bc], writes=[bc])
            for nm in ("rw_w0", "rw_a0", "rw_k_k", "rw_k_a", "rw_ln_w", "rw_ln_b"):
                self.cload(f"{nm}{e}", d[nm][e].rearrange("(k p) -> p k", p=128), 128, NHP)
            self.cload(f"rw_r_k{e}", d["rw_r_k"][e].rearrange("(a h) k -> (h k) a", h=2), 128, NHP)
            self.cload(f"hg_norm{e}", d["hg_norm"][e].rearrange("(k p) -> p k", p=128), 128, NHG)
            self.cload(f"hg_lbraw{e}", d["hg_lb"][e].rearrange("(k p) -> p k", p=128), 128, NHG)
        col = lambda key: self.cst[:, self.ccol[key]:self.ccol[key] + NHG]
        t0 = self.cnext
        self.cnext += (3 * NEV + 1) * NHG
        ex = [self.cst[:, t0 + i * NHG:t0 + (i + 1) * NHG] for i in range(NEV)]
        tot = self.cst[:, t0 + NEV * NHG:t0 + (NEV + 1) * NHG]
        for e in range(NEV):
            self.ccol[f"lb{e}"] = t0 + (NEV + 1 + e) * NHG
            self.ccol[f"oml{e}"] = t0 + (2 * NEV + 1 + e) * NHG
            P.op("act", lambda en, e=e: en.activation(out=ex[e], in_=col(f"hg_lbraw{e}"), func=AF.Exp), reads=[bc], writes=[bc])
        P.op("dve", lambda en: en.tensor_copy(out=tot, in_=ex[0]), reads=[bc], writes=[bc])
        for e in range(1, NEV):
            P.op("dve", lambda en, e=e: en.tensor_tensor(out=tot, in0=tot, in1=ex[e], op=ALU.add), reads=[bc], writes=[bc])
        P.op("dve", lambda en: en.reciprocal(out=tot, in_=tot), reads=[bc], writes=[bc])
        P.op("pool", lambda en: en.memset(col("lb0"), 0.0), reads=[bc], writes=[bc])
        for e in range(1, NEV):
            P.op("dve", lambda en, e=e: en.tensor_tensor(out=ex[e], in0=ex[e], in1=tot, op=ALU.mult), reads=[bc], writes=[bc])
            P.op("dve", lambda en, e=e: en.tensor_tensor(out=col(f"lb{e}"), in0=col(f"lb{e - 1}"), in1=ex[e], op=ALU.add),
                 reads=[bc], writes=[bc])
        for e in range(NEV):
            P.op("dve", lambda en, e=e: en.tensor_scalar(out=col(f"oml{e}"), in0=col(f"lb{e}"), scalar1=-1.0, scalar2=1.0,
                                                         op0=ALU.mult, op1=ALU.add), reads=[bc], writes=[bc])
        self.bones = self.sb("bones", [128, 128])
        self.msu = self.sb("msu", [64, 256])
        self.mui = self.sb("mui", [64, 256])
        self.msl = self.sb("msl", [64, 256])
        self.msu_s = self.sb("msu_s", [64, 128])
        self.mui_s = self.sb("mui_s", [64, 128])
        self.msl_s = self.sb("msl_s", [64, 128])
        self.id4 = self.sb("id4", [64, 256])
        self.rowm = self.sb("rowm", [64, 16])
        self.mrst = self.sb("mrst", [128, 128])
        self.mrst_s = self.sb("mrst_s", [128, 64])
        self.eps2 = self.sb("eps2", [128, 2])
        self.prevp = self.sb("prevp", [128, NEV * NS])
        self.bprev = Buf("prevp")
        self.ST = self.sb("ST", [128, NEV * NHP * 64])
        self.bST = Buf("ST")
        self.SH = self.sb("SHg", [128, NEV * NHG * 128])
        self.bSH = Buf("SH")
        op = lambda eng, fn: P.op(eng, fn, reads=[bc], writes=[bc])
        op("pool", lambda en: en.memset(self.bones[:], 0.0))
        op("pool", lambda en: en.memset(self.bones[0:64, 0:64], 1.0))
        op("pool", lambda en: en.memset(self.bones[64:128, 64:128], 1.0))
        op("pool", lambda en: en.memset(self.eps2[:, 0:1], 64e-5))
        op("pool", lambda en: en.memset(self.eps2[:, 1:2], 1e-5))
        op("pool", lambda en: en.memset(self.prevp[:], 0.0))
        P.op("pool", lambda en: en.memset(self.ST[:], 0.0), writes=[self.bST])
        P.op("pool", lambda en: en.memset(self.SH[:], 0.0), writes=[self.bSH])
        on64 = self.ones[0:64, 0:64]

        def sel(out, in_, pattern, cmp, base, cm):
            op("pool", lambda en: en.affine_select(out=out, in_=in_, pattern=pattern, compare_op=cmp, fill=0.0,
                                                   base=base, channel_multiplier=cm))
        for (t, nsl) in ((self.msu, 4), (self.mui, 4), (self.msl, 4), (self.id4, 4), (self.msu_s, 2), (self.mui_s, 2), (self.msl_s, 2)):
            op("pool", lambda en, t=t: en.memset(t[:], 1.0))
        for (t, nsl) in ((self.msu, 4), (self.msu_s, 2)):
            sel(t[:], t[:], [[0, nsl], [1, 64]], ALU.is_gt, 0, -1)
        for (t, nsl) in ((self.mui, 4), (self.mui_s, 2)):
            sel(t[:], t[:], [[0, nsl], [1, 64]], ALU.is_ge, 0, -1)
        for (t, nsl) in ((self.msl, 4), (self.msl_s, 2)):
            sel(t[:], t[:], [[0, nsl], [-1, 64]], ALU.is_gt, 0, 1)
        sel(self.id4[:], self.id4[:], [[0, 4], [1, 64]], ALU.is_equal, 0, -1)
        for t in (self.msu_s, self.mui_s, self.msl_s):
            sel(t[:], t[:], [[0, 2], [-4, 16], [0, 4]], ALU.is_ge, 0, 1)
            sel(t[:], t[:], [[0, 2], [4, 16], [0, 4]], ALU.is_ge, 3, -1)
        op("pool", lambda en: en.memset(self.rowm[:], 1.0))
        sel(self.rowm[:], self.rowm[:], [[-4, 16]], ALU.is_ge, 0, 1)
        sel(self.rowm[:], self.rowm[:], [[4, 16]], ALU.is_ge, 3, -1)
        op("pool", lambda en: en.memset(self.mrst[:], 1.0))
        op("pool", lambda en: en.memset(self.mrst[:].rearrange("p (c t) -> p c t", t=64)[:, :, 0:1], 0.0))
        op("pool", lambda en: en.memset(self.mrst_s[:], 1.0))
        op("pool", lambda en: en.memset(self.mrst_s[:].rearrange("p (c t) -> p c t", t=4)[:, :, 0:1], 0.0))

    def gain(self, nm, kc):
        i = self.gidx[nm]
        return self.gains[:, i * self.c.KC + kc: i * self.c.KC + kc + 1]

    def wload(self, src, rows, kcn, cols, cast=True, pool_cast=False):
        P = self.P
        i = self.wi % 2
        self.wi += 1
        if not cast:
            st, bst = self.wst[i], self.bwst[i]
            dst = st[0:rows, 0:kcn * cols].rearrange("p (k c) -> p k c", c=cols)
            P.dma("sp", dst, src, writes=[bst])
            return st, bst
        j = self.wj % 2
        self.wj += 1
        st, bf, bst, bbf = self.wst[i], self.wbf[j], self.bwst[i], self.bwbf[j]
        n = kcn * cols
        assert n <= self.WS
        dst = st[0:rows, 0:n].rearrange("p (k c) -> p k c", c=cols)
        P.dma("sp", dst, src, writes=[bst])
        if pool_cast:
            P.op("pool", lambda e: e.tensor_copy(out=bf[0:rows, 0:n], in_=st[0:rows, 0:n]), reads=[bst], writes=[bbf])
            return bf, bbf
        a = (n // 2) // 2 * 2
        b = a + ((n - a) // 2) // 2 * 2
        P.op("pool", lambda e: e.tensor_copy(out=bf[0:rows, 0:a], in_=st[0:rows, 0:a]), reads=[bst], writes=[bbf])
        P.op("act", lambda e: e.copy(out=bf[0:rows, a:b], in_=st[0:rows, a:b]), reads=[bst], writes=[bbf])
        P.op("dve", lambda e: e.tensor_copy(out=bf[0:rows, b:n], in_=st[0:rows, b:n]), reads=[bst], writes=[bbf])
        return bf, bbf

    def rms_stats(self):
        P, c = self.P, self.c
        n, KC = self.n, c.KC
        NM = self.NMAX
        self.brstd = self.aclaim("rstd", self.AR - NM, NM)[1]
        self.bsq = [self.aclaim(f"sq{i}", self.AR - (2 + i) * NM, NM)[1] for i in range(2)]
        nts = self.ntiles(n)
        pss = [self.psum() for _ in nts]
        for kc in range(KC):
            sq, bsq = self.sq[kc % 2], self.bsq[kc % 2]
            P.op("act", lambda e, sq=sq, xin=self.xv(kc), n=n: e.activation(out=sq[:, 0:n], in_=xin, func=AF.Square),
                 reads=[self.bx[kc]], writes=[bsq])
            for (t0, m), (ps, bps) in zip(nts, pss):
                P.op("pe", lambda e, sq=sq, ps=ps, t0=t0, m=m, kc=kc: e.matmul(
                    ps[:, 0:m], lhsT=self.ones[:], rhs=sq[:, t0:t0 + m], start=(kc == 0), stop=(kc == KC - 1)),
                    reads=[bsq, self.bconst], writes=[bps])
        for (t0, m), (ps, bps) in zip(nts, pss):
            P.op("act", lambda e, ps=ps, t0=t0, m=m: e.activation(
                out=self.rstd[:, t0:t0 + m], in_=ps[:, 0:m], func=AF.Ln, scale=1.0 / c.D, bias=self.eps[:, 0:1]),
                reads=[bps, self.bconst], writes=[self.brstd])
        P.op("act", lambda e, n=n: e.activation(out=self.rstd[:, 0:n], in_=self.rstd[:, 0:n], func=AF.Exp, scale=-0.5),
             reads=[self.brstd], writes=[self.brstd])

    def rmsnorm(self, gname):
        P, c = self.P, self.c
        self.rms_stats()
        for kc in range(c.KC):
            P.op("dve", lambda e, o=self.xnv(kc), i0=self.xv(kc), g=self.gain(gname, kc), r=self.rstd[:, 0:self.n]:
                 e.scalar_tensor_tensor(out=o, in0=i0, scalar=g, in1=r, op0=ALU.mult, op1=ALU.mult),
                 reads=[self.bx[kc], self.brstd, self.bconst], writes=[self.bxn])

    def load_x(self, ti):
        P, c, d = self.P, self.c, self.d
        KC, TP, D = c.KC, c.TP, c.D
        blocks = [(d["xp"], ti * TP + t, t, 128) for t in range(0, TP, 128)]
        if ti == 0:
            blocks.append((d["xs"], 0, TP, 64))
        for src, r0, t0, m in blocks:
            for h0 in range(0, D, self.WS):
                i = self.wi % 2
                self.wi += 1
                st, bst = self.wst[i], self.bwst[i]
                w = min(self.WS, D - h0)
                P.dma("sp", st[0:m, 0:w], src[r0:r0 + m, h0:h0 + w], writes=[bst])
                for k0 in range(0, w // 128, 4):
                    ps, bps = self.psum()
                    g = min(4, w // 128 - k0)
                    for j in range(g):
                        P.op("pe", lambda e, ps=ps, st=st, m=m, j=j, k0=k0: e.transpose(
                            out=ps[:, j * 128:j * 128 + m], in_=st[0:m, (k0 + j) * 128:(k0 + j + 1) * 128],
                            identity=self.ident[0:m, 0:m]), reads=[bst, self.bconst], writes=[bps])
                    for j in range(g):
                        kc = h0 // 128 + k0 + j
                        eng = "act" if (self.pi % 2 == 0) else "dve"
                        if eng == "act":
                            P.op("act", lambda e, ps=ps, o=self.xv(kc, t0, m), j=j, m=m: e.copy(
                                out=o, in_=ps[:, j * 128:j * 128 + m]), reads=[bps], writes=[self.bx[kc]])
                        else:
                            P.op("dve", lambda e, ps=ps, o=self.xv(kc, t0, m), j=j, m=m: e.tensor_copy(
                                out=o, in_=ps[:, j * 128:j * 128 + m]), reads=[bps], writes=[self.bx[kc]])

    def store_y(self, ti):
        P, c, d = self.P, self.c, self.d
        KC, TP, D = c.KC, c.TP, c.D
        dbg = "dbg1" in self.stages
        if not dbg:
            self.rms_stats()
        blocks = [(d["yp"], ti * TP + t, t, 128) for t in range(0, TP, 128)]
        if ti == 0:
            blocks.append((d["ys"], 0, TP, 64))
        tmp, btmp = self.aclaim("ytmp", 0, 512)
        for src, r0, t0, m in blocks:
            for h0 in range(0, D, self.WS):
                i = self.wi % 2
                self.wi += 1
                st, bst = self.wst[i], self.bwst[i]
                w = min(self.WS, D - h0)
                for k0 in range(0, w // 128, 4):
                    g = min(4, w // 128 - k0)
                    ps, bps = self.psum()
                    for j in range(g):
                        kc = h0 // 128 + k0 + j
                        if dbg:
                            P.op("dve", lambda e, i0=self.xv(kc, t0, m), j=j, m=m: e.tensor_copy(
                                out=tmp[:, j * 128:j * 128 + m], in_=i0), reads=[self.bx[kc]], writes=[btmp])
                            continue
                        P.op("dve", lambda e, i0=self.xv(kc, t0, m), g=self.gain("final_norm", kc), j=j, t0=t0, m=m: e.scalar_tensor_tensor(
                            out=tmp[:, j * 128:j * 128 + m], in0=i0, scalar=g,
                            in1=self.rstd[:, t0:t0 + m], op0=ALU.mult, op1=ALU.mult),
                            reads=[self.bx[kc], self.brstd, self.bconst], writes=[btmp])
                    for j in range(g):
                        P.op("pe", lambda e, ps=ps, j=j, m=m: e.transpose(
                            out=ps[0:m, j * 128:(j + 1) * 128], in_=tmp[:, j * 128:j * 128 + m], identity=self.ident[:]),
                            reads=[btmp, self.bconst], writes=[bps])
                    P.op("act", lambda e, ps=ps, st=st, k0=k0, g=g, m=m: e.copy(
                        out=st[0:m, k0 * 128:(k0 + g) * 128], in_=ps[0:m, 0:g * 128]), reads=[bps], writes=[bst])
                P.dma("pool", src[r0:r0 + m, h0:h0 + w], st[0:m, 0:w], reads=[bst])

    def ffn(self, l, which):
        P, c, d = self.P, self.c, self.d
        KC, FC, D, DFF = c.KC, c.FC, c.D, c.DFF
        n = self.n
        nts = self.ntiles(n)
        wgu = d[f"ffn{which}_w_gu"][l]
        wdn = d[f"ffn{which}_w_down"][l]
        self.rmsnorm(f"ffn{which}_norm{l}")
        GS = 8
        hN = self.NMAX
        self.live = [(o, z, b) for (o, z, b) in self.live]
        h = self.ar.bitcast(BF16)
        bh = [self.aclaim(f"h{f}", f * hN // 2, hN // 2)[1] for f in range(GS)]
        sg, bsg = self.aclaim("sg", GS * hN // 2, 512)
        jobs = []
        for f0 in range(0, FC, GS):
            gs = min(GS, FC - f0)
            for f in range(gs):
                jobs += [("g", f0, gs, f), ("u", f0, gs, f)]
            DW = max(128, (self.WS // gs) // 128 * 128)
            for d0 in range(0, D, DW):
                jobs.append(("d", f0, gs, (d0, min(DW, D - d0))))
        hnd = {}

        def fetch(i):
            kind, f0, gs, x = jobs[i]
            if kind == "d":
                d0, dw = x
                hnd[i] = self.wload(wdn[f0 * 128:(f0 + gs) * 128, d0:d0 + dw].rearrange("(k p) c -> p k c", p=128), 128, gs, dw)
            else:
                c0 = (f0 + x) * 128 + (DFF if kind == "u" else 0)
                hnd[i] = self.wload(wgu[:, c0:c0 + 128].rearrange("(k p) c -> p k c", p=128), 128, KC, 128)
        fetch(0)
        gp = None
        for i, (kind, f0, gs, x) in enumerate(jobs):
            if i + 1 < len(jobs):
                fetch(i + 1)
            w, bw = hnd.pop(i)
            if kind in ("g", "u"):
                f = x
                pss = []
                for (t0, m) in nts:
                    ps, bps = self.psum()
                    for kc in range(KC):
                        P.op("pe", lambda e, ps=ps, w=w, kc=kc, r=self.xnv(kc, t0, m), m=m: e.matmul(
                            ps[:, 0:m], lhsT=w[:, kc * 128:(kc + 1) * 128], rhs=r, start=(kc == 0), stop=(kc == KC - 1)),
                            reads=[bw, self.bxn], writes=[bps])
                    pss.append((ps, bps))
                    if kind == "u":
                        psg, bpg = gp[len(pss) - 1]
                        P.op("act", lambda e, psg=psg, m=m: e.activation(out=sg[:, 0:m], in_=psg[:, 0:m], func=AF.Silu),
                             reads=[bpg], writes=[bsg])
                        P.op("dve", lambda e, ps=ps, f=f, t0=t0, m=m: e.tensor_tensor(
                            out=h[:, f * hN + t0: f * hN + t0 + m], in0=ps[:, 0:m], in1=sg[:, 0:m], op=ALU.mult),
                            reads=[bps, bsg], writes=[bh[f]])
                if kind == "g":
                    gp = pss
            else:
                d0, dw = x
                wd, bwd = w, bw
                for dj in range(dw // 128):
                    kc = d0 // 128 + dj
                    for (t0, m) in nts:
                        ps, bps = self.psum()
                        for f in range(gs):
                            P.op("pe", lambda e, ps=ps, wd=wd, f=f, dj=dj, dw=dw, t0=t0, m=m: e.matmul(
                                ps[:, 0:m], lhsT=wd[:, f * dw + dj * 128: f * dw + (dj + 1) * 128],
                                rhs=h[:, f * hN + t0: f * hN + t0 + m], start=(f == 0), stop=(f == gs - 1)),
                                reads=[bwd, bh[f]], writes=[bps])
                        P.op("dve", lambda e, ps=ps, xo=self.xv(kc, t0, m), m=m: e.scalar_tensor_tensor(
                            out=xo, in0=ps[:, 0:m], scalar=0.5, in1=xo,
                            op0=ALU.mult, op1=ALU.add), reads=[bps, self.bx[kc]], writes=[self.bx[kc]])

    def outproj(self, wout, ch, ob, bob, c0=0, ncols=None):
        self.outproj_multi(wout, [(ch, ob, bob)], c0, ncols)

    def outproj_multi(self, wout, items, c0=0, ncols=None):
        P, c = self.P, self.c
        D, KC = c.D, c.KC
        assert len(items) <= 2
        ncols = self.n if ncols is None else ncols
        nts = self.ntiles(ncols)
        for d0 in range(0, D, self.WS):
            dw = min(self.WS, D - d0)
            ws = [self.wload(wout[ch * 128:(ch + 1) * 128, d0:d0 + dw].rearrange("(k p) c -> p k c", p=128), 128, 1, dw,
                             pool_cast=True) for (ch, _, _) in items]
            for dj in range(dw // 128):
                kc = d0 // 128 + dj
                for (t0, mm) in nts:
                    ps, bps = self.psum()
                    for i, ((w, bw), (ch, ob, bob)) in enumerate(zip(ws, items)):
                        P.op("pe", lambda e, ps=ps, w=w, ob=ob, dj=dj, t0=t0, mm=mm, i=i: e.matmul(
                            ps[:, 0:mm], lhsT=w[:, dj * 128:(dj + 1) * 128], rhs=ob[:, t0:t0 + mm], start=(i == 0),
                            stop=(i == len(items) - 1)), reads=[bw, bob], writes=[bps])
                    P.op("dve", lambda e, ps=ps, xo=self.xv(kc, c0 + t0, mm), mm=mm: e.tensor_tensor(
                        out=xo, in0=ps[:, 0:mm], in1=xo, op=ALU.add), reads=[bps, self.bx[kc]], writes=[self.bx[kc]])

    def proj(self, wsrc, m, evac):
        P, c = self.P, self.c
        KC = c.KC
        w, bw = self.wload(wsrc.rearrange("(k p) c -> p k c", p=128), 128, KC, m, pool_cast=True)
        for (t0, mm) in self.ntiles(self.n):
            ps, bps = self.psum()
            for kc in range(KC):
                P.op("pe", lambda e, ps=ps, w=w, kc=kc, r=self.xnv(kc, t0, mm), mm=mm: e.matmul(
                    ps[0:m, 0:mm], lhsT=w[:, kc * m:(kc + 1) * m], rhs=r, start=(kc == 0), stop=(kc == KC - 1)),
                    reads=[bw, self.bxn], writes=[bps])
            evac(ps, bps, t0, mm)

    def odd(self, l):
        P, c, d = self.P, self.c, self.d
        o = l // 2
        KC, TP, D = c.KC, c.TP, c.D
        n, ti, NM = self.n, self.ti, self.NMAX
        S = (ti == 0)
        nts = self.ntiles(n)
        win, wout = d["od_w_in"][o], d["od_w_out"][o]
        self.rmsnorm(f"mix_norm{l}")
        off = [0]

        def A(name, size):
            ap, b = self.aclaim(name, off[0], size)
            off[0] += size
            return ap, b
        XE = [A(f"XE{i}", TP + 4) for i in range(2)]
        XS = [A(f"XS{i}", 112) for i in range(2)]
        XC = [A(f"XC{i}", NM) for i in range(2)]
        Aa, bA = A("A", NM)
        T1, bT = A("T1", NM)
        H, bH = A("H", NM)
        GL, bG = A("GL", NM)
        obs = []
        for i in range(2):
            obf_, bob_ = A(f"ob{i}", NM // 2)
            obs.append((obf_.bitcast(BF16), bob_))
        H0, bH0 = A("H0", 16)
        SO, bSO = A("SO", 256)
        pcol = lambda ch, j: self.pers[:, (o * KC + ch) * 4 + j:(o * KC + ch) * 4 + j + 1]
        ptail = lambda ch: self.pers[:, (o * KC + ch) * 4:(o * KC + ch) * 4 + 3]
        if S:
            si = self.wi % 2
            self.wi += 1
            sst, bsst = self.wst[si], self.bwst[si]
            for j in range(3):
                P.dma("sp", sst[j * 16:(j + 1) * 16, 0:D], d["st_conv"][:, o, j, :], writes=[bsst])
            P.dma("sp", sst[48:64, 0:D], d["st_lru"][:, o, :], writes=[bsst])
            SST, bSST = A("SST", KC * 64)
            for k0 in range(0, KC, 8):
                g = min(8, KC - k0)
                ps, bps = self.psum()
                for j in range(g):
                    P.op("pe", lambda e, ps=ps, j=j, k0=k0: e.transpose(
                        out=ps[:, j * 64:(j + 1) * 64], in_=sst[0:64, (k0 + j) * 128:(k0 + j + 1) * 128],
                        identity=self.ident[0:64, 0:64]), reads=[bsst, self.bconst], writes=[bps])
                P.op("act", lambda e, ps=ps, k0=k0, g=g: e.copy(out=SST[:, k0 * 64:(k0 + g) * 64], in_=ps[:, 0:g * 64]),
                     reads=[bps], writes=[bSST])
        for b in range(c.NLB):
            for cc in range(2):
                ch = 2 * b + cc
                xe, bxe = XE[cc]
                xs, bxs = XS[cc]
                xc, bxc = XC[cc]
                if S:
                    P.op("act", lambda e, xs=xs, ch=ch: e.copy(out=xs[:, 0:48], in_=SST[:, ch * 64:ch * 64 + 48]),
                         reads=[bSST], writes=[bxs])
                    P.op("pool", lambda e, xe=xe: e.memset(xe[:, 0:3], 0.0), writes=[bxe])
                else:
                    P.op("pool", lambda e, xe=xe, ch=ch: e.tensor_copy(out=xe[:, 0:3], in_=ptail(ch)), reads=[self.bpers], writes=[bxe])

                def evac(ps, bps, t0, mm, xe=xe, bxe=bxe, xs=xs, bxs=bxs):
                    pe = min(t0 + mm, TP)
                    if pe > t0:
                        P.op("act", lambda e: e.copy(out=xe[:, 3 + t0:3 + pe], in_=ps[:, 0:pe - t0]), reads=[bps], writes=[bxe])
                    if t0 + mm > TP:
                        a0 = max(t0, TP) - t0
                        P.op("act", lambda e: e.copy(
                            out=xs[:, 48:112].rearrange("p (t s) -> p s t", s=16),
                            in_=ps[:, a0:a0 + 64].rearrange("p (s t) -> p s t", t=4)), reads=[bps], writes=[bxs])
                self.proj(win[:, D + ch * 128:D + (ch + 1) * 128], 128, evac)
                cw = lambda j, ch=ch: self.cs(f"convw{o}_{j}", ch)
                cb = self.cs(f"conv_b{o}", ch)
                P.op("dve", lambda e, xe=xe, xc=xc, ch=ch, cb=cb: e.tensor_scalar(
                    out=xc[:, 0:TP], in0=xe[:, 3:3 + TP], scalar1=cw(3, ch), scalar2=cb, op0=ALU.mult, op1=ALU.add),
                    reads=[bxe, self.bconst], writes=[bxc])
                for j in range(3):
                    P.op("dve", lambda e, xe=xe, xc=xc, ch=ch, j=j: e.scalar_tensor_tensor(
                        out=xc[:, 0:TP], in0=xe[:, j:j + TP], scalar=cw(j, ch), in1=xc[:, 0:TP], op0=ALU.mult, op1=ALU.add),
                        reads=[bxe, bxc, self.bconst], writes=[bxc])
                if S:
                    P.op("dve", lambda e, xs=xs, xc=xc, ch=ch, cb=cb: e.tensor_scalar(
                        out=xc[:, TP:TP + 64], in0=xs[:, 48:112], scalar1=cw(3, ch), scalar2=cb, op0=ALU.mult, op1=ALU.add),
                        reads=[bxs, self.bconst], writes=[bxc])
                    for j in range(3):
                        P.op("dve", lambda e, xs=xs, xc=xc, ch=ch, j=j: e.scalar_tensor_tensor(
                            out=xc[:, TP:TP + 64], in0=xs[:, j * 16:j * 16 + 64], scalar=cw(j, ch), in1=xc[:, TP:TP + 64],
                            op0=ALU.mult, op1=ALU.add), reads=[bxs, bxc, self.bconst], writes=[bxc])
                if S:
                    P.op("pool", lambda e, xe=xe, ch=ch: e.tensor_copy(out=ptail(ch), in_=xe[:, TP:TP + 3]),
                         reads=[bxe], writes=[self.bpers])
                    ps, bps = self.psum()
                    P.op("pe", lambda e, ps=ps, xs=xs: e.transpose(out=ps[0:112, 0:128], in_=xs[:, 0:112], identity=self.ident[:]),
                         reads=[bxs, self.bconst], writes=[bps])
                    so, bso = self.aclaim(f"cso{cc}", off[0] + cc * 128, 128)
                    P.op("act", lambda e, ps=ps, so=so: e.copy(out=so[0:112, :], in_=ps[0:112, 0:128]), reads=[bps], writes=[bso])
                    for j in range(3):
                        P.dma("pool", d["o_conv_s"][:, o, j, ch * 128:(ch + 1) * 128], so[64 + 16 * j:80 + 16 * j, :], reads=[bso])
                else:
                    P.dma("pool", d["o_conv_p"][o, :, ch * 128:(ch + 1) * 128].rearrange("j p -> p j"), xe[:, TP:TP + 3],
                          reads=[bxe], allow_slow_non_contiguous=True)
            for cc in range(2):
                ch = 2 * b + cc
                xc, bxc = XC[cc]
                wa, bwa = self.wload(d["lru_wa"][o, b].rearrange("(j p) k -> p j k", p=128), 128, 2, 256, cast=False)
                wx, bwx = self.wload(d["lru_wx"][o, b].rearrange("(j p) k -> p j k", p=128), 128, 2, 256, cast=False)
                for (t0, mm) in nts:
                    for (wm, bwm, dst, bd, bias) in ((wa, bwa, Aa, bA, self.cs(f"lru_ba{o}", ch)), (wx, bwx, T1, bT, self.cs(f"lru_bx{o}", ch))):
                        ps, bps = self.psum()
                        for j in range(2):
                            P.op("pe", lambda e, ps=ps, wm=wm, j=j, cc=cc, t0=t0, mm=mm: e.matmul(
                                ps[:, 0:mm], lhsT=wm[:, j * 256 + cc * 128:j * 256 + (cc + 1) * 128],
                                rhs=XC[j][0][:, t0:t0 + mm], start=(j == 0), stop=(j == 1)),
                                reads=[bwm, XC[j][1]], writes=[bps])
                        P.op("act", lambda e, ps=ps, dst=dst, bias=bias, t0=t0, mm=mm: e.activation(
                            out=dst[:, t0:t0 + mm], in_=ps[:, 0:mm], func=AF.Sigmoid, bias=bias),
                            reads=[bps, self.bconst], writes=[bd])
                csp, csp2 = self.cs(f"csp{o}", ch), self.cs(f"csp2{o}", ch)
                P.op("act", lambda e, csp2=csp2: e.activation(out=H[:, 0:n], in_=Aa[:, 0:n], func=AF.Exp, scale=csp2),
                     reads=[bA, self.bconst], writes=[bH])
                P.op("act", lambda e: e.activation(out=H[:, 0:n], in_=H[:, 0:n], func=AF.Sqrt, scale=-1.0, bias=self.eps[:, 1:2]),
                     reads=[bH, self.bconst], writes=[bH])
                P.op("act", lambda e, csp=csp: e.activation(out=Aa[:, 0:n], in_=Aa[:, 0:n], func=AF.Exp, scale=csp),
                     reads=[bA, self.bconst], writes=[bA])
                P.op("dve", lambda e: e.tensor_tensor(out=T1[:, 0:n], in0=T1[:, 0:n], in1=H[:, 0:n], op=ALU.mult),
                     reads=[bT, bH], writes=[bT])
                P.op("dve", lambda e, xc=xc: e.tensor_tensor(out=T1[:, 0:n], in0=T1[:, 0:n], in1=xc[:, 0:n], op=ALU.mult),
                     reads=[bT, bxc], writes=[bT])
                init = 0.0 if S else pcol(ch, 3)
                P.op("dve", lambda e, init=init: e.tensor_tensor_scan(
                    out=H[:, 0:TP], data0=Aa[:, 0:TP], data1=T1[:, 0:TP], initial=init, op0=ALU.mult, op1=ALU.add),
                    reads=[bA, bT, self.bpers], writes=[bH])
                if S:
                    h0 = SST[:, ch * 64 + 48:ch * 64 + 64]
                    bh0 = bSST
                    for t in range(4):
                        cs_ = slice(TP + 16 * t, TP + 16 * t + 16)
                        prev = h0 if t == 0 else H[:, TP + 16 * (t - 1):TP + 16 * t]
                        P.op("dve", lambda e, cs_=cs_, prev=prev: e.tensor_tensor(out=H[:, cs_], in0=Aa[:, cs_], in1=prev, op=ALU.mult),
                             reads=[bA, bH, bh0], writes=[bH])
                        P.op("dve", lambda e, cs_=cs_: e.tensor_tensor(out=H[:, cs_], in0=H[:, cs_], in1=T1[:, cs_], op=ALU.add),
                             reads=[bH, bT], writes=[bH])
                    P.op("pool", lambda e, ch=ch: e.tensor_copy(out=pcol(ch, 3), in_=H[:, TP - 1:TP]), reads=[bH], writes=[self.bpers])
                    ps, bps = self.psum()
                    P.op("pe", lambda e, ps=ps: e.transpose(out=ps[0:16, 0:128], in_=H[:, TP + 48:TP + 64], identity=self.ident[:]),
                         reads=[bH, self.bconst], writes=[bps])
                    lo, blo = self.aclaim(f"lso{cc}", off[0] + 256 + cc * 128, 128)
                    P.op("act", lambda e, ps=ps, lo=lo: e.copy(out=lo[0:16, :], in_=ps[0:16, 0:128]), reads=[bps], writes=[blo])
                    P.dma("pool", d["o_lru_s"][:, o, ch * 128:(ch + 1) * 128], lo[0:16, :], reads=[blo])
                else:
                    P.dma("pool", d["o_lru_p"][o, ch * 128:(ch + 1) * 128].rearrange("(p o) -> p o", o=1), H[:, TP - 1:TP],
                          reads=[bH], allow_slow_non_contiguous=True)
                def evg(ps, bps, t0, mm):
                    P.op("act", lambda e: e.copy(out=GL[:, t0:t0 + mm], in_=ps[:, 0:mm]), reads=[bps], writes=[bG])
                self.proj(win[:, ch * 128:(ch + 1) * 128], 128, evg)
                P.op("dve", lambda e: e.tensor_tensor(out=T1[:, 0:n], in0=GL[:, 0:n], in1=GL[:, 0:n], op=ALU.mult), reads=[bG], writes=[bT])
                P.op("dve", lambda e: e.tensor_scalar(out=T1[:, 0:n], in0=T1[:, 0:n], scalar1=0.044715, scalar2=1.0, op0=ALU.mult, op1=ALU.add),
                     reads=[bT], writes=[bT])
                P.op("dve", lambda e: e.tensor_tensor(out=T1[:, 0:n], in0=T1[:, 0:n], in1=GL[:, 0:n], op=ALU.mult), reads=[bT, bG], writes=[bT])
                P.op("act", lambda e: e.activation(out=T1[:, 0:n], in_=T1[:, 0:n], func=AF.Sigmoid, scale=1.5957691216057308),
                     reads=[bT], writes=[bT])
                P.op("dve", lambda e: e.tensor_tensor(out=GL[:, 0:n], in0=GL[:, 0:n], in1=T1[:, 0:n], op=ALU.mult), reads=[bT, bG], writes=[bG])
                ob, bob = obs[cc]
                P.op("dve", lambda e, ob=ob: e.tensor_tensor(out=ob[:, 0:TP], in0=H[:, 0:TP], in1=GL[:, 0:TP], op=ALU.mult),
                     reads=[bH, bG], writes=[bob])
                if S:
                    P.op("dve", lambda e, ob=ob: e.tensor_tensor(
                        out=ob[:, TP:TP + 64].rearrange("p (s t) -> p s t", t=4),
                        in0=H[:, TP:TP + 64].rearrange("p (t s) -> p s t", s=16),
                        in1=GL[:, TP:TP + 64].rearrange("p (s t) -> p s t", t=4), op=ALU.mult), reads=[bH, bG], writes=[bob])
            self.outproj_multi(wout, [(2 * b + cc, obs[cc][0], obs[cc][1]) for cc in range(2)])

    def even(self, l):
        P, c, d = self.P, self.c, self.d
        e = l // 2
        KC, TP, D, RW, NHP, NHG = c.KC, c.TP, c.D, c.RW, c.NHP, c.NHG
        n, ti, NM, NS = self.n, self.ti, self.NMAX, self.NSEG
        S = (ti == 0)
        bc = self.bconst
        win, wout = d["ev_w_in"][e], d["ev_w_out"][e]
        self.rmsnorm(f"mix_norm{l}")
        off = [0]

        def A(name, size):
            ap, b = self.aclaim(name, off[0], size)
            off[0] += size
            return ap, b
        nts = self.ntiles(n)
        chunks = [(c0, "p") for c0 in range(0, TP, 128)] + ([(TP, "s")] if S else [])
        lor = {}
        for nm in ("zw", "za", "zg0", "zg1"):
            ap, b = A("L" + nm, NM // 2)
            lor[nm] = (ap.bitcast(BF16), b)
        Pb, bPb = A("Pb", 512)
        shs, bshs = A("shs", 16)
        ob_f, bob = A("ob", NM // 2)
        ob = ob_f.bitcast(BF16)
        base_off = off[0]
        R, bR = A("R", NM)
        Kk, bK = A("Kk", NM)
        V, bV = A("V", NM)
        Zt, bZt = R, bR

        def seginfo(nm):
            i, c0, m = self.segs[nm]
            mu = self.cst[0:m, self.ccol[f"mu{e}"] + i:self.ccol[f"mu{e}"] + i + 1]
            om = self.cst[0:m, self.ccol[f"om{e}"] + i:self.ccol[f"om{e}"] + i + 1]
            pv = self.prevp[0:m, e * NS + i:e * NS + i + 1]
            return c0, m, mu, om, pv

        def lerp_proj(nm, Z, bZ):
            c0, m, mu, om, pv = seginfo(nm)
            if S:
                P.dma("sp", shs[0:m, 0:16], d["st_shift"][:, e, c0:c0 + m].rearrange("s c -> c s"), writes=[bshs],
                      allow_slow_non_contiguous=True)

            def evac(ps, bps, t0, mm):
                P.op("act", lambda en: en.copy(out=Pb[0:m, 0:mm], in_=ps[0:m, 0:mm]), reads=[bps], writes=[bPb])
                P.op("dve", lambda en: en.tensor_scalar(out=Z[0:m, t0:t0 + mm], in0=Pb[0:m, 0:mm], scalar1=om, scalar2=None,
                                                        op0=ALU.mult), reads=[bPb, bc], writes=[bZ])
                pe = min(t0 + mm, TP)
                pw = pe - t0
                if pw > 0:
                    if pw > 1:
                        P.op("dve", lambda en: en.scalar_tensor_tensor(
                            out=Z[0:m, t0 + 1:pe], in0=Pb[0:m, 0:pw - 1], scalar=mu, in1=Z[0:m, t0 + 1:pe],
                            op0=ALU.mult, op1=ALU.add), reads=[bPb, bZ, bc], writes=[bZ])
                    P.op("dve", lambda en: en.scalar_tensor_tensor(
                        out=Z[0:m, t0:t0 + 1], in0=pv, scalar=mu, in1=Z[0:m, t0:t0 + 1], op0=ALU.mult, op1=ALU.add),
                        reads=[self.bprev, bZ, bc], writes=[bZ])
                    P.op("dve", lambda en: en.tensor_copy(out=pv, in_=Pb[0:m, pw - 1:pw]), reads=[bPb], writes=[self.bprev])
                    if ti == 1 and pe == TP:
                        P.dma("pool", d["o_shift_p"][e, c0:c0 + m].rearrange("(p o) -> p o", o=1), Pb[0:m, pw - 1:pw],
                              reads=[bPb], allow_slow_non_contiguous=True)
                if t0 + mm > TP:
                    a0 = max(t0, TP) - t0
                    P3 = Pb[0:m, a0:a0 + 64].rearrange("p (s t) -> p s t", t=4)
                    Z3 = Z[0:m, TP:TP + 64].rearrange("p (s t) -> p s t", t=4)
                    P.op("dve", lambda en: en.scalar_tensor_tensor(
                        out=Z3[:, :, 1:4], in0=P3[:, :, 0:3], scalar=mu, in1=Z3[:, :, 1:4], op0=ALU.mult, op1=ALU.add),
                        reads=[bPb, bZ, bc], writes=[bZ])
                    P.op("dve", lambda en: en.scalar_tensor_tensor(
                        out=Z3[:, :, 0:1], in0=shs[0:m, 0:16].rearrange("p (s o) -> p s o", o=1), scalar=mu,
                        in1=Z3[:, :, 0:1], op0=ALU.mult, op1=ALU.add), reads=[bshs, bZ, bc], writes=[bZ])
                    P.dma("pool", d["o_shift_s"][:, e, c0:c0 + m].rearrange("s (c o) -> c s o", o=1), P3[:, :, 3:4],
                          reads=[bPb], allow_slow_non_contiguous=True)
            self.proj(win[:, c0:c0 + m], m, evac)

        for nm, fn in (("zw", AF.Tanh), ("za", AF.Copy), ("zg0", AF.Sigmoid), ("zg1", AF.Sigmoid)):
            m = self.segs[nm][2]
            lerp_proj(nm, Zt, bZt)
            dst, bd = lor[nm]
            P.op("act", lambda en, dst=dst, m=m, fn=fn: en.activation(out=dst[0:m, 0:n], in_=Zt[0:m, 0:n], func=fn),
                 reads=[bZt], writes=[bd])
        TW, ZA, SG0, SG1 = lor["zw"], lor["za"], lor["zg0"], lor["zg1"]
        names = ["SGW", "AA", "G", "KKN", "CS", "E1", "E2", "E3", "T", "BV", "AL", "BE", "KT", "RT", "OT", "Y"]
        W = {}
        for nm in names:
            W[nm] = A(nm, 128)
        TM, bTM = A("TM", 768)
        PA = {nm: A(nm, 256) for nm in ("N", "L", "N2", "L2", "PM", "MAK", "MRB", "MRK")}
        XS, bXS = A("XS", 128)
        Xsb, bXsb = A("Xsb", 128)
        Usb, bUsb = A("Usb", 128)
        OS, bOS = A("OS", 64)
        T2, bT2 = A("T2", 64)
        if S:
            SL, bSL = A("SL", 512)
            STs, bSTs = A("STs", 1024)
            BTq, bBTq = A("BTq", 128)
            KTq, bKTq = A("KTq", 128)
            T2s, bT2s = A("T2s", 256)

        def cv(key, hp):
            return self.cs(f"{key}{e}", hp)

        for hp in range(NHP):
            lerp_proj(f"r{hp}", R, bR)
            lerp_proj(f"k{hp}", Kk, bK)
            lerp_proj(f"v{hp}", V, bV)
            STp = self.ST[:, (e * NHP + hp) * 64:(e * NHP + hp + 1) * 64]
            def do_chunk(hp, c0, kind, STp):
                sm = (kind == "s")
                WD = 64 if sm else 128
                NCI = WD // 64
                sl = slice(c0, c0 + WD)
                t = {k: v[0][:, 0:WD] for k, v in W.items()}
                b = {k: v[1] for k, v in W.items()}
                for (wkey, src, kr, dst, act, bias) in (("rw_w2", TW, 96, "SGW", AF.Sigmoid, cv("rw_w0", hp)),
                                                        ("rw_a2", ZA, 96, "AA", AF.Sigmoid, cv("rw_a0", hp))):
                    w, bw = self.wload(d[wkey][e][:, hp * 128:(hp + 1) * 128].rearrange("(k p) c -> p k c", p=kr), kr, 1, 128,
                                       pool_cast=True)
                    ps, bps = self.psum()
                    P.op("pe", lambda en, ps=ps, w=w, src=src, kr=kr: en.matmul(
                        ps[:, 0:WD], lhsT=w[0:kr, 0:128], rhs=src[0][0:kr, sl], start=True, stop=True),
                        reads=[bw, src[1]], writes=[bps])
                    P.op("act", lambda en, ps=ps, dst=dst, act=act, bias=bias: en.activation(
                        out=t[dst], in_=ps[:, 0:WD], func=act, bias=bias), reads=[bps, bc], writes=[b[dst]])
                w, bw = self.wload(d["rw_g2"][e][:, hp * 128:(hp + 1) * 128].rearrange("(k p) c -> p k c", p=128), 128, 2, 128,
                                   pool_cast=True)
                ps, bps = self.psum()
                for j, sg in enumerate((SG0, SG1)):
                    P.op("pe", lambda en, ps=ps, w=w, sg=sg, j=j: en.matmul(
                        ps[:, 0:WD], lhsT=w[:, j * 128:(j + 1) * 128], rhs=sg[0][:, sl], start=(j == 0), stop=(j == 1)),
                        reads=[bw, sg[1]], writes=[bps])
                P.op("act", lambda en, ps=ps: en.copy(out=t["G"], in_=ps[:, 0:WD]), reads=[bps], writes=[b["G"]])
                dv = lambda fn, rd, wr: P.op("dve", fn, reads=rd, writes=wr)
                ac = lambda fn, rd, wr: P.op("act", fn, reads=rd, writes=wr)
                dv(lambda en: en.tensor_scalar(out=t["SGW"], in0=t["SGW"], scalar1=-0.6065306597126334, scalar2=None, op0=ALU.mult),
                   [b["SGW"]], [b["SGW"]])
                dv(lambda en: en.tensor_scalar(out=t["KKN"], in0=Kk[:, sl], scalar1=cv("rw_k_k", hp), scalar2=None, op0=ALU.mult),
                   [bK, bc], [b["KKN"]])
                dv(lambda en: en.tensor_tensor(out=t["T"], in0=t["KKN"], in1=t["KKN"], op=ALU.mult), [b["KKN"]], [b["T"]])
                ps, bps = self.psum()
                P.op("pe", lambda en, ps=ps: en.matmul(ps[:, 0:WD], lhsT=self.bones[:], rhs=t["T"], start=True, stop=True),
                     reads=[b["T"], bc], writes=[bps])
                dv(lambda en, ps=ps: en.tensor_scalar(out=t["T"], in0=ps[:, 0:WD], scalar1=1e-19, scalar2=None, op0=ALU.max),
                   [bps], [b["T"]])
                ac(lambda en: en.activation(out=t["T"], in_=t["T"], func=AF.Ln), [b["T"]], [b["T"]])
                ac(lambda en: en.activation(out=t["T"], in_=t["T"], func=AF.Exp, scale=-0.5), [b["T"]], [b["T"]])
                dv(lambda en: en.tensor_tensor(out=t["KKN"], in0=t["KKN"], in1=t["T"], op=ALU.mult), [b["KKN"], b["T"]], [b["KKN"]])
                dv(lambda en: en.tensor_scalar(out=t["T"], in0=t["AA"], scalar1=-1.0, scalar2=cv("rw_k_a", hp), op0=ALU.add, op1=ALU.mult),
                   [b["AA"], bc], [b["T"]])
                dv(lambda en: en.scalar_tensor_tensor(out=t["KT"], in0=t["T"], scalar=1.0, in1=Kk[:, sl], op0=ALU.add, op1=ALU.mult),
                   [b["T"], bK], [b["KT"]])
                dv(lambda en: en.tensor_tensor(out=t["T"], in0=R[:, sl], in1=t["KT"], op=ALU.mult), [bR, b["KT"]], [b["T"]])
                dv(lambda en: en.tensor_scalar(out=t["T"], in0=t["T"], scalar1=cv("rw_r_k", hp), scalar2=None, op0=ALU.mult),
                   [b["T"], bc], [b["T"]])
                ps, bps = self.psum()
                P.op("pe", lambda en, ps=ps: en.matmul(ps[:, 0:WD], lhsT=self.bones[:], rhs=t["T"], start=True, stop=True),
                     reads=[b["T"], bc], writes=[bps])
                dv(lambda en, ps=ps: en.tensor_tensor(out=t["BV"], in0=ps[:, 0:WD], in1=V[:, sl], op=ALU.mult), [bps, bV], [b["BV"]])
                dv(lambda en: en.tensor_tensor(out=t["BE"], in0=t["KKN"], in1=t["AA"], op=ALU.mult), [b["KKN"], b["AA"]], [b["BE"]])
                rst = self.mrst_s[:, 0:64] if sm else self.mrst[:, 0:128]
                dv(lambda en, rst=rst: en.tensor_tensor_scan(out=t["CS"], data0=rst, data1=t["SGW"], initial=0.0,
                                                             op0=ALU.mult, op1=ALU.add), [b["SGW"], bc], [b["CS"]])
                ac(lambda en: en.activation(out=t["E1"], in_=t["CS"], func=AF.Exp), [b["CS"]], [b["E1"]])
                ac(lambda en: en.activation(out=t["E2"], in_=t["CS"], func=AF.Exp, scale=-1.0), [b["CS"]], [b["E2"]])
                dv(lambda en: en.tensor_tensor(out=t["T"], in0=t["CS"], in1=t["SGW"], op=ALU.subtract), [b["CS"], b["SGW"]], [b["T"]])
                ac(lambda en: en.activation(out=t["E3"], in_=t["T"], func=AF.Exp), [b["T"]], [b["E3"]])
                dv(lambda en: en.tensor_tensor(out=t["RT"], in0=R[:, sl], in1=t["E1"], op=ALU.mult), [bR, b["E1"]], [b["RT"]])
                dv(lambda en: en.tensor_tensor(out=t["KT"], in0=t["KT"], in1=t["E2"], op=ALU.mult), [b["KT"], b["E2"]], [b["KT"]])
                dv(lambda en: en.tensor_tensor(out=t["BE"], in0=t["BE"], in1=t["E2"], op=ALU.mult), [b["BE"], b["E2"]], [b["BE"]])
                dv(lambda en: en.scalar_tensor_tensor(out=t["AL"], in0=t["KKN"], scalar=-1.0, in1=t["E3"], op0=ALU.mult, op1=ALU.mult),
                   [b["KKN"], b["E3"]], [b["AL"]])
                for ci in range(NCI):
                    cs_ = slice(ci * 64, (ci + 1) * 64)
                    ps, bps = self.psum()
                    for j, (src, bs_) in enumerate(((V[:, c0 + ci * 64:c0 + (ci + 1) * 64], bV), (t["BE"][:, cs_], b["BE"]),
                                                    (t["KT"][:, cs_], b["KT"]))):
                        P.op("pe", lambda en, ps=ps, src=src, j=j: en.transpose(out=ps[0:64, j * 128:(j + 1) * 128], in_=src,
                                                                                 identity=self.ident[:]),
                             reads=[bs_, bc], writes=[bps])
                    ac(lambda en, ps=ps, ci=ci: en.copy(out=TM[0:64, ci * 384:(ci + 1) * 384], in_=ps[0:64, 0:384]), [bps], [bTM])
                VM = lambda ci, h2: TM[0:64, ci * 384 + h2 * 64:ci * 384 + (h2 + 1) * 64]
                BT = lambda ci, h2: TM[0:64, ci * 384 + 128 + h2 * 64:ci * 384 + 128 + (h2 + 1) * 64]
                KTm = lambda ci, h2: TM[0:64, ci * 384 + 256 + h2 * 64:ci * 384 + 256 + (h2 + 1) * 64]
                msu, mui, msl = (self.msu_s, self.mui_s, self.msl_s) if sm else (self.msu, self.mui, self.msl)
                NSL = 2 * NCI
                SW = NSL * 64
                LOWP = ("N", "L", "N2", "L2", "PM")
                pa = {k: (v[0].bitcast(BF16)[:, 0:SW] if k in LOWP else v[0][:, 0:SW]) for k, v in PA.items()}
                pb = {k: v[1] for k, v in PA.items()}
                Xb = Xsb.bitcast(BF16)
                slot = lambda h2, ci: slice((h2 * NCI + ci) * 64, (h2 * NCI + ci + 1) * 64)

                def kstage(dst, lhs, rhs, mask):
                    for h2 in range(2):
                        ps, bps = self.psum()
                        hs = slice(h2 * 64, (h2 + 1) * 64)
                        for ci in range(NCI):
                            cs_ = slice(ci * 64, (ci + 1) * 64)
                            P.op("pe", lambda en, ps=ps, hs=hs, cs_=cs_: en.matmul(ps[0:64, cs_], lhsT=t[lhs][hs, cs_], rhs=t[rhs][hs, cs_],
                                                                                   start=True, stop=True),
                                 reads=[b[lhs], b[rhs]], writes=[bps])
                        dv(lambda en, ps=ps, h2=h2: en.tensor_tensor(
                            out=pa[dst][0:64, h2 * NCI * 64:(h2 + 1) * NCI * 64], in0=ps[0:64, 0:NCI * 64], in1=mask[0:64, 0:NCI * 64],
                            op=ALU.mult), [bps, bc], [pb[dst]])
                kstage("N", "BE", "AL", msu)
                kstage("L", "AL", "BE", msl)
                kstage("MAK", "KT", "AL", msu)
                kstage("MRB", "BE", "RT", mui)
                kstage("MRK", "KT", "RT", mui)
                dv(lambda en: en.tensor_tensor(out=pa["PM"][0:64, :], in0=pa["N"][0:64, :], in1=self.id4[0:64, 0:SW], op=ALU.add),
                   [pb["N"], bc], [pb["PM"]])
                cur = ("N", "L", "N2", "L2")
                for step in range(1 if sm else 5):
                    Nn, Ln, N2n, L2n = cur
                    for (dst, lh, rh, eng) in ((N2n, Ln, Nn, "act"), (L2n, Nn, Ln, "dve")):
                        ps, bps = self.psum()
                        for u in range(NSL):
                            us = slice(u * 64, (u + 1) * 64)
                            P.op("pe", lambda en, ps=ps, us=us, lh=lh, rh=rh: en.matmul(
                                ps[0:64, us], lhsT=pa[lh][0:64, us], rhs=pa[rh][0:64, us], start=True, stop=True),
                                reads=[pb[lh], pb[rh]], writes=[bps])
                        if eng == "act":
                            ac(lambda en, ps=ps, dst=dst: en.copy(out=pa[dst][0:64, :], in_=ps[0:64, 0:SW]), [bps], [pb[dst]])
                        else:
                            dv(lambda en, ps=ps, dst=dst: en.tensor_copy(out=pa[dst][0:64, :], in_=ps[0:64, 0:SW]), [bps], [pb[dst]])
                    ps, bps = self.psum()
                    for u in range(NSL):
                        us = slice(u * 64, (u + 1) * 64)
                        P.op("pe", lambda en, ps=ps, us=us, L2n=L2n: en.matmul(
                            ps[0:64, us], lhsT=pa[L2n][0:64, us], rhs=pa["PM"][0:64, us], start=True, stop=True),
                            reads=[pb[L2n], pb["PM"]], writes=[bps])
                    dv(lambda en, ps=ps: en.tensor_tensor(out=pa["PM"][0:64, :], in0=ps[0:64, 0:SW], in1=pa["PM"][0:64, :], op=ALU.add),
                       [bps, pb["PM"]], [pb["PM"]])
                    cur = (N2n, L2n, Nn, Ln)
                if sm:
                    for g in range(4):
                        for h2 in range(2):
                            P.dma("sp", SL[0:64, :].rearrange("p (s c) -> p s c", c=128)[:, :, h2 * 64:(h2 + 1) * 64],
                                  d["st_rwkv"][g * 4:(g + 1) * 4, e, 2 * hp + h2].rearrange("s v k -> v s k"), writes=[bSL])
                        ps, bps = self.psum()
                        for q in range(4):
                            P.op("pe", lambda en, ps=ps, q=q: en.transpose(out=ps[:, q * 64:(q + 1) * 64],
                                                                           in_=SL[0:64, q * 128:(q + 1) * 128], identity=self.ident[0:64, 0:64]),
                                 reads=[bSL, bc], writes=[bps])
                        ac(lambda en, ps=ps, g=g: en.copy(out=STs[:, g * 256:(g + 1) * 256], in_=ps[:, 0:256]), [bps], [bSTs])
                for ci in range(NCI):
                    co = ci * 64
                    if sm:
                        subs = [(4 * q, 4, STs[:, q * 64:(q + 1) * 64], bSTs) for q in range(16)]
                    else:
                        subs = [(0, 64, STp, self.bST)]
                    for h2 in range(2):
                        hs = slice(h2 * 64, (h2 + 1) * 64)
                        ps, bps = self.psum()
                        ps2, bps2 = self.psum()
                        for (o_, ln, Ssrc, bS_) in subs:
                            P.op("pe", lambda en, ps=ps, o_=o_, ln=ln, Ssrc=Ssrc, hs=hs, co=co: en.matmul(
                                ps[0:64, o_:o_ + ln], lhsT=Ssrc[hs, :], rhs=t["AL"][hs, co + o_:co + o_ + ln], start=True, stop=True),
                                reads=[bS_, b["AL"]], writes=[bps])
                            P.op("pe", lambda en, ps2=ps2, o_=o_, ln=ln, Ssrc=Ssrc, hs=hs, co=co: en.matmul(
                                ps2[hs, o_:o_ + ln], lhsT=Ssrc[hs, :], rhs=t["RT"][hs, co + o_:co + o_ + ln], start=True, stop=True),
                                reads=[bS_, b["RT"]], writes=[bps2])
                        ac(lambda en, ps=ps, hs=hs: en.copy(out=XS[0:64, hs], in_=ps[0:64, 0:64]), [bps], [bXS])
                        ac(lambda en, ps2=ps2, hs=hs: en.copy(out=OS[hs, 0:64], in_=ps2[hs, 0:64]), [bps2], [bOS])
                    ps, bps = self.psum()
                    for h2 in range(2):
                        hs = slice(h2 * 64, (h2 + 1) * 64)
                        P.op("pe", lambda en, ps=ps, hs=hs, h2=h2, ci=ci: en.matmul(ps[0:64, hs], lhsT=pa["MAK"][0:64, slot(h2, ci)],
                                                                                    rhs=VM(ci, h2), start=True, stop=False),
                             reads=[pb["MAK"], bTM], writes=[bps])
                        P.op("pe", lambda en, ps=ps, hs=hs: en.matmul(ps[0:64, hs], lhsT=XS[0:64, hs], rhs=self.ident[0:64, 0:64],
                                                                      start=False, stop=True),
                             reads=[bXS, bc], writes=[bps])
                    ac(lambda en, ps=ps: en.copy(out=Xb[0:64, 0:128], in_=ps[0:64, 0:128]), [bps], [bXsb])
                    ps, bps = self.psum()
                    for h2 in range(2):
                        hs = slice(h2 * 64, (h2 + 1) * 64)
                        P.op("pe", lambda en, ps=ps, hs=hs, h2=h2, ci=ci: en.matmul(ps[0:64, hs], lhsT=pa["PM"][0:64, slot(h2, ci)],
                                                                                    rhs=Xb[0:64, hs], start=True, stop=True),
                             reads=[pb["PM"], bXsb], writes=[bps])
                    ac(lambda en, ps=ps: en.copy(out=Usb[0:64, :], in_=ps[0:64, 0:128]), [bps], [bUsb])
                    ps, bps = self.psum()
                    for h2 in range(2):
                        hs = slice(h2 * 64, (h2 + 1) * 64)
                        P.op("pe", lambda en, ps=ps, hs=hs, h2=h2, ci=ci: en.matmul(ps[hs, 0:64], lhsT=Usb[0:64, hs],
                                                                                    rhs=pa["MRB"][0:64, slot(h2, ci)], start=True, stop=False),
                             reads=[bUsb, pb["MRB"]], writes=[bps])
                        P.op("pe", lambda en, ps=ps, hs=hs, h2=h2, ci=ci: en.matmul(ps[hs, 0:64], lhsT=VM(ci, h2),
                                                                                    rhs=pa["MRK"][0:64, slot(h2, ci)], start=False, stop=True),
                             reads=[bTM, pb["MRK"]], writes=[bps])
                    dv(lambda en, ps=ps, co=co: en.tensor_tensor(out=t["OT"][:, co:co + 64], in0=ps[:, 0:64], in1=OS[:, 0:64], op=ALU.add),
                       [bps, bOS], [b["OT"]])
                    if not sm:
                        ps, bps = self.psum()
                        for h2 in range(2):
                            hs = slice(h2 * 64, (h2 + 1) * 64)
                            P.op("pe", lambda en, ps=ps, hs=hs, h2=h2, ci=ci: en.matmul(ps[hs, 0:64], lhsT=BT(ci, h2), rhs=Usb[0:64, hs],
                                                                                        start=True, stop=False),
                                 reads=[bTM, bUsb], writes=[bps])
                            P.op("pe", lambda en, ps=ps, hs=hs, h2=h2, ci=ci: en.matmul(ps[hs, 0:64], lhsT=KTm(ci, h2), rhs=VM(ci, h2),
                                                                                        start=False, stop=True),
                                 reads=[bTM], writes=[bps])
                        dv(lambda en, ps=ps: en.tensor_tensor(out=T2[:, 0:64], in0=ps[:, 0:64], in1=STp, op=ALU.add), [bps, self.bST], [bT2])
                        dv(lambda en, co=co: en.tensor_scalar(out=STp, in0=T2[:, 0:64], scalar1=t["E1"][:, co + 63:co + 64], scalar2=None,
                                                              op0=ALU.mult), [bT2, b["E1"]], [self.bST])
                        if ti == 1 and c0 + co == TP - 64:
                            ps, bps = self.psum()
                            P.op("pe", lambda en, ps=ps: en.transpose(out=ps[0:64, 0:128], in_=STp, identity=self.ident[:]),
                                 reads=[self.bST, bc], writes=[bps])
                            ac(lambda en, ps=ps: en.copy(out=Xsb[0:64, :], in_=ps[0:64, 0:128]), [bps], [bXsb])
                            P.dma("pool", d["o_rwkv_p"][e, 2 * hp:2 * hp + 2].rearrange("h v k -> v h k"),
                                  Xsb[0:64, :].rearrange("p (h k) -> p h k", k=64), reads=[bXsb])
                    else:
                        for g in range(4):
                            ps, bps = self.psum()
                            for q4 in range(4):
                                q = g * 4 + q4
                                dv(lambda en, q=q: en.tensor_scalar(out=BTq[0:64, :], in0=TM[0:64, 128:256], scalar1=self.rowm[:, q:q + 1],
                                                                    scalar2=None, op0=ALU.mult), [bTM, bc], [bBTq])
                                dv(lambda en, q=q: en.tensor_scalar(out=KTq[0:64, :], in0=TM[0:64, 256:384], scalar1=self.rowm[:, q:q + 1],
                                                                    scalar2=None, op0=ALU.mult), [bTM, bc], [bKTq])
                                for h2 in range(2):
                                    hs = slice(h2 * 64, (h2 + 1) * 64)
                                    P.op("pe", lambda en, ps=ps, hs=hs, q4=q4: en.matmul(
                                        ps[hs, q4 * 64:(q4 + 1) * 64], lhsT=BTq[0:64, hs], rhs=Usb[0:64, hs], start=True, stop=False),
                                        reads=[bBTq, bUsb], writes=[bps])
                                    P.op("pe", lambda en, ps=ps, hs=hs, h2=h2, q4=q4: en.matmul(
                                        ps[hs, q4 * 64:(q4 + 1) * 64], lhsT=KTq[0:64, hs], rhs=VM(0, h2), start=False, stop=True),
                                        reads=[bKTq, bTM], writes=[bps])
                            gs = slice(g * 256, (g + 1) * 256)
                            dv(lambda en, ps=ps, gs=gs: en.tensor_tensor(out=T2s[:, 0:256], in0=ps[:, 0:256], in1=STs[:, gs], op=ALU.add),
                               [bps, bSTs], [bT2s])
                            lam = t["E1"][:, 0:64].rearrange("p (s t) -> p s t", t=4)[:, g * 4:(g + 1) * 4, 3:4]
                            dv(lambda en, gs=gs, lam=lam: en.tensor_tensor(
                                out=STs[:, gs].rearrange("p (s v) -> p s v", v=64), in0=T2s[:, 0:256].rearrange("p (s v) -> p s v", v=64),
                                in1=lam.broadcast_to([128, 4, 64]), op=ALU.mult), [bT2s, b["E1"]], [bSTs])
                            ps, bps = self.psum()
                            for q4 in range(4):
                                P.op("pe", lambda en, ps=ps, q4=q4, g=g: en.transpose(
                                    out=ps[0:64, q4 * 128:(q4 + 1) * 128], in_=STs[:, (g * 4 + q4) * 64:(g * 4 + q4 + 1) * 64],
                                    identity=self.ident[:]), reads=[bSTs, bc], writes=[bps])
                            ac(lambda en, ps=ps: en.copy(out=SL[0:64, 0:512], in_=ps[0:64, 0:512]), [bps], [bSL])
                            for h2 in range(2):
                                P.dma("pool", d["o_rwkv_s"][g * 4:(g + 1) * 4, e, 2 * hp + h2].rearrange("s v k -> v s k"),
                                      SL[0:64, :].rearrange("p (s c) -> p s c", c=128)[:, :, h2 * 64:(h2 + 1) * 64], reads=[bSL])
                ps, bps = self.psum()
                P.op("pe", lambda en, ps=ps: en.matmul(ps[:, 0:WD], lhsT=self.bones[:], rhs=t["OT"], start=True, stop=True),
                     reads=[b["OT"], bc], writes=[bps])
                dv(lambda en, ps=ps: en.scalar_tensor_tensor(out=t["Y"], in0=ps[:, 0:WD], scalar=-1.0 / 64, in1=t["OT"],
                                                             op0=ALU.mult, op1=ALU.add), [bps, b["OT"]], [b["Y"]])
                dv(lambda en: en.tensor_tensor(out=t["T"], in0=t["Y"], in1=t["Y"], op=ALU.mult), [b["Y"]], [b["T"]])
                ps, bps = self.psum()
                P.op("pe", lambda en, ps=ps: en.matmul(ps[:, 0:WD], lhsT=self.bones[:], rhs=t["T"], start=True, stop=True),
                     reads=[b["T"], bc], writes=[bps])
                ac(lambda en, ps=ps: en.activation(out=t["T"], in_=ps[:, 0:WD], func=AF.Ln, scale=1.0 / 64, bias=self.eps2[:, 0:1]),
                   [bps, bc], [b["T"]])
                ac(lambda en: en.activation(out=t["T"], in_=t["T"], func=AF.Exp, scale=-0.5), [b["T"]], [b["T"]])
                dv(lambda en: en.tensor_tensor(out=t["Y"], in0=t["Y"], in1=t["T"], op=ALU.mult), [b["Y"], b["T"]], [b["Y"]])
                dv(lambda en: en.tensor_scalar(out=t["Y"], in0=t["Y"], scalar1=cv("rw_ln_w", hp), scalar2=cv("rw_ln_b", hp),
                                               op0=ALU.mult, op1=ALU.add), [b["Y"], bc], [b["Y"]])
                dv(lambda en: en.tensor_tensor(out=t["Y"], in0=t["Y"], in1=t["BV"], op=ALU.add), [b["Y"], b["BV"]], [b["Y"]])
                dv(lambda en: en.tensor_tensor(out=ob[:, sl], in0=t["Y"], in1=t["G"], op=ALU.mult), [b["Y"], b["G"]], [bob])
            for (c0, kind) in chunks:
                do_chunk(hp, c0, kind, STp)
            self.outproj(wout, hp, ob, bob)
        if "nohg" not in self.stages:
            self.hgrn(l, base_off, ob, bob)

    def hgrn(self, l, base_off, ob, bob):
        P, c, d = self.P, self.c, self.d
        e = l // 2
        KC, TP, D, RW, NHP, NHG = c.KC, c.TP, c.D, c.RW, c.NHP, c.NHG
        n, ti, NM = self.n, self.ti, self.NMAX
        S = (ti == 0)
        bc = self.bconst
        win, wout = d["ev_w_in"][e], d["ev_w_out"][e]
        off = [base_off]

        def A(name, size):
            ap, b = self.aclaim(name, off[0], size)
            off[0] += size
            return ap, b
        chunks = [(c0, "p") for c0 in range(0, TP, 128)] + ([(TP, "s")] if S else [])
        Q, bQ = A("hQ", NM)
        LF, bLF = A("hLF", NM)
        KH, bKH = A("hKH", NM)
        IV, bIV = A("hIV", NM)
        OG, bOG = A("hOG", NM)
        W = {nm: A("h" + nm, 128) for nm in ("G", "EG", "QP", "KP", "KPP", "T", "OT", "Y")}
        TMf, bTM = A("hTM", 512)
        ATf, bAT = A("hAT", 128)
        if S:
            Ss, bSs = A("hSs", 2048)
            KQ, bKQ = A("hKQ", 128)
        P.op("pool", lambda en: en.memset(TMf[:, :], 0.0), writes=[bTM])
        P.op("pool", lambda en: en.memset(ATf[:, :], 0.0), writes=[bAT])
        P.op("pool", lambda en: en.memset(KQ[:, :], 0.0), writes=[bKQ]) if S else None
        Wt = W
        b = {k: v[1] for k, v in W.items()}
        dv = lambda fn, rd, wr: P.op("dve", fn, reads=rd, writes=wr)
        ac = lambda fn, rd, wr: P.op("act", fn, reads=rd, writes=wr)
        for hg in range(NHG):
            lb = self.cs(f"lb{e}", hg)
            oml = self.cs(f"oml{e}", hg)
            colq = c.P_RW + hg * 128

            def ev_q(ps, bps, t0, mm):
                ac(lambda en: en.activation(out=Q[:, t0:t0 + mm], in_=ps[:, 0:mm], func=AF.Silu), [bps], [bQ])

            def ev_f(ps, bps, t0, mm, lb=lb, oml=oml):
                ac(lambda en: en.activation(out=KH[:, t0:t0 + mm], in_=ps[:, 0:mm], func=AF.Sigmoid), [bps], [bKH])
                dv(lambda en: en.tensor_scalar(out=KH[:, t0:t0 + mm], in0=KH[:, t0:t0 + mm], scalar1=oml, scalar2=lb,
                                               op0=ALU.mult, op1=ALU.add), [bKH, bc], [bKH])
                ac(lambda en: en.activation(out=LF[:, t0:t0 + mm], in_=KH[:, t0:t0 + mm], func=AF.Ln), [bKH], [bLF])
                dv(lambda en: en.tensor_scalar(out=KH[:, t0:t0 + mm], in0=KH[:, t0:t0 + mm], scalar1=-1.0, scalar2=1.0,
                                               op0=ALU.mult, op1=ALU.add), [bKH, bLF], [bKH])

            def ev_i(ps, bps, t0, mm):
                ac(lambda en: en.copy(out=IV[:, t0:t0 + mm], in_=ps[:, 0:mm]), [bps], [bIV])

            def ev_o(ps, bps, t0, mm):
                ac(lambda en: en.activation(out=OG[:, t0:t0 + mm], in_=ps[:, 0:mm], func=AF.Silu), [bps], [bOG])
            self.proj(win[:, colq:colq + 128], 128, ev_q)
            self.proj(win[:, colq + RW:colq + RW + 128], 128, ev_f)
            self.proj(win[:, colq + 2 * RW:colq + 2 * RW + 128], 128, ev_i)
            self.proj(win[:, colq + 3 * RW:colq + 3 * RW + 128], 128, ev_o)
            SHp = self.SH[:, (e * NHG + hg) * 128:(e * NHG + hg + 1) * 128]

            def do_chunk(hg, c0, kind, SHp):
                sm = (kind == "s")
                WD = 64 if sm else 128
                NCI = WD // 64
                sl = slice(c0, c0 + WD)
                t = {k: v[0][:, 0:WD] for k, v in Wt.items()}
                rst = self.mrst_s[:, 0:64] if sm else self.mrst[:, 0:128]
                dv(lambda en: en.tensor_tensor_scan(out=t["G"], data0=rst, data1=LF[:, sl], initial=0.0, op0=ALU.mult, op1=ALU.add),
                   [bLF, bc], [b["G"]])
                ac(lambda en: en.activation(out=t["EG"], in_=t["G"], func=AF.Exp), [b["G"]], [b["EG"]])
                dv(lambda en: en.tensor_tensor(out=t["QP"], in0=Q[:, sl], in1=t["EG"], op=ALU.mult), [bQ, b["EG"]], [b["QP"]])
                ac(lambda en: en.activation(out=t["T"], in_=t["G"], func=AF.Exp, scale=-1.0), [b["G"]], [b["T"]])
                dv(lambda en: en.tensor_tensor(out=t["KP"], in0=KH[:, sl], in1=t["T"], op=ALU.mult), [bKH, b["T"]], [b["KP"]])
                tl = 4 if sm else 64
                g3 = t["G"].rearrange("p (s t) -> p s t", t=tl)
                dv(lambda en: en.tensor_tensor(out=t["T"].rearrange("p (s t) -> p s t", t=tl),
                                               in0=g3[:, :, tl - 1:tl].broadcast_to([128, WD // tl, tl]), in1=g3, op=ALU.subtract),
                   [b["G"]], [b["T"]])
                ac(lambda en: en.activation(out=t["T"], in_=t["T"], func=AF.Exp), [b["T"]], [b["T"]])
                dv(lambda en: en.tensor_tensor(out=t["KPP"], in0=KH[:, sl], in1=t["T"], op=ALU.mult), [bKH, b["T"]], [b["KPP"]])
                mui = self.mui_s if sm else self.mui
                psA, bpsA = self.psum()
                for ci in range(NCI):
                    cs_ = slice(ci * 64, (ci + 1) * 64)
                    ps, bps = self.psum()
                    P.op("pe", lambda en, ps=ps, ci=ci: en.transpose(out=ps[0:64, 0:128], in_=IV[:, c0 + ci * 64:c0 + (ci + 1) * 64],
                                                                     identity=self.ident[:]), reads=[bIV, bc], writes=[bps])
                    P.op("pe", lambda en, ps=ps, cs_=cs_: en.transpose(out=ps[0:64, 128:256], in_=t["KPP"][:, cs_], identity=self.ident[:]),
                         reads=[b["KPP"], bc], writes=[bps])
                    ac(lambda en, ps=ps, ci=ci: en.copy(out=TMf[0:64, ci * 256:(ci + 1) * 256], in_=ps[0:64, 0:256]), [bps], [bTM])
                    P.op("pe", lambda en, cs_=cs_: en.matmul(psA[0:64, cs_], lhsT=t["KP"][:, cs_], rhs=t["QP"][:, cs_], start=True, stop=True),
                         reads=[b["KP"], b["QP"]], writes=[bpsA])
                dv(lambda en: en.tensor_tensor(out=ATf[0:64, 0:WD], in0=psA[0:64, 0:WD], in1=mui[0:64, 0:WD], op=ALU.mult),
                   [bpsA, bc], [bAT])
                if sm:
                    P.dma("sp", Ss[:, :].rearrange("p (s v) -> p s v", v=128), d["st_hgrn"][:, e, hg].rearrange("s k v -> k s v"),
                          writes=[bSs])
                for ci in range(NCI):
                    cs_ = slice(ci * 64, (ci + 1) * 64)
                    IM = TMf[:, ci * 256:ci * 256 + 128]
                    KT2 = TMf[:, ci * 256 + 128:ci * 256 + 256]
                    ps, bps = self.psum()
                    P.op("pe", lambda en, ps=ps, IM=IM, cs_=cs_: en.matmul(ps[:, 0:64], lhsT=IM, rhs=ATf[:, cs_], start=True, stop=False),
                         reads=[bTM, bAT], writes=[bps])
                    if sm:
                        for q in range(16):
                            P.op("pe", lambda en, ps=ps, q=q: en.matmul(ps[:, 4 * q:4 * q + 4], lhsT=Ss[:, q * 128:(q + 1) * 128],
                                                                        rhs=t["QP"][:, 4 * q:4 * q + 4], start=False, stop=(q == 15)),
                                 reads=[bSs, b["QP"]], writes=[bps])
                    else:
                        P.op("pe", lambda en, ps=ps, cs_=cs_: en.matmul(ps[:, 0:64], lhsT=SHp, rhs=t["QP"][:, cs_], start=False, stop=True),
                             reads=[self.bSH, b["QP"]], writes=[bps])
                    ac(lambda en, ps=ps, cs_=cs_: en.copy(out=t["OT"][:, cs_], in_=ps[:, 0:64]), [bps], [b["OT"]])
                    if not sm:
                        ps, bps = self.psum()
                        P.op("pe", lambda en, ps=ps, IM=IM, KT2=KT2: en.matmul(ps[:, 0:128], lhsT=KT2, rhs=IM, start=True, stop=True),
                             reads=[bTM], writes=[bps])
                        dv(lambda en, ps=ps, ci=ci: en.scalar_tensor_tensor(out=SHp, in0=SHp, scalar=t["EG"][:, ci * 64 + 63:ci * 64 + 64],
                                                                            in1=ps[:, 0:128], op0=ALU.mult, op1=ALU.add),
                           [bps, self.bSH, b["EG"]], [self.bSH])
                        if ti == 1 and c0 + ci * 64 == TP - 64:
                            P.dma("pool", d["o_hgrn_p"][e, hg], SHp, reads=[self.bSH])
                    else:
                        eg3 = t["EG"][:, 0:64].rearrange("p (s t) -> p s t", t=4)
                        for g in range(4):
                            ps, bps = self.psum()
                            for q4 in range(4):
                                q = 4 * g + q4
                                dv(lambda en, q=q: en.tensor_scalar(out=KQ[0:64, :], in0=TMf[0:64, 128:256], scalar1=self.rowm[:, q:q + 1],
                                                                    scalar2=None, op0=ALU.mult), [bTM, bc], [bKQ])
                                P.op("pe", lambda en, ps=ps, q4=q4, IM=IM: en.matmul(ps[:, q4 * 128:(q4 + 1) * 128], lhsT=KQ[:, :], rhs=IM,
                                                                                     start=True, stop=True), reads=[bKQ, bTM], writes=[bps])
                            gs = slice(g * 512, (g + 1) * 512)
                            dv(lambda en, gs=gs, g=g: en.tensor_tensor(
                                out=Ss[:, gs].rearrange("p (s v) -> p s v", v=128), in0=Ss[:, gs].rearrange("p (s v) -> p s v", v=128),
                                in1=eg3[:, 4 * g:4 * g + 4, 3:4].broadcast_to([128, 4, 128]), op=ALU.mult), [bSs, b["EG"]], [bSs])
                            dv(lambda en, ps=ps, gs=gs: en.tensor_tensor(out=Ss[:, gs], in0=ps[:, 0:512], in1=Ss[:, gs], op=ALU.add),
                               [bps, bSs], [bSs])
                        P.dma("pool", d["o_hgrn_s"][:, e, hg].rearrange("s k v -> k s v"), Ss[:, :].rearrange("p (s v) -> p s v", v=128),
                              reads=[bSs])
                dv(lambda en: en.tensor_tensor(out=t["T"], in0=t["OT"], in1=t["OT"], op=ALU.mult), [b["OT"]], [b["T"]])
                ps, bps = self.psum()
                P.op("pe", lambda en, ps=ps: en.matmul(ps[:, 0:WD], lhsT=self.ones[:], rhs=t["T"], start=True, stop=True),
                     reads=[b["T"], bc], writes=[bps])
                ac(lambda en, ps=ps: en.activation(out=t["T"], in_=ps[:, 0:WD], func=AF.Ln, scale=1.0 / 128, bias=self.eps2[:, 1:2]),
                   [bps, bc], [b["T"]])
                ac(lambda en: en.activation(out=t["T"], in_=t["T"], func=AF.Exp, scale=-0.5), [b["T"]], [b["T"]])
                dv(lambda en: en.scalar_tensor_tensor(out=t["Y"], in0=t["OT"], scalar=self.cs(f"hg_norm{e}", hg), in1=t["T"],
                                                      op0=ALU.mult, op1=ALU.mult), [b["OT"], b["T"], bc], [b["Y"]])
                dv(lambda en: en.tensor_tensor(out=ob[:, sl], in0=t["Y"], in1=OG[:, sl], op=ALU.mult), [b["Y"], bOG], [bob])
            for (c0, kind) in chunks:
                do_chunk(hg, c0, kind, SHp)
            self.outproj(wout, NHP + hg, ob, bob)

    def tile(self, ti):
        c = self.c
        self.ti = ti
        self.n = c.TP + (64 if ti == 0 else 0)
        self.load_x(ti)
        for l in range(c.DEPTH):
            if "ffn" in self.stages:
                self.ffn(l, 1)
            if l % 2 == 0 and "even" in self.stages:
                self.even(l)
            if l % 2 == 1 and "odd" in self.stages:
                self.odd(l)
            if "ffn" in self.stages:
                self.ffn(l, 2)
        self.store_y(ti)


INPUT_ORDER = ["x_prompt", "x_sample", "state_rwkv_shift", "state_rwkv", "state_hgrn", "state_conv", "state_lru"]


def make_in_maps(cfg, inputs):
    f = lambda a: np.ascontiguousarray(np.asarray(a, dtype=np.float32))
    nb = inputs["x_prompt"].shape[0]
    maps = []
    for core in range(NCORES):
        m = {}
        if core < nb:
            m["xp"] = f(inputs["x_prompt"][core])
        else:
            m["xp"] = np.zeros((cfg.SEQ, cfg.D), np.float32)
        sl = slice(core * 16, (core + 1) * 16)
        m["xs"] = f(inputs["x_sample"][sl]).reshape(64, cfg.D)
        m["st_shift"] = f(inputs["state_rwkv_shift"][sl])
        m["st_rwkv"] = f(inputs["state_rwkv"][sl])
        m["st_hgrn"] = f(inputs["state_hgrn"][sl])
        m["st_conv"] = f(inputs["state_conv"][sl])
        m["st_lru"] = f(inputs["state_lru"][sl])
        for k in inputs:
            if k not in INPUT_ORDER:
                m[k] = f(inputs[k])
        maps.append(m)
    return maps


def assemble(cfg, res, nb):
    R = res.results
    cat = lambda k, cores: np.stack([R[i][k] for i in cores], 0)
    P = list(range(nb))
    A = list(range(NCORES))
    y_prompt = cat("yp", P)
    y_sample = np.concatenate([R[i]["ys"].reshape(16, 4, cfg.D) for i in A], 0)
    outs = [y_prompt, y_sample]
    for k in ["o_shift_p", "o_rwkv_p", "o_hgrn_p", "o_conv_p", "o_lru_p"]:
        outs.append(cat(k, P))
    for k in ["o_shift_s", "o_rwkv_s", "o_hgrn_s", "o_conv_s", "o_lru_s"]:
        outs.append(np.concatenate([R[i][k] for i in A], 0))
    return tuple(np.ascontiguousarray(o.astype(np.float32)) for o in outs)


_CACHE = {}
STAGES = ("ffn", "odd", "even")


def run(cfg, inputs, stages=("ffn", "odd", "even")):
    key = (cfg.D, cfg.DFF, cfg.SEQ, stages)
    if key not in _CACHE:
        _CACHE[key] = K(cfg, stages).build()
    nc = _CACHE[key]
    maps = make_in_maps(cfg, inputs)
    res = run_bass_kernel_spmd(nc, maps, core_ids=list(range(NCORES)))
    return assemble(cfg, res, inputs["x_prompt"].shape[0])


def kernel(**inputs):
    return run(Cfg(), inputs, STAGES)
```

```python
import numpy as np
import concourse.bass as bass
import concourse.mybir as mybir
from concourse.bass_utils import run_bass_kernel_spmd

F32 = mybir.dt.float32
BF16 = mybir.dt.bfloat16
AF = mybir.ActivationFunctionType
ALU = mybir.AluOpType
AX = mybir.AxisListType
MAXV = 30000
NCORES = 8


class Cfg:
    def __init__(self, D=2048, DFF=5632, SEQ=2048, NSS=16, DEPTH=4):
        self.D, self.DFF, self.SEQ, self.NSS, self.DEPTH = D, DFF, SEQ, NSS, DEPTH
        self.KC = D // 128
        self.FC = DFF // 128
        self.TP = SEQ // 2
        self.RW = D // 2
        self.NHP = self.RW // 128
        self.NH = self.RW // 64
        self.NHG = self.RW // 128
        self.NLB = D // 256
        self.P_RW = 3 * self.RW + 448
        self.P_EVEN = self.P_RW + 4 * self.RW
        self.NEV = (DEPTH + 1) // 2
        self.NOD = DEPTH // 2


class Buf:
    __slots__ = ("name", "w", "r", "excl")

    def __init__(self, name, excl=False):
        self.name = name
        self.w = None
        self.r = {}
        self.excl = excl


class Prog:
    def __init__(self, nc, sems):
        self.nc = nc
        self.sems = list(sems)
        self.si = 0
        self.E = {}
        for e in ("pe", "act", "dve", "pool", "sp"):
            self.E[e] = dict(ops=[], sem=None, val=0, waited={})
        self.ring = {}
        self.nops = 0

    def new_sem(self):
        s = self.sems[self.si]
        self.si += 1
        return s

    def _deps(self, e, reads, writes):
        deps = {}

        def add(t):
            if t is None:
                return
            s, v, te = t
            if e == "pe" and te == "pe":
                return
            k = id(s)
            if k not in deps or deps[k][1] < v:
                deps[k] = (s, v)

        for b in reads:
            add(b.w)
            if b.excl:
                for t in b.r.values():
                    if t[2] != e:
                        add(t)
        for b in writes:
            add(b.w)
            for t in b.r.values():
                add(t)
        W = self.E[e]["waited"]
        out = []
        for k, (s, v) in deps.items():
            if W.get(k, 0) >= v:
                continue
            W[k] = v
            out.append((s, v))
        return out

    def _mark(self, tok, reads, writes):
        k = id(tok[0])
        for b in reads:
            o = b.r.get(k)
            if o is None or o[1] < tok[1]:
                b.r[k] = tok
        for b in writes:
            b.w = tok
            b.r = {}

    def op(self, e, fn, reads=(), writes=()):
        E = self.E[e]
        waits = self._deps(e, reads, writes)
        if E["sem"] is None or E["val"] >= MAXV:
            E["sem"] = self.new_sem()
            E["val"] = 0
        E["val"] += 1
        tok = (E["sem"], E["val"], e)
        E["ops"].append((waits, fn, E["sem"], 1))
        self._mark(tok, reads, writes)
        self.nops += 1
        return tok

    def dma(self, q, out, in_, reads=(), writes=(), **kw):
        if q not in self.ring:
            self.ring[q] = dict(slots=[[self.new_sem(), 0] for _ in range(10)], i=0)
        R = self.ring[q]
        E = self.E[q]
        sl = R["slots"][R["i"] % len(R["slots"])]
        R["i"] += 1
        waits = self._deps(q, reads, writes)
        if sl[1] > 0:
            k = id(sl[0])
            if E["waited"].get(k, 0) < sl[1]:
                E["waited"][k] = sl[1]
                waits.append((sl[0], sl[1]))
        if sl[1] + 16 > MAXV:
            sl[0] = self.new_sem()
            sl[1] = 0
        sl[1] += 16
        tok = (sl[0], sl[1], "dma")
        E["ops"].append((waits, lambda eng: eng.dma_start(out=out, in_=in_, **kw), sl[0], 16))
        self._mark(tok, reads, writes)
        return tok

    def emit(self, block):
        nc = self.nc
        P = self

        def run(eng, name):
            for waits, fn, sem, inc in P.E[name]["ops"]:
                for s, v in waits:
                    eng.wait_ge(s, v)
                fn(eng).then_inc(sem, inc)
            if name in P.ring:
                for s, v in P.ring[name]["slots"]:
                    if v > 0:
                        eng.wait_ge(s, v)

        @block.tensor
        def _(eng):
            run(eng, "pe")

        @block.scalar
        def _(eng):
            run(eng, "act")

        @block.vector
        def _(eng):
            run(eng, "dve")

        @block.gpsimd
        def _(eng):
            run(eng, "pool")

        @block.sync
        def _(eng):
            run(eng, "sp")


class K:
    def __init__(self, cfg, stages=("ffn", "odd", "even")):
        self.c = cfg
        self.stages = stages
        self.nc = bass.Bass("TRN2", target_bir_lowering=False)
        self.ctx = []

    def sb(self, name, shape, dt=F32):
        cm = self.nc.sbuf_tensor(name, list(shape), dt)
        t = cm.__enter__()
        self.ctx.append(cm)
        return t

    def dram(self, name, shape, kind, dt=F32):
        return self.nc.dram_tensor(name, list(shape), dt, kind=kind).ap()

    def build(self):
        c, nc = self.c, self.nc
        D, KC, TP = c.D, c.KC, c.TP
        NMAX = TP + 64
        self.NMAX = NMAX
        I, O = "ExternalInput", "ExternalOutput"
        d = self.d = {}
        d["xp"] = self.dram("xp", [2 * TP, D], I)
        d["xs"] = self.dram("xs", [64, D], I)
        d["st_shift"] = self.dram("st_shift", [16, c.NEV, c.P_RW], I)
        d["st_rwkv"] = self.dram("st_rwkv", [16, c.NEV, c.NH, 64, 64], I)
        d["st_hgrn"] = self.dram("st_hgrn", [16, c.NEV, c.NHG, 128, 128], I)
        d["st_conv"] = self.dram("st_conv", [16, c.NOD, 3, D], I)
        d["st_lru"] = self.dram("st_lru", [16, c.NOD, D], I)
        L = c.DEPTH
        wshapes = dict(
            ffn1_norm=[L, D], ffn1_w_gu=[L, D, 2 * c.DFF], ffn1_w_down=[L, c.DFF, D], mix_norm=[L, D],
            ffn2_norm=[L, D], ffn2_w_gu=[L, D, 2 * c.DFF], ffn2_w_down=[L, c.DFF, D],
            ev_w_in=[c.NEV, D, c.P_EVEN], rw_mu=[c.NEV, c.P_RW], rw_w0=[c.NEV, c.RW],
            rw_w2=[c.NEV, 96, c.RW], rw_a0=[c.NEV, c.RW], rw_a2=[c.NEV, 96, c.RW], rw_g2=[c.NEV, 256, c.RW],
            rw_k_k=[c.NEV, c.RW], rw_k_a=[c.NEV, c.RW], rw_r_k=[c.NEV, c.NH, 64], rw_ln_w=[c.NEV, c.RW],
            rw_ln_b=[c.NEV, c.RW], hg_lb=[c.NEV, c.RW], hg_norm=[c.NEV, c.RW], ev_w_out=[c.NEV, D, D],
            od_w_in=[c.NOD, D, 2 * D], conv_w=[c.NOD, 4, D], conv_b=[c.NOD, D],
            lru_wa=[c.NOD, c.NLB, 256, 256], lru_ba=[c.NOD, D], lru_wx=[c.NOD, c.NLB, 256, 256],
            lru_bx=[c.NOD, D], lru_lambda=[c.NOD, D], od_w_out=[c.NOD, D, D], final_norm=[D])
        self.wshapes = wshapes
        for k, s in wshapes.items():
            d[k] = self.dram(k, s, I)
        d["yp"] = self.dram("yp", [2 * TP, D], O)
        d["ys"] = self.dram("ys", [64, D], O)
        d["o_shift_p"] = self.dram("o_shift_p", [c.NEV, c.P_RW], O)
        d["o_rwkv_p"] = self.dram("o_rwkv_p", [c.NEV, c.NH, 64, 64], O)
        d["o_hgrn_p"] = self.dram("o_hgrn_p", [c.NEV, c.NHG, 128, 128], O)
        d["o_conv_p"] = self.dram("o_conv_p", [c.NOD, 3, D], O)
        d["o_lru_p"] = self.dram("o_lru_p", [c.NOD, D], O)
        d["o_shift_s"] = self.dram("o_shift_s", [16, c.NEV, c.P_RW], O)
        d["o_rwkv_s"] = self.dram("o_rwkv_s", [16, c.NEV, c.NH, 64, 64], O)
        d["o_hgrn_s"] = self.dram("o_hgrn_s", [16, c.NEV, c.NHG, 128, 128], O)
        d["o_conv_s"] = self.dram("o_conv_s", [16, c.NOD, 3, D], O)
        d["o_lru_s"] = self.dram("o_lru_s", [16, c.NOD, D], O)

        self.x = self.sb("x", [128, KC * NMAX])
        self.xn = self.sb("xn", [128, KC * NMAX], BF16)
        self.bx = [Buf(f"x{k}") for k in range(KC)]
        self.bxn = Buf("xn")
        self.WS = 16 * 128
        self.wst = [self.sb(f"wst{i}", [128, self.WS]) for i in range(2)]
        self.wbf = [self.sb(f"wbf{i}", [128, self.WS], BF16) for i in range(2)]
        self.bwst = [Buf(f"wst{i}") for i in range(2)]
        self.bwbf = [Buf(f"wbf{i}") for i in range(2)]
        self.wi = 0
        self.wj = 0
        self.AR = 14 * 1024
        self.ar = self.sb("arena", [128, self.AR])
        self.live = []
        self.ident = self.sb("ident", [128, 128])
        self.ones = self.sb("ones", [128, 128])
        self.gains = self.sb("gains", [128, (3 * L + 1) * KC])
        self.cst = self.sb("cst", [128, 800])
        self.ccol = {}
        self.cnext = 0
        self.pers = self.sb("pers", [128, c.NOD * KC * 4])
        self.bpers = Buf("pers")
        self.rstd = self.ar[:, self.AR - NMAX:self.AR]
        self.sq = [self.ar[:, self.AR - (2 + i) * NMAX:self.AR - (1 + i) * NMAX] for i in range(2)]
        self.eps = self.sb("eps", [128, 4])
        self.bconst = Buf("const")
        self.ps = []
        for i in range(8):
            cm = nc.psum_tensor(f"ps{i}", [128, 512], F32)
            self.ps.append(cm.__enter__())
            self.ctx.append(cm)
        self.bps = [Buf(f"ps{i}", excl=True) for i in range(8)]
        self.pi = 0
        sems = []
        for i in range(56):
            cm = nc.semaphore(f"s{i}")
            sems.append(cm.__enter__())
            self.ctx.append(cm)
        self.P = Prog(nc, sems)
        self.setup()
        for ti in range(2):
            self.tile(ti)
        cmb = nc.Block()
        block = cmb.__enter__()
        self.P.emit(block)
        cmb.__exit__(None, None, None)
        for cm in reversed(self.ctx):
            cm.__exit__(None, None, None)
        return nc

    def aclaim(self, name, off, size):
        assert off + size <= self.AR, (name, off, size)
        nb = Buf(name)
        keep = []
        for (o, sz, b) in self.live:
            if o < off + size and off < o + sz:
                toks = list(b.r.values()) + ([b.w] if b.w is not None else [])
                for t in toks:
                    k = id(t[0])
                    if k not in nb.r or nb.r[k][1] < t[1]:
                        nb.r[k] = t
                if not (off <= o and o + sz <= off + size):
                    keep.append((o, sz, b))
            else:
                keep.append((o, sz, b))
        keep.append((off, size, nb))
        self.live = keep
        return self.ar[:, off:off + size], nb

    def psum(self):
        i = self.pi % 8
        self.pi += 1
        return self.ps[i], self.bps[i]

    def ntiles(self, n):
        out, t = [], 0
        while t < n:
            m = min(512, n - t)
            out.append((t, m))
            t += m
        return out

    def xv(self, kc, t0=0, m=None):
        m = self.n - t0 if m is None else m
        return self.x[:, kc * self.NMAX + t0: kc * self.NMAX + t0 + m]

    def xnv(self, kc, t0=0, m=None):
        m = self.n - t0 if m is None else m
        return self.xn[:, kc * self.NMAX + t0: kc * self.NMAX + t0 + m]

    def setup(self):
        P, c, d = self.P, self.c, self.d
        L, KC = c.DEPTH, c.KC
        bc = self.bconst
        P.op("pool", lambda e: e.memset(self.ones[:], 1.0), writes=[bc])
        P.op("pool", lambda e: e.memset(self.cst[:], 0.0), writes=[bc])
        P.op("pool", lambda e: e.memset(self.eps[:, 0:1], 1e-6), writes=[bc])
        P.op("pool", lambda e: e.memset(self.eps[:, 1:2], 1.0), writes=[bc])
        P.op("pool", lambda e: e.affine_select(out=self.ident[:], in_=self.ones[:], pattern=[[1, 128]],
                                                compare_op=ALU.is_equal, fill=0.0, base=0, channel_multiplier=-1),
             reads=[bc], writes=[bc])
        names = [f"ffn1_norm{l}" for l in range(L)] + [f"mix_norm{l}" for l in range(L)] + \
                [f"ffn2_norm{l}" for l in range(L)] + ["final_norm"]
        self.gidx = {nm: i for i, nm in enumerate(names)}
        for i, nm in enumerate(names):
            if "nogain" in self.stages:
                break
            src = d["final_norm"] if nm == "final_norm" else d[nm[:-1]][int(nm[-1])]
            P.dma("sp", self.gains[:, i * KC:(i + 1) * KC], src.rearrange("(k p) -> p k", p=128),
                  writes=[bc], allow_slow_non_contiguous=True)
        if "odd" in self.stages:
            self.setup_odd()
        if "even" in self.stages:
            self.setup_even()

    def cload(self, key, src, rows, ncols):
        c0 = self.cnext
        self.cnext += ncols
        assert self.cnext <= 800
        self.ccol[key] = c0
        dst = self.cst[0:rows, c0:c0 + ncols]
        if len(src.shape) == 3:
            dst = dst.rearrange("p (k j) -> p k j", j=src.shape[2])
        self.P.dma("sp", dst, src, writes=[self.bconst], allow_slow_non_contiguous=True)
        return c0

    def cs(self, key, i=0, rows=128):
        c0 = self.ccol[key] + i
        return self.cst[0:rows, c0:c0 + 1]

    def setup_odd(self):
        P, c, d = self.P, self.c, self.d
        KC = c.KC
        bc = self.bconst
        for o in range(c.NOD):
            for j in range(4):
                c0 = self.cload(f"convw{o}_{j}", d["conv_w"][o, j].rearrange("(k p) -> p k", p=128), 128, KC)
            for nm in ("conv_b", "lru_ba", "lru_bx", "lru_lambda"):
                self.cload(f"{nm}{o}", d[nm][o].rearrange("(k p) -> p k", p=128), 128, KC)
            lam = self.cst[:, self.ccol[f"lru_lambda{o}"]: self.ccol[f"lru_lambda{o}"] + KC]
            c0 = self.cnext
            self.cnext += 6 * KC
            t = [self.cst[:, c0 + i * KC: c0 + (i + 1) * KC] for i in range(6)]
            self.ccol[f"csp{o}"] = c0 + 4 * KC
            self.ccol[f"csp2{o}"] = c0 + 5 * KC
            op = lambda eng, fn: P.op(eng, fn, reads=[bc], writes=[bc])
            op("act", lambda e, t=t, lam=lam: e.activation(out=t[0], in_=lam, func=AF.Abs))
            op("act", lambda e, t=t, lam=lam: e.activation(out=t[0], in_=t[0], func=AF.Exp, scale=-1.0))
            op("dve", lambda e, t=t, lam=lam: e.tensor_scalar(out=t[1], in0=t[0], scalar1=2.0, scalar2=None, op0=ALU.add))
            op("dve", lambda e, t=t, lam=lam: e.reciprocal(out=t[1], in_=t[1]))
            op("dve", lambda e, t=t, lam=lam: e.tensor_tensor(out=t[1], in0=t[1], in1=t[0], op=ALU.mult))
            op("dve", lambda e, t=t, lam=lam: e.tensor_tensor(out=t[2], in0=t[1], in1=t[1], op=ALU.mult))
            op("dve", lambda e, t=t, lam=lam: e.tensor_scalar(out=t[3], in0=t[2], scalar1=1.0 / 9, scalar2=1.0 / 7, op0=ALU.mult, op1=ALU.add))
            op("dve", lambda e, t=t, lam=lam: e.tensor_tensor(out=t[3], in0=t[3], in1=t[2], op=ALU.mult))
            op("dve", lambda e, t=t, lam=lam: e.tensor_scalar(out=t[3], in0=t[3], scalar1=1.0 / 5, scalar2=None, op0=ALU.add))
            op("dve", lambda e, t=t, lam=lam: e.tensor_tensor(out=t[3], in0=t[3], in1=t[2], op=ALU.mult))
            op("dve", lambda e, t=t, lam=lam: e.tensor_scalar(out=t[3], in0=t[3], scalar1=1.0 / 3, scalar2=None, op0=ALU.add))
            op("dve", lambda e, t=t, lam=lam: e.tensor_tensor(out=t[3], in0=t[3], in1=t[2], op=ALU.mult))
            op("dve", lambda e, t=t, lam=lam: e.tensor_scalar(out=t[3], in0=t[3], scalar1=1.0, scalar2=None, op0=ALU.add))
            op("dve", lambda e, t=t, lam=lam: e.tensor_tensor(out=t[3], in0=t[3], in1=t[1], op=ALU.mult))
            op("dve", lambda e, t=t, lam=lam: e.tensor_scalar(out=t[2], in0=lam, scalar1=-1.0, scalar2=0.0, op0=ALU.mult, op1=ALU.max))
            op("dve", lambda e, t=t, lam=lam: e.scalar_tensor_tensor(out=t[3], in0=t[3], scalar=2.0, in1=t[2], op0=ALU.mult, op1=ALU.add))
            op("dve", lambda e, t=t, lam=lam: e.tensor_scalar(out=t[4], in0=t[3], scalar1=-8.0, scalar2=None, op0=ALU.mult))
            op("dve", lambda e, t=t, lam=lam: e.tensor_scalar(out=t[5], in0=t[3], scalar1=-16.0, scalar2=None, op0=ALU.mult))
        P.op("pool", lambda e: e.memset(self.pers[:], 0.0), writes=[self.bpers])

    def setup_even(self):
        P, c, d = self.P, self.c, self.d
        KC, RW, NHP, NHG, NEV = c.KC, c.RW, c.NHP, c.NHG, c.NEV
        bc = self.bconst
        segs = []
        for hp in range(NHP):
            segs += [(f"r{hp}", hp * 128, 128), (f"k{hp}", RW + hp * 128, 128), (f"v{hp}", 2 * RW + hp * 128, 128)]
        segs += [("zw", 3 * RW, 96), ("za", 3 * RW + 96, 96), ("zg0", 3 * RW + 192, 128), ("zg1", 3 * RW + 320, 128)]
        self.segs = {nm: (i, c0, m) for i, (nm, c0, m) in enumerate(segs)}
        NS = len(segs)
        self.NSEG = NS
        for e in range(NEV):
            mu0 = self.cnext
            for (nm, c0, m) in segs:
                self.cload(f"mu{e}_{nm}", d["rw_mu"][e, c0:c0 + m].rearrange("(p o) -> p o", o=1), m, 1)
            om0 = self.cnext
            self.cnext += NS
            self.ccol[f"mu{e}"] = mu0
            self.ccol[f"om{e}"] = om0
            P.op("dve", lambda en, mu0=mu0, om0=om0: en.tensor_scalar(
                out=self.cst[:, om0:om0 + NS], in0=self.cst[:, mu0:mu0 + NS], scalar1=-1.0, scalar2=1.0,
                op0=ALU.mult, op1=ALU.add), reads=[bc], writes=[bc])
            for nm in ("rw_w0", "rw_a0", "rw_k_k", "rw_k_a", "rw_ln_w", "rw_ln_b"):
                self.cload(f"{nm}{e}", d[nm][e].rearrange("(k p) -> p k", p=128), 128, NHP)
            self.cload(f"rw_r_k{e}", d["rw_r_k"][e].rearrange("(a h) k -> (h k) a", h=2), 128, NHP)
            self.cload(f"hg_norm{e}", d["hg_norm"][e].rearrange("(k p) -> p k", p=128), 128, NHG)
            self.cload(f"hg_lbraw{e}", d["hg_lb"][e].rearrange("(k p) -> p k", p=128), 128, NHG)
        col = lambda key: self.cst[:, self.ccol[key]:self.ccol[key] + NHG]
        t0 = self.cnext
        self.cnext += (3 * NEV + 1) * NHG
        ex = [self.cst[:, t0 + i * NHG:t0 + (i + 1) * NHG] for i in range(NEV)]
        tot = self.cst[:, t0 + NEV * NHG:t0 + (NEV + 1) * NHG]
        for e in range(NEV):
            self.ccol[f"lb{e}"] = t0 + (NEV + 1 + e) * NHG
            self.ccol[f"oml{e}"] = t0 + (2 * NEV + 1 + e) * NHG
            P.op("act", lambda en, e=e: en.activation(out=ex[e], in_=col(f"hg_lbraw{e}"), func=AF.Exp), reads=[bc], writes=[bc])
        P.op("dve", lambda en: en.tensor_copy(out=tot, in_=ex[0]), reads=[bc], writes=[bc])
        for e in range(1, NEV):
            P.op("dve", lambda en, e=e: en.tensor_tensor(out=tot, in0=tot, in1=ex[e], op=ALU.add), reads=[bc], writes=[bc])
        P.op("dve", lambda en: en.reciprocal(out=tot, in_=tot), reads=[bc], writes=[bc])
        P.op("pool", lambda en: en.memset(col("lb0"), 0.0), reads=[bc], writes=[bc])
        for e in range(1, NEV):
            P.op("dve", lambda en, e=e: en.tensor_tensor(out=ex[e], in0=ex[e], in1=tot, op=ALU.mult), reads=[bc], writes=[bc])
            P.op("dve", lambda en, e=e: en.tensor_tensor(out=col(f"lb{e}"), in0=col(f"lb{e - 1}"), in1=ex[e], op=ALU.add),
                 reads=[bc], writes=[bc])
        for e in range(NEV):
            P.op("dve", lambda en, e=e: en.tensor_scalar(out=col(f"oml{e}"), in0=col(f"lb{e}"), scalar1=-1.0, scalar2=1.0,
                                                         op0=ALU.mult, op1=ALU.add), reads=[bc], writes=[bc])
        self.bones = self.sb("bones", [128, 128])
        self.msu = self.sb("msu", [64, 256])
        self.mui = self.sb("mui", [64, 256])
        self.msl = self.sb("msl", [64, 256])
        self.msu_s = self.sb("msu_s", [64, 128])
        self.mui_s = self.sb("mui_s", [64, 128])
        self.msl_s = self.sb("msl_s", [64, 128])
        self.id4 = self.sb("id4", [64, 256])
        self.rowm = self.sb("rowm", [64, 16])
        self.mrst = self.sb("mrst", [128, 128])
        self.mrst_s = self.sb("mrst_s", [128, 64])
        self.eps2 = self.sb("eps2", [128, 2])
        self.prevp = self.sb("prevp", [128, NEV * NS])
        self.bprev = Buf("prevp")
        self.ST = self.sb("ST", [128, NEV * NHP * 64])
        self.bST = Buf("ST")
        self.SH = self.sb("SHg", [128, NEV * NHG * 128])
        self.bSH = Buf("SH")
        op = lambda eng, fn: P.op(eng, fn, reads=[bc], writes=[bc])
        op("pool", lambda en: en.memset(self.bones[:], 0.0))
        op("pool", lambda en: en.memset(self.bones[0:64, 0:64], 1.0))
        op("pool", lambda en: en.memset(self.bones[64:128, 64:128], 1.0))
        op("pool", lambda en: en.memset(self.eps2[:, 0:1], 64e-5))
        op("pool", lambda en: en.memset(self.eps2[:, 1:2], 1e-5))
        op("pool", lambda en: en.memset(self.prevp[:], 0.0))
        P.op("pool", lambda en: en.memset(self.ST[:], 0.0), writes=[self.bST])
        P.op("pool", lambda en: en.memset(self.SH[:], 0.0), writes=[self.bSH])
        on64 = self.ones[0:64, 0:64]

        def sel(out, in_, pattern, cmp, base, cm):
            op("pool", lambda en: en.affine_select(out=out, in_=in_, pattern=pattern, compare_op=cmp, fill=0.0,
                                                   base=base, channel_multiplier=cm))
        for (t, nsl) in ((self.msu, 4), (self.mui, 4), (self.msl, 4), (self.id4, 4), (self.msu_s, 2), (self.mui_s, 2), (self.msl_s, 2)):
            op("pool", lambda en, t=t: en.memset(t[:], 1.0))
        for (t, nsl) in ((self.msu, 4), (self.msu_s, 2)):
            sel(t[:], t[:], [[0, nsl], [1, 64]], ALU.is_gt, 0, -1)
        for (t, nsl) in ((self.mui, 4), (self.mui_s, 2)):
            sel(t[:], t[:], [[0, nsl], [1, 64]], ALU.is_ge, 0, -1)
        for (t, nsl) in ((self.msl, 4), (self.msl_s, 2)):
            sel(t[:], t[:], [[0, nsl], [-1, 64]], ALU.is_gt, 0, 1)
        sel(self.id4[:], self.id4[:], [[0, 4], [1, 64]], ALU.is_equal, 0, -1)
        for t in (self.msu_s, self.mui_s, self.msl_s):
            sel(t[:], t[:], [[0, 2], [-4, 16], [0, 4]], ALU.is_ge, 0, 1)
            sel(t[:], t[:], [[0, 2], [4, 16], [0, 4]], ALU.is_ge, 3, -1)
        op("pool", lambda en: en.memset(self.rowm[:], 1.0))
        sel(self.rowm[:], self.rowm[:], [[-4, 16]], ALU.is_ge, 0, 1)
        sel(self.rowm[:], self.rowm[:], [[4, 16]], ALU.is_ge, 3, -1)
        op("pool", lambda en: en.memset(self.mrst[:], 1.0))
        op("pool", lambda en: en.memset(self.mrst[:].rearrange("p (c t) -> p c t", t=64)[:, :, 0:1], 0.0))
        op("pool", lambda en: en.memset(self.mrst_s[:], 1.0))
        op("pool", lambda en: en.memset(self.mrst_s[:].rearrange("p (c t) -> p c t", t=4)[:, :, 0:1], 0.0))

    def gain(self, nm, kc):
        i = self.gidx[nm]
        return self.gains[:, i * self.c.KC + kc: i * self.c.KC + kc + 1]

    def wload(self, src, rows, kcn, cols, cast=True, pool_cast=False):
        P = self.P
        i = self.wi % 2
        self.wi += 1
        if not cast:
            st, bst = self.wst[i], self.bwst[i]
            dst = st[0:rows, 0:kcn * cols].rearrange("p (k c) -> p k c", c=cols)
            P.dma("sp", dst, src, writes=[bst])
            return st, bst
        j = self.wj % 2
        self.wj += 1
        st, bf, bst, bbf = self.wst[i], self.wbf[j], self.bwst[i], self.bwbf[j]
        n = kcn * cols
        assert n <= self.WS
        dst = st[0:rows, 0:n].rearrange("p (k c) -> p k c", c=cols)
        P.dma("sp", dst, src, writes=[bst])
        if pool_cast:
            P.op("pool", lambda e: e.tensor_copy(out=bf[0:rows, 0:n], in_=st[0:rows, 0:n]), reads=[bst], writes=[bbf])
            return bf, bbf
        a = (n // 2) // 2 * 2
        b = a + ((n - a) // 2) // 2 * 2
        P.op("pool", lambda e: e.tensor_copy(out=bf[0:rows, 0:a], in_=st[0:rows, 0:a]), reads=[bst], writes=[bbf])
        P.op("act", lambda e: e.copy(out=bf[0:rows, a:b], in_=st[0:rows, a:b]), reads=[bst], writes=[bbf])
        P.op("dve", lambda e: e.tensor_copy(out=bf[0:rows, b:n], in_=st[0:rows, b:n]), reads=[bst], writes=[bbf])
        return bf, bbf

    def rms_stats(self):
        P, c = self.P, self.c
        n, KC = self.n, c.KC
        NM = self.NMAX
        self.brstd = self.aclaim("rstd", self.AR - NM, NM)[1]
        self.bsq = [self.aclaim(f"sq{i}", self.AR - (2 + i) * NM, NM)[1] for i in range(2)]
        nts = self.ntiles(n)
        pss = [self.psum() for _ in nts]
        for kc in range(KC):
            sq, bsq = self.sq[kc % 2], self.bsq[kc % 2]
            P.op("act", lambda e, sq=sq, xin=self.xv(kc), n=n: e.activation(out=sq[:, 0:n], in_=xin, func=AF.Square),
                 reads=[self.bx[kc]], writes=[bsq])
            for (t0, m), (ps, bps) in zip(nts, pss):
                P.op("pe", lambda e, sq=sq, ps=ps, t0=t0, m=m, kc=kc: e.matmul(
                    ps[:, 0:m], lhsT=self.ones[:], rhs=sq[:, t0:t0 + m], start=(kc == 0), stop=(kc == KC - 1)),
                    reads=[bsq, self.bconst], writes=[bps])
        for (t0, m), (ps, bps) in zip(nts, pss):
            P.op("act", lambda e, ps=ps, t0=t0, m=m: e.activation(
                out=self.rstd[:, t0:t0 + m], in_=ps[:, 0:m], func=AF.Ln, scale=1.0 / c.D, bias=self.eps[:, 0:1]),
                reads=[bps, self.bconst], writes=[self.brstd])
        P.op("act", lambda e, n=n: e.activation(out=self.rstd[:, 0:n], in_=self.rstd[:, 0:n], func=AF.Exp, scale=-0.5),
             reads=[self.brstd], writes=[self.brstd])

    def rmsnorm(self, gname):
        P, c = self.P, self.c
        self.rms_stats()
        for kc in range(c.KC):
            P.op("dve", lambda e, o=self.xnv(kc), i0=self.xv(kc), g=self.gain(gname, kc), r=self.rstd[:, 0:self.n]:
                 e.scalar_tensor_tensor(out=o, in0=i0, scalar=g, in1=r, op0=ALU.mult, op1=ALU.mult),
                 reads=[self.bx[kc], self.brstd, self.bconst], writes=[self.bxn])

    def load_x(self, ti):
        P, c, d = self.P, self.c, self.d
        KC, TP, D = c.KC, c.TP, c.D
        blocks = [(d["xp"], ti * TP + t, t, 128) for t in range(0, TP, 128)]
        if ti == 0:
            blocks.append((d["xs"], 0, TP, 64))
        for src, r0, t0, m in blocks:
            for h0 in range(0, D, self.WS):
                i = self.wi % 2
                self.wi += 1
                st, bst = self.wst[i], self.bwst[i]
                w = min(self.WS, D - h0)
                P.dma("sp", st[0:m, 0:w], src[r0:r0 + m, h0:h0 + w], writes=[bst])
                for k0 in range(0, w // 128, 4):
                    ps, bps = self.psum()
                    g = min(4, w // 128 - k0)
                    for j in range(g):
                        P.op("pe", lambda e, ps=ps, st=st, m=m, j=j, k0=k0: e.transpose(
                            out=ps[:, j * 128:j * 128 + m], in_=st[0:m, (k0 + j) * 128:(k0 + j + 1) * 128],
                            identity=self.ident[0:m, 0:m]), reads=[bst, self.bconst], writes=[bps])
                    for j in range(g):
                        kc = h0 // 128 + k0 + j
                        eng = "act" if (self.pi % 2 == 0) else "dve"
                        if eng == "act":
                            P.op("act", lambda e, ps=ps, o=self.xv(kc, t0, m), j=j, m=m: e.copy(
                                out=o, in_=ps[:, j * 128:j * 128 + m]), reads=[bps], writes=[self.bx[kc]])
                        else:
                            P.op("dve", lambda e, ps=ps, o=self.xv(kc, t0, m), j=j, m=m: e.tensor_copy(
                                out=o, in_=ps[:, j * 128:j * 128 + m]), reads=[bps], writes=[self.bx[kc]])

    def store_y(self, ti):
        P, c, d = self.P, self.c, self.d
        KC, TP, D = c.KC, c.TP, c.D
        dbg = "dbg1" in self.stages
        if not dbg:
            self.rms_stats()
        blocks = [(d["yp"], ti * TP + t, t, 128) for t in range(0, TP, 128)]
        if ti == 0:
            blocks.append((d["ys"], 0, TP, 64))
        tmp, btmp = self.aclaim("ytmp", 0, 512)
        for src, r0, t0, m in blocks:
            for h0 in range(0, D, self.WS):
                i = self.wi % 2
                self.wi += 1
                st, bst = self.wst[i], self.bwst[i]
                w = min(self.WS, D - h0)
                for k0 in range(0, w // 128, 4):
                    g = min(4, w // 128 - k0)
                    ps, bps = self.psum()
                    for j in range(g):
                        kc = h0 // 128 + k0 + j
                        if dbg:
                            P.op("dve", lambda e, i0=self.xv(kc, t0, m), j=j, m=m: e.tensor_copy(
                                out=tmp[:, j * 128:j * 128 + m], in_=i0), reads=[self.bx[kc]], writes=[btmp])
                            continue
                        P.op("dve", lambda e, i0=self.xv(kc, t0, m), g=self.gain("final_norm", kc), j=j, t0=t0, m=m: e.scalar_tensor_tensor(
                            out=tmp[:, j * 128:j * 128 + m], in0=i0, scalar=g,
                            in1=self.rstd[:, t0:t0 + m], op0=ALU.mult, op1=ALU.mult),
                            reads=[self.bx[kc], self.brstd, self.bconst], writes=[btmp])
                    for j in range(g):
                        P.op("pe", lambda e, ps=ps, j=j, m=m: e.transpose(
                            out=ps[0:m, j * 128:(j + 1) * 128], in_=tmp[:, j * 128:j * 128 + m], identity=self.ident[:]),
                            reads=[btmp, self.bconst], writes=[bps])
                    P.op("act", lambda e, ps=ps, st=st, k0=k0, g=g, m=m: e.copy(
                        out=st[0:m, k0 * 128:(k0 + g) * 128], in_=ps[0:m, 0:g * 128]), reads=[bps], writes=[bst])
                P.dma("pool", src[r0:r0 + m, h0:h0 + w], st[0:m, 0:w], reads=[bst])

    def ffn(self, l, which):
        P, c, d = self.P, self.c, self.d
        KC, FC, D, DFF = c.KC, c.FC, c.D, c.DFF
        n = self.n
        nts = self.ntiles(n)
        wgu = d[f"ffn{which}_w_gu"][l]
        wdn = d[f"ffn{which}_w_down"][l]
        self.rmsnorm(f"ffn{which}_norm{l}")
        GS = 8
        hN = self.NMAX
        self.live = [(o, z, b) for (o, z, b) in self.live]
        h = self.ar.bitcast(BF16)
        bh = [self.aclaim(f"h{f}", f * hN // 2, hN // 2)[1] for f in range(GS)]
        sg, bsg = self.aclaim("sg", GS * hN // 2, 512)
        jobs = []
        for f0 in range(0, FC, GS):
            gs = min(GS, FC - f0)
            for f in range(gs):
                jobs += [("g", f0, gs, f), ("u", f0, gs, f)]
            DW = max(128, (self.WS // gs) // 128 * 128)
            for d0 in range(0, D, DW):
                jobs.append(("d", f0, gs, (d0, min(DW, D - d0))))
        hnd = {}

        def fetch(i):
            kind, f0, gs, x = jobs[i]
            if kind == "d":
                d0, dw = x
                hnd[i] = self.wload(wdn[f0 * 128:(f0 + gs) * 128, d0:d0 + dw].rearrange("(k p) c -> p k c", p=128), 128, gs, dw)
            else:
                c0 = (f0 + x) * 128 + (DFF if kind == "u" else 0)
                hnd[i] = self.wload(wgu[:, c0:c0 + 128].rearrange("(k p) c -> p k c", p=128), 128, KC, 128)
        fetch(0)
        gp = None
        for i, (kind, f0, gs, x) in enumerate(jobs):
            if i + 1 < len(jobs):
                fetch(i + 1)
            w, bw = hnd.pop(i)
            if kind in ("g", "u"):
                f = x
                pss = []
                for (t0, m) in nts:
                    ps, bps = self.psum()
                    for kc in range(KC):
                        P.op("pe", lambda e, ps=ps, w=w, kc=kc, r=self.xnv(kc, t0, m), m=m: e.matmul(
                            ps[:, 0:m], lhsT=w[:, kc * 128:(kc + 1) * 128], rhs=r, start=(kc == 0), stop=(kc == KC - 1)),
                            reads=[bw, self.bxn], writes=[bps])
                    pss.append((ps, bps))
                    if kind == "u":
                        psg, bpg = gp[len(pss) - 1]
                        P.op("act", lambda e, psg=psg, m=m: e.activation(out=sg[:, 0:m], in_=psg[:, 0:m], func=AF.Silu),
                             reads=[bpg], writes=[bsg])
                        P.op("dve", lambda e, ps=ps, f=f, t0=t0, m=m: e.tensor_tensor(
                            out=h[:, f * hN + t0: f * hN + t0 + m], in0=ps[:, 0:m], in1=sg[:, 0:m], op=ALU.mult),
                            reads=[bps, bsg], writes=[bh[f]])
                if kind == "g":
                    gp = pss
            else:
                d0, dw = x
                wd, bwd = w, bw
                for dj in range(dw // 128):
                    kc = d0 // 128 + dj
                    for (t0, m) in nts:
                        ps, bps = self.psum()
                        for f in range(gs):
                            P.op("pe", lambda e, ps=ps, wd=wd, f=f, dj=dj, dw=dw, t0=t0, m=m: e.matmul(
                                ps[:, 0:m], lhsT=wd[:, f * dw + dj * 128: f * dw + (dj + 1) * 128],
                                rhs=h[:, f * hN + t0: f * hN + t0 + m], start=(f == 0), stop=(f == gs - 1)),
                                reads=[bwd, bh[f]], writes=[bps])
                        P.op("dve", lambda e, ps=ps, xo=self.xv(kc, t0, m), m=m: e.scalar_tensor_tensor(
                            out=xo, in0=ps[:, 0:m], scalar=0.5, in1=xo,
                            op0=ALU.mult, op1=ALU.add), reads=[bps, self.bx[kc]], writes=[self.bx[kc]])

    def outproj(self, wout, ch, ob, bob, c0=0, ncols=None):
        self.outproj_multi(wout, [(ch, ob, bob)], c0, ncols)

    def outproj_multi(self, wout, items, c0=0, ncols=None):
        P, c = self.P, self.c
        D, KC = c.D, c.KC
        assert len(items) <= 2
        ncols = self.n if ncols is None else ncols
        nts = self.ntiles(ncols)
        for d0 in range(0, D, self.WS):
            dw = min(self.WS, D - d0)
            ws = [self.wload(wout[ch * 128:(ch + 1) * 128, d0:d0 + dw].rearrange("(k p) c -> p k c", p=128), 128, 1, dw,
                             pool_cast=True) for (ch, _, _) in items]
            for dj in range(dw // 128):
                kc = d0 // 128 + dj
                for (t0, mm) in nts:
                    ps, bps = self.psum()
                    for i, ((w, bw), (ch, ob, bob)) in enumerate(zip(ws, items)):
                        P.op("pe", lambda e, ps=ps, w=w, ob=ob, dj=dj, t0=t0, mm=mm, i=i: e.matmul(
                            ps[:, 0:mm], lhsT=w[:, dj * 128:(dj + 1) * 128], rhs=ob[:, t0:t0 + mm], start=(i == 0),
                            stop=(i == len(items) - 1)), reads=[bw, bob], writes=[bps])
                    P.op("dve", lambda e, ps=ps, xo=self.xv(kc, c0 + t0, mm), mm=mm: e.tensor_tensor(
                        out=xo, in0=ps[:, 0:mm], in1=xo, op=ALU.add), reads=[bps, self.bx[kc]], writes=[self.bx[kc]])

    def proj(self, wsrc, m, evac):
        P, c = self.P, self.c
        KC = c.KC
        w, bw = self.wload(wsrc.rearrange("(k p) c -> p k c", p=128), 128, KC, m, pool_cast=True)
        for (t0, mm) in self.ntiles(self.n):
            ps, bps = self.psum()
            for kc in range(KC):
                P.op("pe", lambda e, ps=ps, w=w, kc=kc, r=self.xnv(kc, t0, mm), mm=mm: e.matmul(
                    ps[0:m, 0:mm], lhsT=w[:, kc * m:(kc + 1) * m], rhs=r, start=(kc == 0), stop=(kc == KC - 1)),
                    reads=[bw, self.bxn], writes=[bps])
            evac(ps, bps, t0, mm)

    def odd(self, l):
        P, c, d = self.P, self.c, self.d
        o = l // 2
        KC, TP, D = c.KC, c.TP, c.D
        n, ti, NM = self.n, self.ti, self.NMAX
        S = (ti == 0)
        nts = self.ntiles(n)
        win, wout = d["od_w_in"][o], d["od_w_out"][o]
        self.rmsnorm(f"mix_norm{l}")
        off = [0]

        def A(name, size):
            ap, b = self.aclaim(name, off[0], size)
            off[0] += size
            return ap, b
        XE = [A(f"XE{i}", TP + 4) for i in range(2)]
        XS = [A(f"XS{i}", 112) for i in range(2)]
        XC = [A(f"XC{i}", NM) for i in range(2)]
        Aa, bA = A("A", NM)
        T1, bT = A("T1", NM)
        H, bH = A("H", NM)
        GL, bG = A("GL", NM)
        obs = []
        for i in range(2):
            obf_, bob_ = A(f"ob{i}", NM // 2)
            obs.append((obf_.bitcast(BF16), bob_))
        H0, bH0 = A("H0", 16)
        SO, bSO = A("SO", 256)
        pcol = lambda ch, j: self.pers[:, (o * KC + ch) * 4 + j:(o * KC + ch) * 4 + j + 1]
        ptail = lambda ch: self.pers[:, (o * KC + ch) * 4:(o * KC + ch) * 4 + 3]
        if S:
            si = self.wi % 2
            self.wi += 1
            sst, bsst = self.wst[si], self.bwst[si]
            for j in range(3):
                P.dma("sp", sst[j * 16:(j + 1) * 16, 0:D], d["st_conv"][:, o, j, :], writes=[bsst])
            P.dma("sp", sst[48:64, 0:D], d["st_lru"][:, o, :], writes=[bsst])
            SST, bSST = A("SST", KC * 64)
            for k0 in range(0, KC, 8):
                g = min(8, KC - k0)
                ps, bps = self.psum()
                for j in range(g):
                    P.op("pe", lambda e, ps=ps, j=j, k0=k0: e.transpose(
                        out=ps[:, j * 64:(j + 1) * 64], in_=sst[0:64, (k0 + j) * 128:(k0 + j + 1) * 128],
                        identity=self.ident[0:64, 0:64]), reads=[bsst, self.bconst], writes=[bps])
                P.op("act", lambda e, ps=ps, k0=k0, g=g: e.copy(out=SST[:, k0 * 64:(k0 + g) * 64], in_=ps[:, 0:g * 64]),
                     reads=[bps], writes=[bSST])
        for b in range(c.NLB):
            for cc in range(2):
                ch = 2 * b + cc
                xe, bxe = XE[cc]
                xs, bxs = XS[cc]
                xc, bxc = XC[cc]
                if S:
                    P.op("act", lambda e, xs=xs, ch=ch: e.copy(out=xs[:, 0:48], in_=SST[:, ch * 64:ch * 64 + 48]),
                         reads=[bSST], writes=[bxs])
                    P.op("pool", lambda e, xe=xe: e.memset(xe[:, 0:3], 0.0), writes=[bxe])
                else:
                    P.op("pool", lambda e, xe=xe, ch=ch: e.tensor_copy(out=xe[:, 0:3], in_=ptail(ch)), reads=[self.bpers], writes=[bxe])

                def evac(ps, bps, t0, mm, xe=xe, bxe=bxe, xs=xs, bxs=bxs):
                    pe = min(t0 + mm, TP)
                    if pe > t0:
                        P.op("act", lambda e: e.copy(out=xe[:, 3 + t0:3 + pe], in_=ps[:, 0:pe - t0]), reads=[bps], writes=[bxe])
                    if t0 + mm > TP:
                        a0 = max(t0, TP) - t0
                        P.op("act", lambda e: e.copy(
                            out=xs[:, 48:112].rearrange("p (t s) -> p s t", s=16),
                            in_=ps[:, a0:a0 + 64].rearrange("p (s t) -> p s t", t=4)), reads=[bps], writes=[bxs])
                self.proj(win[:, D + ch * 128:D + (ch + 1) * 128], 128, evac)
                cw = lambda j, ch=ch: self.cs(f"convw{o}_{j}", ch)
                cb = self.cs(f"conv_b{o}", ch)
                P.op("dve", lambda e, xe=xe, xc=xc, ch=ch, cb=cb: e.tensor_scalar(
                    out=xc[:, 0:TP], in0=xe[:, 3:3 + TP], scalar1=cw(3, ch), scalar2=cb, op0=ALU.mult, op1=ALU.add),
                    reads=[bxe, self.bconst], writes=[bxc])
                for j in range(3):
                    P.op("dve", lambda e, xe=xe, xc=xc, ch=ch, j=j: e.scalar_tensor_tensor(
                        out=xc[:, 0:TP], in0=xe[:, j:j + TP], scalar=cw(j, ch), in1=xc[:, 0:TP], op0=ALU.mult, op1=ALU.add),
                        reads=[bxe, bxc, self.bconst], writes=[bxc])
                if S:
                    P.op("dve", lambda e, xs=xs, xc=xc, ch=ch, cb=cb: e.tensor_scalar(
                        out=xc[:, TP:TP + 64], in0=xs[:, 48:112], scalar1=cw(3, ch), scalar2=cb, op0=ALU.mult, op1=ALU.add),
                        reads=[bxs, self.bconst], writes=[bxc])
                    for j in range(3):
                        P.op("dve", lambda e, xs=xs, xc=xc, ch=ch, j=j: e.scalar_tensor_tensor(
                            out=xc[:, TP:TP + 64], in0=xs[:, j * 16:j * 16 + 64], scalar=cw(j, ch), in1=xc[:, TP:TP + 64],
                            op0=ALU.mult, op1=ALU.add), reads=[bxs, bxc, self.bconst], writes=[bxc])
                if S:
                    P.op("pool", lambda e, xe=xe, ch=ch: e.tensor_copy(out=ptail(ch), in_=xe[:, TP:TP + 3]),
                         reads=[bxe], writes=[self.bpers])
                    ps, bps = self.psum()
                    P.op("pe", lambda e, ps=ps, xs=xs: e.transpose(out=ps[0:112, 0:128], in_=xs[:, 0:112], identity=self.ident[:]),
                         reads=[bxs, self.bconst], writes=[bps])
                    so, bso = self.aclaim(f"cso{cc}", off[0] + cc * 128, 128)
                    P.op("act", lambda e, ps=ps, so=so: e.copy(out=so[0:112, :], in_=ps[0:112, 0:128]), reads=[bps], writes=[bso])
                    for j in range(3):
                        P.dma("pool", d["o_conv_s"][:, o, j, ch * 128:(ch + 1) * 128], so[64 + 16 * j:80 + 16 * j, :], reads=[bso])
                else:
                    P.dma("pool", d["o_conv_p"][o, :, ch * 128:(ch + 1) * 128].rearrange("j p -> p j"), xe[:, TP:TP + 3],
                          reads=[bxe], allow_slow_non_contiguous=True)
            for cc in range(2):
                ch = 2 * b + cc
                xc, bxc = XC[cc]
                wa, bwa = self.wload(d["lru_wa"][o, b].rearrange("(j p) k -> p j k", p=128), 128, 2, 256, cast=False)
                wx, bwx = self.wload(d["lru_wx"][o, b].rearrange("(j p) k -> p j k", p=128), 128, 2, 256, cast=False)
                for (t0, mm) in nts:
                    for (wm, bwm, dst, bd, bias) in ((wa, bwa, Aa, bA, self.cs(f"lru_ba{o}", ch)), (wx, bwx, T1, bT, self.cs(f"lru_bx{o}", ch))):
                        ps, bps = self.psum()
                        for j in range(2):
                            P.op("pe", lambda e, ps=ps, wm=wm, j=j, cc=cc, t0=t0, mm=mm: e.matmul(
                                ps[:, 0:mm], lhsT=wm[:, j * 256 + cc * 128:j * 256 + (cc + 1) * 128],
                                rhs=XC[j][0][:, t0:t0 + mm], start=(j == 0), stop=(j == 1)),
                                reads=[bwm, XC[j][1]], writes=[bps])
                        P.op("act", lambda e, ps=ps, dst=dst, bias=bias, t0=t0, mm=mm: e.activation(
                            out=dst[:, t0:t0 + mm], in_=ps[:, 0:mm], func=AF.Sigmoid, bias=bias),
                            reads=[bps, self.bconst], writes=[bd])
                csp, csp2 = self.cs(f"csp{o}", ch), self.cs(f"csp2{o}", ch)
                P.op("act", lambda e, csp2=csp2: e.activation(out=H[:, 0:n], in_=Aa[:, 0:n], func=AF.Exp, scale=csp2),
                     reads=[bA, self.bconst], writes=[bH])
                P.op("act", lambda e: e.activation(out=H[:, 0:n], in_=H[:, 0:n], func=AF.Sqrt, scale=-1.0, bias=self.eps[:, 1:2]),
                     reads=[bH, self.bconst], writes=[bH])
                P.op("act", lambda e, csp=csp: e.activation(out=Aa[:, 0:n], in_=Aa[:, 0:n], func=AF.Exp, scale=csp),
                     reads=[bA, self.bconst], writes=[bA])
                P.op("dve", lambda e: e.tensor_tensor(out=T1[:, 0:n], in0=T1[:, 0:n], in1=H[:, 0:n], op=ALU.mult),
                     reads=[bT, bH], writes=[bT])
                P.op("dve", lambda e, xc=xc: e.tensor_tensor(out=T1[:, 0:n], in0=T1[:, 0:n], in1=xc[:, 0:n], op=ALU.mult),
                     reads=[bT, bxc], writes=[bT])
                init = 0.0 if S else pcol(ch, 3)
                P.op("dve", lambda e, init=init: e.tensor_tensor_scan(
                    out=H[:, 0:TP], data0=Aa[:, 0:TP], data1=T1[:, 0:TP], initial=init, op0=ALU.mult, op1=ALU.add),
                    reads=[bA, bT, self.bpers], writes=[bH])
                if S:
                    h0 = SST[:, ch * 64 + 48:ch * 64 + 64]
                    bh0 = bSST
                    for t in range(4):
                        cs_ = slice(TP + 16 * t, TP + 16 * t + 16)
                        prev = h0 if t == 0 else H[:, TP + 16 * (t - 1):TP + 16 * t]
                        P.op("dve", lambda e, cs_=cs_, prev=prev: e.tensor_tensor(out=H[:, cs_], in0=Aa[:, cs_], in1=prev, op=ALU.mult),
                             reads=[bA, bH, bh0], writes=[bH])
                        P.op("dve", lambda e, cs_=cs_: e.tensor_tensor(out=H[:, cs_], in0=H[:, cs_], in1=T1[:, cs_], op=ALU.add),
                             reads=[bH, bT], writes=[bH])
                    P.op("pool", lambda e, ch=ch: e.tensor_copy(out=pcol(ch, 3), in_=H[:, TP - 1:TP]), reads=[bH], writes=[self.bpers])
                    ps, bps = self.psum()
                    P.op("pe", lambda e, ps=ps: e.transpose(out=ps[0:16, 0:128], in_=H[:, TP + 48:TP + 64], identity=self.ident[:]),
                         reads=[bH, self.bconst], writes=[bps])
                    lo, blo = self.aclaim(f"lso{cc}", off[0] + 256 + cc * 128, 128)
                    P.op("act", lambda e, ps=ps, lo=lo: e.copy(out=lo[0:16, :], in_=ps[0:16, 0:128]), reads=[bps], writes=[blo])
                    P.dma("pool", d["o_lru_s"][:, o, ch * 128:(ch + 1) * 128], lo[0:16, :], reads=[blo])
                else:
                    P.dma("pool", d["o_lru_p"][o, ch * 128:(ch + 1) * 128].rearrange("(p o) -> p o", o=1), H[:, TP - 1:TP],
                          reads=[bH], allow_slow_non_contiguous=True)
                def evg(ps, bps, t0, mm):
                    P.op("act", lambda e: e.copy(out=GL[:, t0:t0 + mm], in_=ps[:, 0:mm]), reads=[bps], writes=[bG])
                self.proj(win[:, ch * 128:(ch + 1) * 128], 128, evg)
                P.op("dve", lambda e: e.tensor_tensor(out=T1[:, 0:n], in0=GL[:, 0:n], in1=GL[:, 0:n], op=ALU.mult), reads=[bG], writes=[bT])
                P.op("dve", lambda e: e.tensor_scalar(out=T1[:, 0:n], in0=T1[:, 0:n], scalar1=0.044715, scalar2=1.0, op0=ALU.mult, op1=ALU.add),
                     reads=[bT], writes=[bT])
                P.op("dve", lambda e: e.tensor_tensor(out=T1[:, 0:n], in0=T1[:, 0:n], in1=GL[:, 0:n], op=ALU.mult), reads=[bT, bG], writes=[bT])
                P.op("act", lambda e: e.activation(out=T1[:, 0:n], in_=T1[:, 0:n], func=AF.Sigmoid, scale=1.5957691216057308),
                     reads=[bT], writes=[bT])
                P.op("dve", lambda e: e.tensor_tensor(out=GL[:, 0:n], in0=GL[:, 0:n], in1=T1[:, 0:n], op=ALU.mult), reads=[bT, bG], writes=[bG])
                ob, bob = obs[cc]
                P.op("dve", lambda e, ob=ob: e.tensor_tensor(out=ob[:, 0:TP], in0=H[:, 0:TP], in1=GL[:, 0:TP], op=ALU.mult),
                     reads=[bH, bG], writes=[bob])
                if S:
                    P.op("dve", lambda e, ob=ob: e.tensor_tensor(
                        out=ob[:, TP:TP + 64].rearrange("p (s t) -> p s t", t=4),
                        in0=H[:, TP:TP + 64].rearrange("p (t s) -> p s t", s=16),
                        in1=GL[:, TP:TP + 64].rearrange("p (s t) -> p s t", t=4), op=ALU.mult), reads=[bH, bG], writes=[bob])
            self.outproj_multi(wout, [(2 * b + cc, obs[cc][0], obs[cc][1]) for cc in range(2)])

    def even(self, l):
        P, c, d = self.P, self.c, self.d
        e = l // 2
        KC, TP, D, RW, NHP, NHG = c.KC, c.TP, c.D, c.RW, c.NHP, c.NHG
        n, ti, NM, NS = self.n, self.ti, self.NMAX, self.NSEG
        S = (ti == 0)
        bc = self.bconst
        win, wout = d["ev_w_in"][e], d["ev_w_out"][e]
        self.rmsnorm(f"mix_norm{l}")
        off = [0]

        def A(name, size):
            ap, b = self.aclaim(name, off[0], size)
            off[0] += size
            return ap, b
        nts = self.ntiles(n)
        chunks = [(c0, "p") for c0 in range(0, TP, 128)] + ([(TP, "s")] if S else [])
        lor = {}
        for nm in ("zw", "za", "zg0", "zg1"):
            ap, b = A("L" + nm, NM // 2)
            lor[nm] = (ap.bitcast(BF16), b)
        Pb, bPb = A("Pb", 512)
        shs, bshs = A("shs", 16)
        ob_f, bob = A("ob", NM // 2)
        ob = ob_f.bitcast(BF16)
        base_off = off[0]
        R, bR = A("R", NM)
        Kk, bK = A("Kk", NM)
        V, bV = A("V", NM)
        Zt, bZt = R, bR

        def seginfo(nm):
            i, c0, m = self.segs[nm]
            mu = self.cst[0:m, self.ccol[f"mu{e}"] + i:self.ccol[f"mu{e}"] + i + 1]
            om = self.cst[0:m, self.ccol[f"om{e}"] + i:self.ccol[f"om{e}"] + i + 1]
            pv = self.prevp[0:m, e * NS + i:e * NS + i + 1]
            return c0, m, mu, om, pv

        def lerp_proj(nm, Z, bZ):
            c0, m, mu, om, pv = seginfo(nm)
            if S:
                P.dma("sp", shs[0:m, 0:16], d["st_shift"][:, e, c0:c0 + m].rearrange("s c -> c s"), writes=[bshs],
                      allow_slow_non_contiguous=True)

            def evac(ps, bps, t0, mm):
                P.op("act", lambda en: en.copy(out=Pb[0:m, 0:mm], in_=ps[0:m, 0:mm]), reads=[bps], writes=[bPb])
                P.op("dve", lambda en: en.tensor_scalar(out=Z[0:m, t0:t0 + mm], in0=Pb[0:m, 0:mm], scalar1=om, scalar2=None,
                                                        op0=ALU.mult), reads=[bPb, bc], writes=[bZ])
                pe = min(t0 + mm, TP)
                pw = pe - t0
                if pw > 0:
                    if pw > 1:
                        P.op("dve", lambda en: en.scalar_tensor_tensor(
                            out=Z[0:m, t0 + 1:pe], in0=Pb[0:m, 0:pw - 1], scalar=mu, in1=Z[0:m, t0 + 1:pe],
                            op0=ALU.mult, op1=ALU.add), reads=[bPb, bZ, bc], writes=[bZ])
                    P.op("dve", lambda en: en.scalar_tensor_tensor(
                        out=Z[0:m, t0:t0 + 1], in0=pv, scalar=mu, in1=Z[0:m, t0:t0 + 1], op0=ALU.mult, op1=ALU.add),
                        reads=[self.bprev, bZ, bc], writes=[bZ])
                    P.op("dve", lambda en: en.tensor_copy(out=pv, in_=Pb[0:m, pw - 1:pw]), reads=[bPb], writes=[self.bprev])
                    if ti == 1 and pe == TP:
                        P.dma("pool", d["o_shift_p"][e, c0:c0 + m].rearrange("(p o) -> p o", o=1), Pb[0:m, pw - 1:pw],
                              reads=[bPb], allow_slow_non_contiguous=True)
                if t0 + mm > TP:
                    a0 = max(t0, TP) - t0
                    P3 = Pb[0:m, a0:a0 + 64].rearrange("p (s t) -> p s t", t=4)
                    Z3 = Z[0:m, TP:TP + 64].rearrange("p (s t) -> p s t", t=4)
                    P.op("dve", lambda en: en.scalar_tensor_tensor(
                        out=Z3[:, :, 1:4], in0=P3[:, :, 0:3], scalar=mu, in1=Z3[:, :, 1:4], op0=ALU.mult, op1=ALU.add),
                        reads=[bPb, bZ, bc], writes=[bZ])
                    P.op("dve", lambda en: en.scalar_tensor_tensor(
                        out=Z3[:, :, 0:1], in0=shs[0:m, 0:16].rearrange("p (s o) -> p s o", o=1), scalar=mu,
                        in1=Z3[:, :, 0:1], op0=ALU.mult, op1=ALU.add), reads=[bshs, bZ, bc], writes=[bZ])
                    P.dma("pool", d["o_shift_s"][:, e, c0:c0 + m].rearrange("s (c o) -> c s o", o=1), P3[:, :, 3:4],
                          reads=[bPb], allow_slow_non_contiguous=True)
            self.proj(win[:, c0:c0 + m], m, evac)

        for nm, fn in (("zw", AF.Tanh), ("za", AF.Copy), ("zg0", AF.Sigmoid), ("zg1", AF.Sigmoid)):
            m = self.segs[nm][2]
            lerp_proj(nm, Zt, bZt)
            dst, bd = lor[nm]
            P.op("act", lambda en, dst=dst, m=m, fn=fn: en.activation(out=dst[0:m, 0:n], in_=Zt[0:m, 0:n], func=fn),
                 reads=[bZt], writes=[bd])
        TW, ZA, SG0, SG1 = lor["zw"], lor["za"], lor["zg0"], lor["zg1"]
        names = ["SGW", "AA", "G", "KKN", "CS", "E1", "E2", "E3", "T", "BV", "AL", "BE", "KT", "RT", "OT", "Y"]
        W = {}
        for nm in names:
            W[nm] = A(nm, 128)
        TM, bTM = A("TM", 768)
        PA = {nm: A(nm, 256) for nm in ("N", "L", "N2", "L2", "PM", "MAK", "MRB", "MRK")}
        XS, bXS = A("XS", 128)
        Xsb, bXsb = A("Xsb", 128)
        Usb, bUsb = A("Usb", 128)
        OS, bOS = A("OS", 64)
        T2, bT2 = A("T2", 64)
        if S:
            SL, bSL = A("SL", 512)
            STs, bSTs = A("STs", 1024)
            BTq, bBTq = A("BTq", 128)
            KTq, bKTq = A("KTq", 128)
            T2s, bT2s = A("T2s", 256)

        def cv(key, hp):
            return self.cs(f"{key}{e}", hp)

        for hp in range(NHP):
            lerp_proj(f"r{hp}", R, bR)
            lerp_proj(f"k{hp}", Kk, bK)
            lerp_proj(f"v{hp}", V, bV)
            STp = self.ST[:, (e * NHP + hp) * 64:(e * NHP + hp + 1) * 64]
            def do_chunk(hp, c0, kind, STp):
                sm = (kind == "s")
                WD = 64 if sm else 128
                NCI = WD // 64
                sl = slice(c0, c0 + WD)
                t = {k: v[0][:, 0:WD] for k, v in W.items()}
                b = {k: v[1] for k, v in W.items()}
                for (wkey, src, kr, dst, act, bias) in (("rw_w2", TW, 96, "SGW", AF.Sigmoid, cv("rw_w0", hp)),
                                                        ("rw_a2", ZA, 96, "AA", AF.Sigmoid, cv("rw_a0", hp))):
                    w, bw = self.wload(d[wkey][e][:, hp * 128:(hp + 1) * 128].rearrange("(k p) c -> p k c", p=kr), kr, 1, 128,
                                       pool_cast=True)
                    ps, bps = self.psum()
                    P.op("pe", lambda en, ps=ps, w=w, src=src, kr=kr: en.matmul(
                        ps[:, 0:WD], lhsT=w[0:kr, 0:128], rhs=src[0][0:kr, sl], start=True, stop=True),
                        reads=[bw, src[1]], writes=[bps])
                    P.op("act", lambda en, ps=ps, dst=dst, act=act, bias=bias: en.activation(
                        out=t[dst], in_=ps[:, 0:WD], func=act, bias=bias), reads=[bps, bc], writes=[b[dst]])
                w, bw = self.wload(d["rw_g2"][e][:, hp * 128:(hp + 1) * 128].rearrange("(k p) c -> p k c", p=128), 128, 2, 128,
                                   pool_cast=True)
                ps, bps = self.psum()
                for j, sg in enumerate((SG0, SG1)):
                    P.op("pe", lambda en, ps=ps, w=w, sg=sg, j=j: en.matmul(
                        ps[:, 0:WD], lhsT=w[:, j * 128:(j + 1) * 128], rhs=sg[0][:, sl], start=(j == 0), stop=(j == 1)),
                        reads=[bw, sg[1]], writes=[bps])
                P.op("act", lambda en, ps=ps: en.copy(out=t["G"], in_=ps[:, 0:WD]), reads=[bps], writes=[b["G"]])
                dv = lambda fn, rd, wr: P.op("dve", fn, reads=rd, writes=wr)
                ac = lambda fn, rd, wr: P.op("act", fn, reads=rd, writes=wr)
                dv(lambda en: en.tensor_scalar(out=t["SGW"], in0=t["SGW"], scalar1=-0.6065306597126334, scalar2=None, op0=ALU.mult),
                   [b["SGW"]], [b["SGW"]])
                dv(lambda en: en.tensor_scalar(out=t["KKN"], in0=Kk[:, sl], scalar1=cv("rw_k_k", hp), scalar2=None, op0=ALU.mult),
                   [bK, bc], [b["KKN"]])
                dv(lambda en: en.tensor_tensor(out=t["T"], in0=t["KKN"], in1=t["KKN"], op=ALU.mult), [b["KKN"]], [b["T"]])
                ps, bps = self.psum()
                P.op("pe", lambda en, ps=ps: en.matmul(ps[:, 0:WD], lhsT=self.bones[:], rhs=t["T"], start=True, stop=True),
                     reads=[b["T"], bc], writes=[bps])
                dv(lambda en, ps=ps: en.tensor_scalar(out=t["T"], in0=ps[:, 0:WD], scalar1=1e-19, scalar2=None, op0=ALU.max),
                   [bps], [b["T"]])
                ac(lambda en: en.activation(out=t["T"], in_=t["T"], func=AF.Ln), [b["T"]], [b["T"]])
                ac(lambda en: en.activation(out=t["T"], in_=t["T"], func=AF.Exp, scale=-0.5), [b["T"]], [b["T"]])
                dv(lambda en: en.tensor_tensor(out=t["KKN"], in0=t["KKN"], in1=t["T"], op=ALU.mult), [b["KKN"], b["T"]], [b["KKN"]])
                dv(lambda en: en.tensor_scalar(out=t["T"], in0=t["AA"], scalar1=-1.0, scalar2=cv("rw_k_a", hp), op0=ALU.add, op1=ALU.mult),
                   [b["AA"], bc], [b["T"]])
                dv(lambda en: en.scalar_tensor_tensor(out=t["KT"], in0=t["T"], scalar=1.0, in1=Kk[:, sl], op0=ALU.add, op1=ALU.mult),
                   [b["T"], bK], [b["KT"]])
                dv(lambda en: en.tensor_tensor(out=t["T"], in0=R[:, sl], in1=t["KT"], op=ALU.mult), [bR, b["KT"]], [b["T"]])
                dv(lambda en: en.tensor_scalar(out=t["T"], in0=t["T"], scalar1=cv("rw_r_k", hp), scalar2=None, op0=ALU.mult),
                   [b["T"], bc], [b["T"]])
                ps, bps = self.psum()
                P.op("pe", lambda en, ps=ps: en.matmul(ps[:, 0:WD], lhsT=self.bones[:], rhs=t["T"], start=True, stop=True),
                     reads=[b["T"], bc], writes=[bps])
                dv(lambda en, ps=ps: en.tensor_tensor(out=t["BV"], in0=ps[:, 0:WD], in1=V[:, sl], op=ALU.mult), [bps, bV], [b["BV"]])
                dv(lambda en: en.tensor_tensor(out=t["BE"], in0=t["KKN"], in1=t["AA"], op=ALU.mult), [b["KKN"], b["AA"]], [b["BE"]])
                rst = self.mrst_s[:, 0:64] if sm else self.mrst[:, 0:128]
                dv(lambda en, rst=rst: en.tensor_tensor_scan(out=t["CS"], data0=rst, data1=t["SGW"], initial=0.0,
                                                             op0=ALU.mult, op1=ALU.add), [b["SGW"], bc], [b["CS"]])
                ac(lambda en: en.activation(out=t["E1"], in_=t["CS"], func=AF.Exp), [b["CS"]], [b["E1"]])
                ac(lambda en: en.activation(out=t["E2"], in_=t["CS"], func=AF.Exp, scale=-1.0), [b["CS"]], [b["E2"]])
                dv(lambda en: en.tensor_tensor(out=t["T"], in0=t["CS"], in1=t["SGW"], op=ALU.subtract), [b["CS"], b["SGW"]], [b["T"]])
                ac(lambda en: en.activation(out=t["E3"], in_=t["T"], func=AF.Exp), [b["T"]], [b["E3"]])
                dv(lambda en: en.tensor_tensor(out=t["RT"], in0=R[:, sl], in1=t["E1"], op=ALU.mult), [bR, b["E1"]], [b["RT"]])
                dv(lambda en: en.tensor_tensor(out=t["KT"], in0=t["KT"], in1=t["E2"], op=ALU.mult), [b["KT"], b["E2"]], [b["KT"]])
                dv(lambda en: en.tensor_tensor(out=t["BE"], in0=t["BE"], in1=t["E2"], op=ALU.mult), [b["BE"], b["E2"]], [b["BE"]])
                dv(lambda en: en.scalar_tensor_tensor(out=t["AL"], in0=t["KKN"], scalar=-1.0, in1=t["E3"], op0=ALU.mult, op1=ALU.mult),
                   [b["KKN"], b["E3"]], [b["AL"]])
                for ci in range(NCI):
                    cs_ = slice(ci * 64, (ci + 1) * 64)
                    ps, bps = self.psum()
                    for j, (src, bs_) in enumerate(((V[:, c0 + ci * 64:c0 + (ci + 1) * 64], bV), (t["BE"][:, cs_], b["BE"]),
                                                    (t["KT"][:, cs_], b["KT"]))):
                        P.op("pe", lambda en, ps=ps, src=src, j=j: en.transpose(out=ps[0:64, j * 128:(j + 1) * 128], in_=src,
                                                                                 identity=self.ident[:]),
                             reads=[bs_, bc], writes=[bps])
                    ac(lambda en, ps=ps, ci=ci: en.copy(out=TM[0:64, ci * 384:(ci + 1) * 384], in_=ps[0:64, 0:384]), [bps], [bTM])
                VM = lambda ci, h2: TM[0:64, ci * 384 + h2 * 64:ci * 384 + (h2 + 1) * 64]
                BT = lambda ci, h2: TM[0:64, ci * 384 + 128 + h2 * 64:ci * 384 + 128 + (h2 + 1) * 64]
                KTm = lambda ci, h2: TM[0:64, ci * 384 + 256 + h2 * 64:ci * 384 + 256 + (h2 + 1) * 64]
                msu, mui, msl = (self.msu_s, self.mui_s, self.msl_s) if sm else (self.msu, self.mui, self.msl)
                NSL = 2 * NCI
                SW = NSL * 64
                LOWP = ("N", "L", "N2", "L2", "PM")
                pa = {k: (v[0].bitcast(BF16)[:, 0:SW] if k in LOWP else v[0][:, 0:SW]) for k, v in PA.items()}
                pb = {k: v[1] for k, v in PA.items()}
                Xb = Xsb.bitcast(BF16)
                slot = lambda h2, ci: slice((h2 * NCI + ci) * 64, (h2 * NCI + ci + 1) * 64)

                def kstage(dst, lhs, rhs, mask):
                    for h2 in range(2):
                        ps, bps = self.psum()
                        hs = slice(h2 * 64, (h2 + 1) * 64)
                        for ci in range(NCI):
                            cs_ = slice(ci * 64, (ci + 1) * 64)
                            P.op("pe", lambda en, ps=ps, hs=hs, cs_=cs_: en.matmul(ps[0:64, cs_], lhsT=t[lhs][hs, cs_], rhs=t[rhs][hs, cs_],
                                                                                   start=True, stop=True),
                                 reads=[b[lhs], b[rhs]], writes=[bps])
                        dv(lambda en, ps=ps, h2=h2: en.tensor_tensor(
                            out=pa[dst][0:64, h2 * NCI * 64:(h2 + 1) * NCI * 64], in0=ps[0:64, 0:NCI * 64], in1=mask[0:64, 0:NCI * 64],
                            op=ALU.mult), [bps, bc], [pb[dst]])
                kstage("N", "BE", "AL", msu)
                kstage("L", "AL", "BE", msl)
                kstage("MAK", "KT", "AL", msu)
                kstage("MRB", "BE", "RT", mui)
                kstage("MRK", "KT", "RT", mui)
                dv(lambda en: en.tensor_tensor(out=pa["PM"][0:64, :], in0=pa["N"][0:64, :], in1=self.id4[0:64, 0:SW], op=ALU.add),
                   [pb["N"], bc], [pb["PM"]])
                cur = ("N", "L", "N2", "L2")
                for step in range(1 if sm else 5):
                    Nn, Ln, N2n, L2n = cur
                    for (dst, lh, rh, eng) in ((N2n, Ln, Nn, "act"), (L2n, Nn, Ln, "dve")):
                        ps, bps = self.psum()
                        for u in range(NSL):
                            us = slice(u * 64, (u + 1) * 64)
                            P.op("pe", lambda en, ps=ps, us=us, lh=lh, rh=rh: en.matmul(
                                ps[0:64, us], lhsT=pa[lh][0:64, us], rhs=pa[rh][0:64, us], start=True, stop=True),
                                reads=[pb[lh], pb[rh]], writes=[bps])
                        if eng == "act":
                            ac(lambda en, ps=ps, dst=dst: en.copy(out=pa[dst][0:64, :], in_=ps[0:64, 0:SW]), [bps], [pb[dst]])
                        else:
                            dv(lambda en, ps=ps, dst=dst: en.tensor_copy(out=pa[dst][0:64, :], in_=ps[0:64, 0:SW]), [bps], [pb[dst]])
                    ps, bps = self.psum()
                    for u in range(NSL):
                        us = slice(u * 64, (u + 1) * 64)
                        P.op("pe", lambda en, ps=ps, us=us, L2n=L2n: en.matmul(
                            ps[0:64, us], lhsT=pa[L2n][0:64, us], rhs=pa["PM"][0:64, us], start=True, stop=True),
                            reads=[pb[L2n], pb["PM"]], writes=[bps])
                    dv(lambda en, ps=ps: en.tensor_tensor(out=pa["PM"][0:64, :], in0=ps[0:64, 0:SW], in1=pa["PM"][0:64, :], op=ALU.add),
                       [bps, pb["PM"]], [pb["PM"]])
                    cur = (N2n, L2n, Nn, Ln)
                if sm:
                    for g in range(4):
                        for h2 in range(2):
                            P.dma("sp", SL[0:64, :].rearrange("p (s c) -> p s c", c=128)[:, :, h2 * 64:(h2 + 1) * 64],
                                  d["st_rwkv"][g * 4:(g + 1) * 4, e, 2 * hp + h2].rearrange("s v k -> v s k"), writes=[bSL])
                        ps, bps = self.psum()
                        for q in range(4):
                            P.op("pe", lambda en, ps=ps, q=q: en.transpose(out=ps[:, q * 64:(q + 1) * 64],
                                                                           in_=SL[0:64, q * 128:(q + 1) * 128], identity=self.ident[0:64, 0:64]),
                                 reads=[bSL, bc], writes=[bps])
                        ac(lambda en, ps=ps, g=g: en.copy(out=STs[:, g * 256:(g + 1) * 256], in_=ps[:, 0:256]), [bps], [bSTs])
                for ci in range(NCI):
                    co = ci * 64
                    if sm:
                        subs = [(4 * q, 4, STs[:, q * 64:(q + 1) * 64], bSTs) for q in range(16)]
                    else:
                        subs = [(0, 64, STp, self.bST)]
                    for h2 in range(2):
                        hs = slice(h2 * 64, (h2 + 1) * 64)
                        ps, bps = self.psum()
                        ps2, bps2 = self.psum()
                        for (o_, ln, Ssrc, bS_) in subs:
                            P.op("pe", lambda en, ps=ps, o_=o_, ln=ln, Ssrc=Ssrc, hs=hs, co=co: en.matmul(
                                ps[0:64, o_:o_ + ln], lhsT=Ssrc[hs, :], rhs=t["AL"][hs, co + o_:co + o_ + ln], start=True, stop=True),
                                reads=[bS_, b["AL"]], writes=[bps])
                            P.op("pe", lambda en, ps2=ps2, o_=o_, ln=ln, Ssrc=Ssrc, hs=hs, co=co: en.matmul(
                                ps2[hs, o_:o_ + ln], lhsT=Ssrc[hs, :], rhs=t["RT"][hs, co + o_:co + o_ + ln], start=True, stop=True),
                                reads=[bS_, b["RT"]], writes=[bps2])
                        ac(lambda en, ps=ps, hs=hs: en.copy(out=XS[0:64, hs], in_=ps[0:64, 0:64]), [bps], [bXS])
                        ac(lambda en, ps2=ps2, hs=hs: en.copy(out=OS[hs, 0:64], in_=ps2[hs, 0:64]), [bps2], [bOS])
                    ps, bps = self.psum()
                    for h2 in range(2):
                        hs = slice(h2 * 64, (h2 + 1) * 64)
                        P.op("pe", lambda en, ps=ps, hs=hs, h2=h2, ci=ci: en.matmul(ps[0:64, hs], lhsT=pa["MAK"][0:64, slot(h2, ci)],
                                                                                    rhs=VM(ci, h2), start=True, stop=False),
                             reads=[pb["MAK"], bTM], writes=[bps])
                        P.op("pe", lambda en, ps=ps, hs=hs: en.matmul(ps[0:64, hs], lhsT=XS[0:64, hs], rhs=self.ident[0:64, 0:64],
                                                                      start=False, stop=True),
                             reads=[bXS, bc], writes=[bps])
                    ac(lambda en, ps=ps: en.copy(out=Xb[0:64, 0:128], in_=ps[0:64, 0:128]), [bps], [bXsb])
                    ps, bps = self.psum()
                    for h2 in range(2):
                        hs = slice(h2 * 64, (h2 + 1) * 64)
                        P.op("pe", lambda en, ps=ps, hs=hs, h2=h2, ci=ci: en.matmul(ps[0:64, hs], lhsT=pa["PM"][0:64, slot(h2, ci)],
                                                                                    rhs=Xb[0:64, hs], start=True, stop=True),
                             reads=[pb["PM"], bXsb], writes=[bps])
                    ac(lambda en, ps=ps: en.copy(out=Usb[0:64, :], in_=ps[0:64, 0:128]), [bps], [bUsb])
                    ps, bps = self.psum()
                    for h2 in range(2):
                        hs = slice(h2 * 64, (h2 + 1) * 64)
                        P.op("pe", lambda en, ps=ps, hs=hs, h2=h2, ci=ci: en.matmul(ps[hs, 0:64], lhsT=Usb[0:64, hs],
                                                                                    rhs=pa["MRB"][0:64, slot(h2, ci)], start=True, stop=False),
                             reads=[bUsb, pb["MRB"]], writes=[bps])
                        P.op("pe", lambda en, ps=ps, hs=hs, h2=h2, ci=ci: en.matmul(ps[hs, 0:64], lhsT=VM(ci, h2),
                                                                                    rhs=pa["MRK"][0:64, slot(h2, ci)], start=False, stop=True),
                             reads=[bTM, pb["MRK"]], writes=[bps])
                    dv(lambda en, ps=ps, co=co: en.tensor_tensor(out=t["OT"][:, co:co + 64], in0=ps[:, 0:64], in1=OS[:, 0:64], op=ALU.add),
                       [bps, bOS], [b["OT"]])
                    if not sm:
                        ps, bps = self.psum()
                        for h2 in range(2):
                            hs = slice(h2 * 64, (h2 + 1) * 64)
                            P.op("pe", lambda en, ps=ps, hs=hs, h2=h2, ci=ci: en.matmul(ps[hs, 0:64], lhsT=BT(ci, h2), rhs=Usb[0:64, hs],
                                                                                        start=True, stop=False),
                                 reads=[bTM, bUsb], writes=[bps])
                            P.op("pe", lambda en, ps=ps, hs=hs, h2=h2, ci=ci: en.matmul(ps[hs, 0:64], lhsT=KTm(ci, h2), rhs=VM(ci, h2),
                                                                                        start=False, stop=True),
                                 reads=[bTM], writes=[bps])
                        dv(lambda en, ps=ps: en.tensor_tensor(out=T2[:, 0:64], in0=ps[:, 0:64], in1=STp, op=ALU.add), [bps, self.bST], [bT2])
                        dv(lambda en, co=co: en.tensor_scalar(out=STp, in0=T2[:, 0:64], scalar1=t["E1"][:, co + 63:co + 64], scalar2=None,
                                                              op0=ALU.mult), [bT2, b["E1"]], [self.bST])
                        if ti == 1 and c0 + co == TP - 64:
                            ps, bps = self.psum()
                            P.op("pe", lambda en, ps=ps: en.transpose(out=ps[0:64, 0:128], in_=STp, identity=self.ident[:]),
                                 reads=[self.bST, bc], writes=[bps])
                            ac(lambda en, ps=ps: en.copy(out=Xsb[0:64, :], in_=ps[0:64, 0:128]), [bps], [bXsb])
                            P.dma("pool", d["o_rwkv_p"][e, 2 * hp:2 * hp + 2].rearrange("h v k -> v h k"),
                                  Xsb[0:64, :].rearrange("p (h k) -> p h k", k=64), reads=[bXsb])
                    else:
                        for g in range(4):
                            ps, bps = self.psum()
                            for q4 in range(4):
                                q = g * 4 + q4
                                dv(lambda en, q=q: en.tensor_scalar(out=BTq[0:64, :], in0=TM[0:64, 128:256], scalar1=self.rowm[:, q:q + 1],
                                                                    scalar2=None, op0=ALU.mult), [bTM, bc], [bBTq])
                                dv(lambda en, q=q: en.tensor_scalar(out=KTq[0:64, :], in0=TM[0:64, 256:384], scalar1=self.rowm[:, q:q + 1],
                                                                    scalar2=None, op0=ALU.mult), [bTM, bc], [bKTq])
                                for h2 in range(2):
                                    hs = slice(h2 * 64, (h2 + 1) * 64)
                                    P.op("pe", lambda en, ps=ps, hs=hs, q4=q4: en.matmul(
                                        ps[hs, q4 * 64:(q4 + 1) * 64], lhsT=BTq[0:64, hs], rhs=Usb[0:64, hs], start=True, stop=False),
                                        reads=[bBTq, bUsb], writes=[bps])
                                    P.op("pe", lambda en, ps=ps, hs=hs, h2=h2, q4=q4: en.matmul(
                                        ps[hs, q4 * 64:(q4 + 1) * 64], lhsT=KTq[0:64, hs], rhs=VM(0, h2), start=False, stop=True),
                                        reads=[bKTq, bTM], writes=[bps])
                            gs = slice(g * 256, (g + 1) * 256)
                            dv(lambda en, ps=ps, gs=gs: en.tensor_tensor(out=T2s[:, 0:256], in0=ps[:, 0:256], in1=STs[:, gs], op=ALU.add),
                               [bps, bSTs], [bT2s])
                            lam = t["E1"][:, 0:64].rearrange("p (s t) -> p s t", t=4)[:, g * 4:(g + 1) * 4, 3:4]
                            dv(lambda en, gs=gs, lam=lam: en.tensor_tensor(
                                out=STs[:, gs].rearrange("p (s v) -> p s v", v=64), in0=T2s[:, 0:256].rearrange("p (s v) -> p s v", v=64),
                                in1=lam.broadcast_to([128, 4, 64]), op=ALU.mult), [bT2s, b["E1"]], [bSTs])
                            ps, bps = self.psum()
                            for q4 in range(4):
                                P.op("pe", lambda en, ps=ps, q4=q4, g=g: en.transpose(
                                    out=ps[0:64, q4 * 128:(q4 + 1) * 128], in_=STs[:, (g * 4 + q4) * 64:(g * 4 + q4 + 1) * 64],
                                    identity=self.ident[:]), reads=[bSTs, bc], writes=[bps])
                            ac(lambda en, ps=ps: en.copy(out=SL[0:64, 0:512], in_=ps[0:64, 0:512]), [bps], [bSL])
                            for h2 in range(2):
                                P.dma("pool", d["o_rwkv_s"][g * 4:(g + 1) * 4, e, 2 * hp + h2].rearrange("s v k -> v s k"),
                                      SL[0:64, :].rearrange("p (s c) -> p s c", c=128)[:, :, h2 * 64:(h2 + 1) * 64], reads=[bSL])
                ps, bps = self.psum()
                P.op("pe", lambda en, ps=ps: en.matmul(ps[:, 0:WD], lhsT=self.bones[:], rhs=t["OT"], start=True, stop=True),
                     reads=[b["OT"], bc], writes=[bps])
                dv(lambda en, ps=ps: en.scalar_tensor_tensor(out=t["Y"], in0=ps[:, 0:WD], scalar=-1.0 / 64, in1=t["OT"],
                                                             op0=ALU.mult, op1=ALU.add), [bps, b["OT"]], [b["Y"]])
                dv(lambda en: en.tensor_tensor(out=t["T"], in0=t["Y"], in1=t["Y"], op=ALU.mult), [b["Y"]], [b["T"]])
                ps, bps = self.psum()
                P.op("pe", lambda en, ps=ps: en.matmul(ps[:, 0:WD], lhsT=self.bones[:], rhs=t["T"], start=True, stop=True),
                     reads=[b["T"], bc], writes=[bps])
                ac(lambda en, ps=ps: en.activation(out=t["T"], in_=ps[:, 0:WD], func=AF.Ln, scale=1.0 / 64, bias=self.eps2[:, 0:1]),
                   [bps, bc], [b["T"]])
                ac(lambda en: en.activation(out=t["T"], in_=t["T"], func=AF.Exp, scale=-0.5), [b["T"]], [b["T"]])
                dv(lambda en: en.tensor_tensor(out=t["Y"], in0=t["Y"], in1=t["T"], op=ALU.mult), [b["Y"], b["T"]], [b["Y"]])
                dv(lambda en: en.tensor_scalar(out=t["Y"], in0=t["Y"], scalar1=cv("rw_ln_w", hp), scalar2=cv("rw_ln_b", hp),
                                               op0=ALU.mult, op1=ALU.add), [b["Y"], bc], [b["Y"]])
                dv(lambda en: en.tensor_tensor(out=t["Y"], in0=t["Y"], in1=t["BV"], op=ALU.add), [b["Y"], b["BV"]], [b["Y"]])
                dv(lambda en: en.tensor_tensor(out=ob[:, sl], in0=t["Y"], in1=t["G"], op=ALU.mult), [b["Y"], b["G"]], [bob])
            for (c0, kind) in chunks:
                do_chunk(hp, c0, kind, STp)
            self.outproj(wout, hp, ob, bob)
        if "nohg" not in self.stages:
            self.hgrn(l, base_off, ob, bob)

    def hgrn(self, l, base_off, ob, bob):
        P, c, d = self.P, self.c, self.d
        e = l // 2
        KC, TP, D, RW, NHP, NHG = c.KC, c.TP, c.D, c.RW, c.NHP, c.NHG
        n, ti, NM = self.n, self.ti, self.NMAX
        S = (ti == 0)
        bc = self.bconst
        win, wout = d["ev_w_in"][e], d["ev_w_out"][e]
        off = [base_off]

        def A(name, size):
            ap, b = self.aclaim(name, off[0], size)
            off[0] += size
            return ap, b
        chunks = [(c0, "p") for c0 in range(0, TP, 128)] + ([(TP, "s")] if S else [])
        Q, bQ = A("hQ", NM)
        LF, bLF = A("hLF", NM)
        KH, bKH = A("hKH", NM)
        IV, bIV = A("hIV", NM)
        OG, bOG = A("hOG", NM)
        W = {nm: A("h" + nm, 128) for nm in ("G", "EG", "QP", "KP", "KPP", "T", "OT", "Y")}
        TMf, bTM = A("hTM", 512)
        ATf, bAT = A("hAT", 128)
        if S:
            Ss, bSs = A("hSs", 2048)
            KQ, bKQ = A("hKQ", 128)
        ob2f, bob2 = A("hob2", NM // 2)
        obs = [(ob, bob), (ob2f.bitcast(BF16), bob2)]
        P.op("pool", lambda en: en.memset(TMf[:, :], 0.0), writes=[bTM])
        P.op("pool", lambda en: en.memset(ATf[:, :], 0.0), writes=[bAT])
        P.op("pool", lambda en: en.memset(KQ[:, :], 0.0), writes=[bKQ]) if S else None
        Wt = W
        b = {k: v[1] for k, v in W.items()}
        dv = lambda fn, rd, wr: P.op("dve", fn, reads=rd, writes=wr)
        ac = lambda fn, rd, wr: P.op("act", fn, reads=rd, writes=wr)
        for hg in range(NHG):
            lb = self.cs(f"lb{e}", hg)
            oml = self.cs(f"oml{e}", hg)
            colq = c.P_RW + hg * 128

            def ev_q(ps, bps, t0, mm):
                ac(lambda en: en.activation(out=Q[:, t0:t0 + mm], in_=ps[:, 0:mm], func=AF.Silu), [bps], [bQ])

            def ev_f(ps, bps, t0, mm, lb=lb, oml=oml):
                ac(lambda en: en.activation(out=KH[:, t0:t0 + mm], in_=ps[:, 0:mm], func=AF.Sigmoid), [bps], [bKH])
                dv(lambda en: en.tensor_scalar(out=KH[:, t0:t0 + mm], in0=KH[:, t0:t0 + mm], scalar1=oml, scalar2=lb,
                                               op0=ALU.mult, op1=ALU.add), [bKH, bc], [bKH])
                ac(lambda en: en.activation(out=LF[:, t0:t0 + mm], in_=KH[:, t0:t0 + mm], func=AF.Ln), [bKH], [bLF])
                dv(lambda en: en.tensor_scalar(out=KH[:, t0:t0 + mm], in0=KH[:, t0:t0 + mm], scalar1=-1.0, scalar2=1.0,
                                               op0=ALU.mult, op1=ALU.add), [bKH, bLF], [bKH])

            def ev_i(ps, bps, t0, mm):
                ac(lambda en: en.copy(out=IV[:, t0:t0 + mm], in_=ps[:, 0:mm]), [bps], [bIV])

            def ev_o(ps, bps, t0, mm):
                ac(lambda en: en.activation(out=OG[:, t0:t0 + mm], in_=ps[:, 0:mm], func=AF.Silu), [bps], [bOG])
            self.proj(win[:, colq:colq + 128], 128, ev_q)
            self.proj(win[:, colq + RW:colq + RW + 128], 128, ev_f)
            self.proj(win[:, colq + 2 * RW:colq + 2 * RW + 128], 128, ev_i)
            self.proj(win[:, colq + 3 * RW:colq + 3 * RW + 128], 128, ev_o)
            SHp = self.SH[:, (e * NHG + hg) * 128:(e * NHG + hg + 1) * 128]

            def do_chunk(hg, c0, kind, SHp):
                ob, bob = obs[hg % 2]
                sm = (kind == "s")
                WD = 64 if sm else 128
                NCI = WD // 64
                sl = slice(c0, c0 + WD)
                t = {k: v[0][:, 0:WD] for k, v in Wt.items()}
                rst = self.mrst_s[:, 0:64] if sm else self.mrst[:, 0:128]
                dv(lambda en: en.tensor_tensor_scan(out=t["G"], data0=rst, data1=LF[:, sl], initial=0.0, op0=ALU.mult, op1=ALU.add),
                   [bLF, bc], [b["G"]])
                ac(lambda en: en.activation(out=t["EG"], in_=t["G"], func=AF.Exp), [b["G"]], [b["EG"]])
                dv(lambda en: en.tensor_tensor(out=t["QP"], in0=Q[:, sl], in1=t["EG"], op=ALU.mult), [bQ, b["EG"]], [b["QP"]])
                ac(lambda en: en.activation(out=t["T"], in_=t["G"], func=AF.Exp, scale=-1.0), [b["G"]], [b["T"]])
                dv(lambda en: en.tensor_tensor(out=t["KP"], in0=KH[:, sl], in1=t["T"], op=ALU.mult), [bKH, b["T"]], [b["KP"]])
                tl = 4 if sm else 64
                g3 = t["G"].rearrange("p (s t) -> p s t", t=tl)
                dv(lambda en: en.tensor_tensor(out=t["T"].rearrange("p (s t) -> p s t", t=tl),
                                               in0=g3[:, :, tl - 1:tl].broadcast_to([128, WD // tl, tl]), in1=g3, op=ALU.subtract),
                   [b["G"]], [b["T"]])
                ac(lambda en: en.activation(out=t["T"], in_=t["T"], func=AF.Exp), [b["T"]], [b["T"]])
                dv(lambda en: en.tensor_tensor(out=t["KPP"], in0=KH[:, sl], in1=t["T"], op=ALU.mult), [bKH, b["T"]], [b["KPP"]])
                mui = self.mui_s if sm else self.mui
                psA, bpsA = self.psum()
                for ci in range(NCI):
                    cs_ = slice(ci * 64, (ci + 1) * 64)
                    ps, bps = self.psum()
                    P.op("pe", lambda en, ps=ps, ci=ci: en.transpose(out=ps[0:64, 0:128], in_=IV[:, c0 + ci * 64:c0 + (ci + 1) * 64],
                                                                     identity=self.ident[:]), reads=[bIV, bc], writes=[bps])
                    P.op("pe", lambda en, ps=ps, cs_=cs_: en.transpose(out=ps[0:64, 128:256], in_=t["KPP"][:, cs_], identity=self.ident[:]),
                         reads=[b["KPP"], bc], writes=[bps])
                    ac(lambda en, ps=ps, ci=ci: en.copy(out=TMf[0:64, ci * 256:(ci + 1) * 256], in_=ps[0:64, 0:256]), [bps], [bTM])
                    P.op("pe", lambda en, cs_=cs_: en.matmul(psA[0:64, cs_], lhsT=t["KP"][:, cs_], rhs=t["QP"][:, cs_], start=True, stop=True),
                         reads=[b["KP"], b["QP"]], writes=[bpsA])
                dv(lambda en: en.tensor_tensor(out=ATf[0:64, 0:WD], in0=psA[0:64, 0:WD], in1=mui[0:64, 0:WD], op=ALU.mult),
                   [bpsA, bc], [bAT])
                if sm:
                    P.dma("sp", Ss[:, :].rearrange("p (s v) -> p s v", v=128), d["st_hgrn"][:, e, hg].rearrange("s k v -> k s v"),
                          writes=[bSs])
                for ci in range(NCI):
                    cs_ = slice(ci * 64, (ci + 1) * 64)
                    IM = TMf[:, ci * 256:ci * 256 + 128]
                    KT2 = TMf[:, ci * 256 + 128:ci * 256 + 256]
                    ps, bps = self.psum()
                    P.op("pe", lambda en, ps=ps, IM=IM, cs_=cs_: en.matmul(ps[:, 0:64], lhsT=IM, rhs=ATf[:, cs_], start=True, stop=False),
                         reads=[bTM, bAT], writes=[bps])
                    if sm:
                        for q in range(16):
                            P.op("pe", lambda en, ps=ps, q=q: en.matmul(ps[:, 4 * q:4 * q + 4], lhsT=Ss[:, q * 128:(q + 1) * 128],
                                                                        rhs=t["QP"][:, 4 * q:4 * q + 4], start=False, stop=(q == 15)),
                                 reads=[bSs, b["QP"]], writes=[bps])
                    else:
                        P.op("pe", lambda en, ps=ps, cs_=cs_: en.matmul(ps[:, 0:64], lhsT=SHp, rhs=t["QP"][:, cs_], start=False, stop=True),
                             reads=[self.bSH, b["QP"]], writes=[bps])
                    ac(lambda en, ps=ps, cs_=cs_: en.copy(out=t["OT"][:, cs_], in_=ps[:, 0:64]), [bps], [b["OT"]])
                    if not sm:
                        ps, bps = self.psum()
                        P.op("pe", lambda en, ps=ps, IM=IM, KT2=KT2: en.matmul(ps[:, 0:128], lhsT=KT2, rhs=IM, start=True, stop=True),
                             reads=[bTM], writes=[bps])
                        dv(lambda en, ps=ps, ci=ci: en.scalar_tensor_tensor(out=SHp, in0=SHp, scalar=t["EG"][:, ci * 64 + 63:ci * 64 + 64],
                                                                            in1=ps[:, 0:128], op0=ALU.mult, op1=ALU.add),
                           [bps, self.bSH, b["EG"]], [self.bSH])
                        if ti == 1 and c0 + ci * 64 == TP - 64:
                            P.dma("pool", d["o_hgrn_p"][e, hg], SHp, reads=[self.bSH])
                    else:
                        eg3 = t["EG"][:, 0:64].rearrange("p (s t) -> p s t", t=4)
                        for g in range(4):
                            ps, bps = self.psum()
                            for q4 in range(4):
                                q = 4 * g + q4
                                dv(lambda en, q=q: en.tensor_scalar(out=KQ[0:64, :], in0=TMf[0:64, 128:256], scalar1=self.rowm[:, q:q + 1],
                                                                    scalar2=None, op0=ALU.mult), [bTM, bc], [bKQ])
                                P.op("pe", lambda en, ps=ps, q4=q4, IM=IM: en.matmul(ps[:, q4 * 128:(q4 + 1) * 128], lhsT=KQ[:, :], rhs=IM,
                                                                                     start=True, stop=True), reads=[bKQ, bTM], writes=[bps])
                            gs = slice(g * 512, (g + 1) * 512)
                            dv(lambda en, gs=gs, g=g: en.tensor_tensor(
                                out=Ss[:, gs].rearrange("p (s v) -> p s v", v=128), in0=Ss[:, gs].rearrange("p (s v) -> p s v", v=128),
                                in1=eg3[:, 4 * g:4 * g + 4, 3:4].broadcast_to([128, 4, 128]), op=ALU.mult), [bSs, b["EG"]], [bSs])
                            dv(lambda en, ps=ps, gs=gs: en.tensor_tensor(out=Ss[:, gs], in0=ps[:, 0:512], in1=Ss[:, gs], op=ALU.add),
                               [bps, bSs], [bSs])
                        P.dma("pool", d["o_hgrn_s"][:, e, hg].rearrange("s k v -> k s v"), Ss[:, :].rearrange("p (s v) -> p s v", v=128),
                              reads=[bSs])
                dv(lambda en: en.tensor_tensor(out=t["T"], in0=t["OT"], in1=t["OT"], op=ALU.mult), [b["OT"]], [b["T"]])
                ps, bps = self.psum()
                P.op("pe", lambda en, ps=ps: en.matmul(ps[:, 0:WD], lhsT=self.ones[:], rhs=t["T"], start=True, stop=True),
                     reads=[b["T"], bc], writes=[bps])
                ac(lambda en, ps=ps: en.activation(out=t["T"], in_=ps[:, 0:WD], func=AF.Ln, scale=1.0 / 128, bias=self.eps2[:, 1:2]),
                   [bps, bc], [b["T"]])
                ac(lambda en: en.activation(out=t["T"], in_=t["T"], func=AF.Exp, scale=-0.5), [b["T"]], [b["T"]])
                dv(lambda en: en.scalar_tensor_tensor(out=t["Y"], in0=t["OT"], scalar=self.cs(f"hg_norm{e}", hg), in1=t["T"],
                                                      op0=ALU.mult, op1=ALU.mult), [b["OT"], b["T"], bc], [b["Y"]])
                dv(lambda en: en.tensor_tensor(out=ob[:, sl], in0=t["Y"], in1=OG[:, sl], op=ALU.mult), [b["Y"], bOG], [bob])
            for (c0, kind) in chunks:
                do_chunk(hg, c0, kind, SHp)
            if hg % 2 == 1:
                self.outproj_multi(wout, [(NHP + hg - 1, obs[0][0], obs[0][1]), (NHP + hg, obs[1][0], obs[1][1])])
            elif hg == NHG - 1:
                self.outproj(wout, NHP + hg, obs[0][0], obs[0][1])

    def tile(self, ti):
        c = self.c
        self.ti = ti
        self.n = c.TP + (64 if ti == 0 else 0)
        self.load_x(ti)
        for l in range(c.DEPTH):
            if "ffn" in self.stages:
                self.ffn(l, 1)
            if l % 2 == 0 and "even" in self.stages:
                self.even(l)
            if l % 2 == 1 and "odd" in self.stages:
                self.odd(l)
            if "ffn" in self.stages:
                self.ffn(l, 2)
        self.store_y(ti)


INPUT_ORDER = ["x_prompt", "x_sample", "state_rwkv_shift", "state_rwkv", "state_hgrn", "state_conv", "state_lru"]


def make_in_maps(cfg, inputs):
    f = lambda a: np.ascontiguousarray(np.asarray(a, dtype=np.float32))
    nb = inputs["x_prompt"].shape[0]
    maps = []
    for core in range(NCORES):
        m = {}
        if core < nb:
            m["xp"] = f(inputs["x_prompt"][core])
        else:
            m["xp"] = np.zeros((cfg.SEQ, cfg.D), np.float32)
        sl = slice(core * 16, (core + 1) * 16)
        m["xs"] = f(inputs["x_sample"][sl]).reshape(64, cfg.D)
        m["st_shift"] = f(inputs["state_rwkv_shift"][sl])
        m["st_rwkv"] = f(inputs["state_rwkv"][sl])
        m["st_hgrn"] = f(inputs["state_hgrn"][sl])
        m["st_conv"] = f(inputs["state_conv"][sl])
        m["st_lru"] = f(inputs["state_lru"][sl])
        for k in inputs:
            if k not in INPUT_ORDER:
                m[k] = f(inputs[k])
        maps.append(m)
    return maps


def assemble(cfg, res, nb):
    R = res.results
    cat = lambda k, cores: np.stack([R[i][k] for i in cores], 0)
    P = list(range(nb))
    A = list(range(NCORES))
    y_prompt = cat("yp", P)
    y_sample = np.concatenate([R[i]["ys"].reshape(16, 4, cfg.D) for i in A], 0)
    outs = [y_prompt, y_sample]
    for k in ["o_shift_p", "o_rwkv_p", "o_hgrn_p", "o_conv_p", "o_lru_p"]:
        outs.append(cat(k, P))
    for k in ["o_shift_s", "o_rwkv_s", "o_hgrn_s", "o_conv_s", "o_lru_s"]:
        outs.append(np.concatenate([R[i][k] for i in A], 0))
    return tuple(np.ascontiguousarray(o.astype(np.float32)) for o in outs)


_CACHE = {}
STAGES = ("ffn", "odd", "even")


def run(cfg, inputs, stages=("ffn", "odd", "even")):
    key = (cfg.D, cfg.DFF, cfg.SEQ, stages)
    if key not in _CACHE:
        _CACHE[key] = K(cfg, stages).build()
    nc = _CACHE[key]
    maps = make_in_maps(cfg, inputs)
    res = run_bass_kernel_spmd(nc, maps, core_ids=list(range(NCORES)))
    return assemble(cfg, res, inputs["x_prompt"].shape[0])


def kernel(**inputs):
    return run(Cfg(), inputs, STAGES)
```
